# Optimizing a Trainium2 kernel written in Bass

```python
import jax, jax.numpy as jnp
from jax import lax
import numpy as np

D_MODEL = 2048
BATCH = 8
SEQ = 2048
DEPTH = 2

GRID_W = 64
CTX_LEN = 256
EPS = 1e-6

HEAD_DIM = 128
N_Q_HEADS = D_MODEL // (2 * HEAD_DIM)
N_KV_HEADS = max(1, N_Q_HEADS // 4)
Q_PER_KV = N_Q_HEADS // N_KV_HEADS
ATTN_WIDTH = N_Q_HEADS * HEAD_DIM
KV_WIDTH = N_KV_HEADS * HEAD_DIM
Q_BLOCK = 128
ROPE_THETA = 10000.0
AXIS_DIM = HEAD_DIM // 2

CHUNK = 128
MLP_WIDTH = D_MODEL // 4
MLP_GROUP_W = 128
MLP_GROUPS = MLP_WIDTH // MLP_GROUP_W

LRU_WIDTH = D_MODEL // 4
LRU_BLOCKS = 4
LRU_BLOCK_W = LRU_WIDTH // LRU_BLOCKS
LRU_C = 8.0
CONV_W = 4
CONV_LEFT = 2

MIX_WIDTH = ATTN_WIDTH + MLP_WIDTH + LRU_WIDTH
IN_COLS = ATTN_WIDTH + 2 * KV_WIDTH + 2 * MLP_WIDTH + 2 * LRU_WIDTH
SPLITS = (ATTN_WIDTH,
          ATTN_WIDTH + KV_WIDTH,
          ATTN_WIDTH + 2 * KV_WIDTH,
          ATTN_WIDTH + 2 * KV_WIDTH + 2 * MLP_WIDTH,
          ATTN_WIDTH + 2 * KV_WIDTH + 2 * MLP_WIDTH + LRU_WIDTH)

D_FF = -(-8 * D_MODEL // (3 * 256)) * 256

kernel_name = "hybrid_parallel_heads_dit_block"


def rmsnorm(x, g):
    xf = x.astype(jnp.float32)
    y = xf * lax.rsqrt(jnp.mean(xf * xf, axis=-1, keepdims=True) + EPS)
    return (y * g.astype(jnp.float32)).astype(x.dtype)


def modulate(h, shift, scale):
    return h * (1 + scale) + shift


def axial_rope_tables(n_tokens):
    rows = n_tokens // GRID_W
    row_ids = jnp.repeat(jnp.arange(rows, dtype=jnp.float32), GRID_W)
    col_ids = jnp.tile(jnp.arange(GRID_W, dtype=jnp.float32), rows)
    inv_freq = ROPE_THETA ** (-jnp.arange(0, AXIS_DIM, 2, dtype=jnp.float32) / AXIS_DIM)
    ang_r = row_ids[:, None] * inv_freq
    ang_c = col_ids[:, None] * inv_freq
    ang = jnp.concatenate([ang_r, ang_r, ang_c, ang_c], axis=-1)
    return jnp.cos(ang), jnp.sin(ang)


def apply_axial_rope(x, cos, sin):
    shape = x.shape
    xf = x.astype(jnp.float32)
    xs = xf.reshape(shape[:-1] + (2, 2, AXIS_DIM // 2))
    rot = jnp.stack([-xs[..., 1, :], xs[..., 0, :]], axis=-2).reshape(shape)
    bshape = (shape[1],) + (1,) * (x.ndim - 3) + (HEAD_DIM,)
    return (xf * cos.reshape(bshape) + rot * sin.reshape(bshape)).astype(x.dtype)


def attend(q, k, v):
    s = jnp.einsum('bkgqd,bktd->bkgqt', q, k, preferred_element_type=jnp.float32) * (HEAD_DIM ** -0.5)
    p = jax.nn.softmax(s, axis=-1).astype(v.dtype)
    return jnp.einsum('bkgqt,bktd->bkgqd', p, v)


def chunk_mlp(z, w_s, b_s):
    B, L, _ = z.shape
    u, v = jnp.split(jax.nn.gelu(z), 2, axis=-1)
    vb = v.reshape(B, L // CHUNK, CHUNK, MLP_GROUPS, MLP_GROUP_W)
    s = jnp.einsum('gpq,bnqgc->bnpgc', w_s, vb) + b_s.T[None, None, :, :, None]
    return u * s.reshape(B, L, MLP_WIDTH)


def depthwise_conv(x, w, b):
    L = x.shape[1]
    xp = jnp.pad(x, ((0, 0), (CONV_LEFT, CONV_W - 1 - CONV_LEFT), (0, 0)))
    y = b
    for tap in range(CONV_W):
        y = y + xp[:, tap:tap + L] * w[tap]
    return y


def lru_coeffs(x, w_a, w_x, b_a, b_x, lam):
    B, L, W = x.shape
    xb = x.reshape(B, L, LRU_BLOCKS, LRU_BLOCK_W)
    r = jax.nn.sigmoid((jnp.einsum('blhi,hij->blhj', xb, w_a).reshape(B, L, W) + b_a).astype(jnp.float32))
    i = jax.nn.sigmoid((jnp.einsum('blhi,hij->blhj', xb, w_x).reshape(B, L, W) + b_x).astype(jnp.float32))
    log_a = -LRU_C * jax.nn.softplus(-lam.astype(jnp.float32)) * r
    a = jnp.exp(log_a)
    bx = jnp.sqrt(-jnp.expm1(2.0 * log_a)) * (i * x.astype(jnp.float32))
    return a, bx


def linear_scan(a, bx, h0, reverse):
    def combine(l, r):
        return (l[0] * r[0], r[0] * l[1] + r[1])
    A, Bc = lax.associative_scan(combine, (a, bx), reverse=reverse, axis=1)
    if h0 is None:
        return Bc
    return A * h0[:, None, :] + Bc


def swiglu(h, w_ffn_in, w_ffn_out):
    g, u = jnp.split(h @ w_ffn_in, 2, axis=-1)
    return (jax.nn.silu(g) * u) @ w_ffn_out


def mixer(h, hc, w_in, g_qk, w_s, b_s, conv_w, conv_b, lru_w, lru_b, lru_lam, w_out, cos, sin, ctx_out):
    B, N, _ = h.shape
    C = hc.shape[1]
    q, k, v, zm, zx, zg = jnp.split(h @ w_in, SPLITS, axis=-1)
    qc, kc, vc, zmc, zxc, zgc = jnp.split(hc @ w_in, SPLITS, axis=-1)

    q = apply_axial_rope(rmsnorm(q.reshape(B, N, N_KV_HEADS, Q_PER_KV, HEAD_DIM), g_qk[0]), cos, sin)
    q = q.transpose(0, 2, 3, 1, 4)
    k = apply_axial_rope(rmsnorm(k.reshape(B, N, N_KV_HEADS, HEAD_DIM), g_qk[1]), cos, sin)
    k = k.transpose(0, 2, 1, 3)
    v = v.reshape(B, N, N_KV_HEADS, HEAD_DIM).transpose(0, 2, 1, 3)
    kc = rmsnorm(kc.reshape(B, C, N_KV_HEADS, HEAD_DIM), g_qk[1]).transpose(0, 2, 1, 3)
    vc = vc.reshape(B, C, N_KV_HEADS, HEAD_DIM).transpose(0, 2, 1, 3)
    k_all = jnp.concatenate([kc, k], axis=2)
    v_all = jnp.concatenate([vc, v], axis=2)
    n_blocks = N // Q_BLOCK
    qb = q.reshape(B, N_KV_HEADS, Q_PER_KV, n_blocks, Q_BLOCK, HEAD_DIM).transpose(3, 0, 1, 2, 4, 5)
    o = lax.map(lambda qi: attend(qi, k_all, v_all), qb)
    attn = o.transpose(1, 0, 4, 2, 3, 5).reshape(B, N, ATTN_WIDTH)

    mlp = chunk_mlp(zm, w_s, b_s)

    xr = depthwise_conv(zx, conv_w, conv_b)
    xrc = depthwise_conv(zxc, conv_w, conv_b)
    h_lat, h_ctx = [], []
    for d, rev in ((0, False), (1, True)):
        ac, bcx = lru_coeffs(xrc, lru_w[d, 0], lru_w[d, 1], lru_b[d, 0], lru_b[d, 1], lru_lam[d])
        hcs = linear_scan(ac, bcx, None, rev)
        h0 = hcs[:, 0] if rev else hcs[:, -1]
        al, blx = lru_coeffs(xr, lru_w[d, 0], lru_w[d, 1], lru_b[d, 0], lru_b[d, 1], lru_lam[d])
        h_lat.append(linear_scan(al, blx, h0, rev))
        h_ctx.append(hcs)
    lru = ((h_lat[0] + h_lat[1]) * jax.nn.gelu(zg.astype(jnp.float32))).astype(h.dtype)

    y = jnp.concatenate([attn, mlp, lru], axis=-1) @ w_out
    if not ctx_out:
        return y, None

    qc = rmsnorm(qc.reshape(B, C, N_KV_HEADS, Q_PER_KV, HEAD_DIM), g_qk[0]).transpose(0, 2, 3, 1, 4)
    attn_c = attend(qc, kc, vc).transpose(0, 3, 1, 2, 4).reshape(B, C, ATTN_WIDTH)
    mlp_c = chunk_mlp(zmc, w_s, b_s)
    lru_c = ((h_ctx[0] + h_ctx[1]) * jax.nn.gelu(zgc.astype(jnp.float32))).astype(hc.dtype)
    yc = jnp.concatenate([attn_c, mlp_c, lru_c], axis=-1) @ w_out
    return y, yc


def setup_inputs(seed: int = 0) -> dict:
    key = jax.random.key(seed)
    ks = jax.random.split(key, 24)
    f32 = jnp.float32

    def nrm(k, shape, scale):
        return jax.random.normal(k, shape, f32) * scale

    u = jax.random.uniform(ks[17], (DEPTH, 2, LRU_WIDTH), f32, minval=0.9, maxval=0.999)
    a0 = u ** (1.0 / LRU_C)
    lru_lam = jnp.log(a0) - jnp.log1p(-a0)
    return {
        "x": nrm(ks[0], (BATCH, SEQ, D_MODEL), 1.0),
        "c": nrm(ks[1], (BATCH, D_MODEL), 1.0),
        "ctx": nrm(ks[2], (BATCH, CTX_LEN, D_MODEL), 1.0),
        "c_ctx": nrm(ks[3], (D_MODEL,), 1.0),
        "w_mod": nrm(ks[4], (DEPTH, D_MODEL, 6 * D_MODEL), 0.5 * D_MODEL ** -0.5),
        "b_mod": nrm(ks[5], (DEPTH, 6 * D_MODEL), 0.02),
        "g_norm": 1.0 + nrm(ks[6], (DEPTH, 4, D_MODEL), 0.02),
        "w_in": nrm(ks[7], (DEPTH, D_MODEL, IN_COLS), D_MODEL ** -0.5),
        "g_qk": 1.0 + nrm(ks[8], (DEPTH, 2, HEAD_DIM), 0.02),
        "w_s": nrm(ks[9], (DEPTH, MLP_GROUPS, CHUNK, CHUNK), CHUNK ** -0.5),
        "b_s": nrm(ks[10], (DEPTH, MLP_GROUPS, CHUNK), 0.1),
        "conv_w": nrm(ks[11], (DEPTH, CONV_W, LRU_WIDTH), CONV_W ** -0.5),
        "conv_b": nrm(ks[12], (DEPTH, LRU_WIDTH), 0.02),
        "lru_w": nrm(ks[13], (DEPTH, 2, 2, LRU_BLOCKS, LRU_BLOCK_W, LRU_BLOCK_W), LRU_BLOCK_W ** -0.5),
        "lru_b": nrm(ks[14], (DEPTH, 2, 2, LRU_WIDTH), 0.1),
        "lru_lam": lru_lam,
        "w_out": nrm(ks[15], (DEPTH, MIX_WIDTH, D_MODEL), MIX_WIDTH ** -0.5),
        "w_ffn_in": nrm(ks[16], (DEPTH, D_MODEL, 2 * D_FF), D_MODEL ** -0.5),
        "w_ffn_out": nrm(ks[18], (DEPTH, D_FF, D_MODEL), D_FF ** -0.5),
    }


def reference(x, c, ctx, c_ctx, w_mod, b_mod, g_norm, w_in, g_qk, w_s, b_s, conv_w, conv_b,
              lru_w, lru_b, lru_lam, w_out, w_ffn_in, w_ffn_out):
    cos, sin = axial_rope_tables(x.shape[1])
    c_act = jax.nn.silu(c)
    cc_act = jax.nn.silu(c_ctx)
    xc = ctx
    for l in range(DEPTH):
        last = l == DEPTH - 1
        mod = (c_act @ w_mod[l] + b_mod[l])[:, None, :]
        modc = cc_act @ w_mod[l] + b_mod[l]
        sh1, sc1, gt1, sh2, sc2, gt2 = jnp.split(mod, 6, axis=-1)
        shc1, scc1, gtc1, shc2, scc2, gtc2 = jnp.split(modc, 6, axis=-1)

        h = modulate(rmsnorm(x, g_norm[l, 0]), sh1, sc1)
        hc = modulate(rmsnorm(xc, g_norm[l, 0]), shc1, scc1)
        y, yc = mixer(h, hc, w_in[l], g_qk[l], w_s[l], b_s[l], conv_w[l], conv_b[l],
                      lru_w[l], lru_b[l], lru_lam[l], w_out[l], cos, sin, not last)
        x = x + gt1 * rmsnorm(y, g_norm[l, 1])
        f = swiglu(modulate(rmsnorm(x, g_norm[l, 2]), sh2, sc2), w_ffn_in[l], w_ffn_out[l])
        x = x + gt2 * rmsnorm(f, g_norm[l, 3])

        if not last:
            xc = xc + gtc1 * rmsnorm(yc, g_norm[l, 1])
            fc = swiglu(modulate(rmsnorm(xc, g_norm[l, 2]), shc2, scc2), w_ffn_in[l], w_ffn_out[l])
            xc = xc + gtc2 * rmsnorm(fc, g_norm[l, 3])
    return x
```

```python
import numpy as np
import ml_dtypes
from contextlib import ExitStack
import concourse.bass as bass
import concourse.mybir as mybir
from concourse.bass_utils import run_bass_kernel_spmd

F32 = mybir.dt.float32
BF16 = mybir.dt.bfloat16
AF = mybir.ActivationFunctionType
ALU = mybir.AluOpType

ENGS = ("pe", "act", "dve", "pool", "sp")
NDSEM = {"sp": 12, "act": 2, "pool": 4}

D = 2048
NLAT = 2048
NCTX = 256
T = NCTX + NLAT
DFF = 5632
NL = 2
KC = D // 128
FC = DFF // 128
GROUPS = [(0, 256), (256, 512), (768, 512), (1280, 512), (1792, 512)]
EPS = 1e-6
QSCALE = 128.0 ** -0.5


class Buf:
    __slots__ = ("w", "r", "rd", "name", "excl")

    def __init__(self, name="", excl=False):
        self.w = None
        self.r = {}
        self.rd = []
        self.name = name
        self.excl = excl


class Op:
    __slots__ = ("eng", "fn", "deps", "sig", "need_sig", "is_dma", "prev_wait")


class Prog:
    def __init__(self, nc):
        self.nc = nc
        self.ops = {e: [] for e in ENGS}
        self.nops = 0
        self.last_dmas = {e: [] for e in ENGS}

    def op(self, eng, fn, reads=(), writes=(), dma=False, extra_deps=()):
        o = Op()
        o.eng = eng
        o.fn = fn
        o.is_dma = dma
        o.need_sig = dma
        o.sig = None
        o.prev_wait = None
        deps = set(extra_deps)
        for b in reads:
            if b.w is not None:
                deps.add(b.w)
            if b.excl:
                for e2, r in b.r.items():
                    if e2 != eng:
                        deps.add(r)
        for b in writes:
            if b.w is not None:
                deps.add(b.w)
            for r in b.r.values():
                deps.add(r)
            for r in b.rd:
                deps.add(r)
        if eng == "pe" and not dma:
            deps = {d for d in deps if d.is_dma or d.eng != "pe"}
        for d in deps:
            d.need_sig = True
        o.deps = deps
        for b in reads:
            if dma:
                b.rd.append(o)
            else:
                b.r[eng] = o
        for b in writes:
            b.w = o
            b.r = {}
            b.rd = []
        self.ops[eng].append(o)
        if dma:
            ld = self.last_dmas[eng]
            ld.append(o)
            if len(ld) > NDSEM[eng]:
                ld.pop(0)
        self.nops += 1
        return o

    def barrier(self):
        deps = []
        for e in ENGS:
            for o in reversed(self.ops[e]):
                if not o.is_dma and o.fn is not None:
                    deps.append(o)
                    break
            if e != "pool":
                deps.extend(self.last_dmas[e])
        for e in ENGS:
            self.op(e, None, extra_deps=deps)

    def dma(self, q, out, in_, reads=(), writes=()):
        return self.op(q, lambda e: e.dma_start(out=out, in_=in_), reads, writes, dma=True)

    def mm(self, out, lhsT, rhs, start, stop, reads=(), writes=()):
        return self.op("pe", lambda e: e.matmul(out, lhsT, rhs, start=start, stop=stop), reads, writes)

    def tr(self, out, in_, ident, reads=(), writes=()):
        return self.op("pe", lambda e: e.transpose(out=out, in_=in_, identity=ident), reads, writes)

    def act(self, out, in_, func, reads=(), writes=(), scale=None, bias=None):
        kw = {}
        if scale is not None:
            kw["scale"] = scale
        if bias is not None:
            kw["bias"] = bias
        return self.op("act", lambda e: e.activation(out=out, in_=in_, func=func, **kw), reads, writes)

    def tt(self, eng, out, in0, in1, op, reads=(), writes=()):
        return self.op(eng, lambda e: e.tensor_tensor(out=out, in0=in0, in1=in1, op=op), reads, writes)

    def stt(self, out, in0, scalar, in1, op0, op1, reads=(), writes=()):
        return self.op("dve", lambda e: e.scalar_tensor_tensor(out=out, in0=in0, scalar=scalar, in1=in1, op0=op0, op1=op1),
                       reads, writes)

    def ts(self, eng, out, in0, s1, op0, s2=None, op1=None, reads=(), writes=()):
        if op1 is None:
            return self.op(eng, lambda e: e.tensor_scalar(out=out, in0=in0, scalar1=s1, scalar2=None, op0=op0), reads, writes)
        return self.op(eng, lambda e: e.tensor_scalar(out=out, in0=in0, scalar1=s1, scalar2=s2, op0=op0, op1=op1), reads, writes)

    def copy(self, eng, out, in_, reads=(), writes=()):
        if eng == "act":
            return self.op("act", lambda e: e.activation(out=out, in_=in_, func=AF.Identity), reads, writes)
        return self.op(eng, lambda e: e.tensor_copy(out=out, in_=in_), reads, writes)

    def recip(self, out, in_, reads=(), writes=()):
        return self.op("dve", lambda e: e.reciprocal(out=out, in_=in_), reads, writes)

    def scan(self, out, d0, d1, initial, reads=(), writes=()):
        return self.op("dve", lambda e: e.tensor_tensor_scan(out=out, data0=d0, data1=d1, initial=initial,
                                                              op0=ALU.mult, op1=ALU.add), reads, writes)

    def memset(self, eng, ap, val, writes=()):
        return self.op(eng, lambda e: e.memset(ap, val), (), writes)

    def emit(self, final_wait_ops=()):
        nc = self.nc
        with ExitStack() as st:
            sems = {e: st.enter_context(nc.semaphore("s_" + e)) for e in ENGS}
            dsems = {e: [st.enter_context(nc.semaphore("d_%s%d" % (e, i))) for i in range(NDSEM[e])]
                     for e in NDSEM}
            semobj = {}
            for e in ENGS:
                cnt = 0
                nd = NDSEM.get(e, 1)
                duse = [0] * nd
                di = 0
                for o in self.ops[e]:
                    if o.is_dma:
                        s = di % nd
                        di += 1
                        key = ("d", e, s)
                        semobj[key] = dsems[e][s]
                        if duse[s]:
                            o.prev_wait = (key, 16 * duse[s])
                        duse[s] += 1
                        o.sig = (key, 16 * duse[s])
                    elif o.need_sig:
                        assert o.fn is not None
                        cnt += 1
                        key = ("c", e)
                        semobj[key] = sems[e]
                        o.sig = (key, cnt)
            finals = [o.sig for o in final_wait_ops]
            block = st.enter_context(nc.Block())

            def run(e, eng):
                waited = {}
                for o in self.ops[e]:
                    waits = {}
                    for d in o.deps:
                        k, v = d.sig
                        if waited.get(k, 0) < v and waits.get(k, 0) < v:
                            waits[k] = v
                    if o.prev_wait is not None:
                        k, v = o.prev_wait
                        if waited.get(k, 0) < v and waits.get(k, 0) < v:
                            waits[k] = v
                    for k, v in waits.items():
                        eng.wait_ge(semobj[k], v)
                        waited[k] = v
                    if o.fn is None:
                        continue
                    ins = o.fn(eng)
                    if o.sig is not None:
                        ins.then_inc(semobj[o.sig[0]], 16 if o.is_dma else 1)
                if e == "sp":
                    fw = {}
                    for k, v in finals:
                        fw[k] = max(fw.get(k, 0), v)
                    for k, v in fw.items():
                        if waited.get(k, 0) < v:
                            eng.wait_ge(semobj[k], v)

            @block.tensor
            def _(eng):
                run("pe", eng)

            @block.scalar
            def _(eng):
                run("act", eng)

            @block.vector
            def _(eng):
                run("dve", eng)

            @block.gpsimd
            def _(eng):
                run("pool", eng)

            @block.sync
            def _(eng):
                run("sp", eng)


class Arena:
    def __init__(self, nc, name, nbytes):
        self.ap = nc.alloc_sbuf_tensor(name, [128, nbytes // 4], F32).ap()
        self.n = nbytes
        self.off = 0

    def reset(self):
        self.off = 0

    def alloc(self, free, dtype):
        esz = 4 if dtype == F32 else 2
        ne = 1
        for s in free:
            ne *= s
        nb = ne * esz
        assert nb % 4 == 0
        nba = (nb + 63) // 64 * 64
        assert self.off + nba <= self.n, ("arena overflow", self.off, nba, self.n)
        v = self.ap[:, self.off // 4:(self.off + nb) // 4]
        if dtype != F32:
            v = v.bitcast(dtype)
        self.off += nba
        if len(free) == 2:
            v = v.rearrange("p (a b) -> p a b", a=free[0])
        elif len(free) == 3:
            v = v.rearrange("p (a b c) -> p a b c", a=free[0], b=free[1])
        elif len(free) == 4:
            v = v.rearrange("p (a b c d) -> p a b c d", a=free[0], b=free[1], c=free[2])
        return v


def bufs(n, name="", excl=False):
    return [Buf(name + str(i), excl) for i in range(n)]


INPUT_SPECS = [
    ("x", [NLAT, D]), ("c", [D]), ("ctx", [NCTX, D]), ("c_ctx", [D]),
    ("w_mod", [NL, D, 6 * D]), ("b_mod", [NL, 6 * D]), ("g_norm", [NL, 4, D]), ("w_in", [NL, D, 3584]),
    ("g_qk", [NL, 2, 128]), ("w_s", [NL, 4, 128, 128]), ("b_s", [NL, 4, 128]), ("conv_w", [NL, 4, 512]),
    ("conv_b", [NL, 512]), ("lru_w", [NL, 2, 2, 4, 128, 128]), ("lru_b", [NL, 2, 2, 512]),
    ("lru_lam", [NL, 2, 512]), ("w_out", [NL, D, D]), ("w_ffn_in", [NL, D, 2 * DFF]), ("w_ffn_out", [NL, DFF, D]),
]


def build(debug=False, stop=None):
    nc = bass.Bass("TRN2", target_bir_lowering=False)
    I = {}
    for name, shape in INPUT_SPECS:
        I[name] = nc.dram_tensor(name, shape, F32, kind="ExternalInput").ap()
    cosT_d = nc.dram_tensor("k_cos", [128, NLAT], F32, kind="ExternalInput").ap()
    sinT_d = nc.dram_tensor("k_sin", [128, NLAT], F32, kind="ExternalInput").ap()
    rot_d = nc.dram_tensor("k_rot", [128, 128], F32, kind="ExternalInput").ap()
    idn_d = nc.dram_tensor("k_idn", [128, 128], F32, kind="ExternalInput").ap()
    out_d = nc.dram_tensor("out", [NLAT, D], F32, kind="ExternalOutput").ap()

    skind = "ExternalOutput" if debug else "Internal"

    def scratch(name, shape, dt, dbg=True):
        return nc.dram_tensor(name, shape, dt, kind=(skind if dbg else "Internal")).ap()

    WBin = [scratch("WBin%d" % l, [7, 128, KC, 512], BF16, False) for l in range(NL)]
    WBout = [scratch("WBout%d" % l, [4, 128, KC, 512], BF16, False) for l in range(NL)]
    WBf1 = [scratch("WBf1%d" % l, [22, 128, KC, 512], BF16, False) for l in range(NL)]
    WBf2 = [scratch("WBf2%d" % l, [4, 128, FC, 512], BF16, False) for l in range(NL)]
    XT = scratch("XT", [D, T], F32)
    QT = scratch("QT", [1024, T], BF16)
    KT = scratch("KT", [256, T], BF16)
    VTOK = scratch("VTOK", [T, 256], BF16)
    MIXT = scratch("MIXT", [D, T], BF16)
    ZXT = scratch("ZXT", [512, T], BF16)
    GZT = scratch("GZT", [512, T], BF16)
    if debug:
        DBG = scratch("DBG", [128, 2048], F32)

    P = Prog(nc)

    pers = Arena(nc, "pers", 10 * 1024 + 4 * 16384)
    ones_bf = pers.alloc((128,), BF16)
    ones_f = pers.alloc((128,), F32)
    idn_f = pers.alloc((128,), F32)
    rot_f = pers.alloc((128,), F32)
    CT = pers.alloc((2, 16), F32)
    CTb = pers.alloc((2, 16), BF16)
    SMA = [pers.alloc((128,), F32) for _ in range(NL)]
    SMB = [pers.alloc((128,), F32) for _ in range(NL)]
    MOD = [pers.alloc((2, 96), F32) for _ in range(NL)]
    DER = [pers.alloc((6, 2, 16), F32) for _ in range(NL)]
    CL = [pers.alloc((2, 8), F32) for _ in range(NL)]
    NEGB = [pers.alloc((2,), F32) for _ in range(NL)]
    PSTG = [pers.alloc((128,), F32) for _ in range(2)]
    PTMP = pers.alloc((128,), F32)
    NRING = 4
    RING = [pers.alloc((KC, 512), BF16) for _ in range(NRING)]
    B_const = Buf("const")
    B_sm = [Buf("sm%d" % l) for l in range(NL)]
    B_ring = bufs(NRING, "ring")
    ring_ctr = [0]

    ring_pending = set()

    def ring_next():
        while True:
            s = ring_ctr[0] % NRING
            ring_ctr[0] += 1
            if s not in ring_pending:
                return s

    arena = Arena(nc, "arena", 133 * 1024)
    ps = nc.alloc_psum_tensor("ps", [128, 8, 512], F32).ap()
    PSB = bufs(8, "psb", excl=True)

    B_WBin = [bufs(7) for _ in range(NL)]
    B_WBout = [bufs(4) for _ in range(NL)]
    B_WBf1 = [bufs(22) for _ in range(NL)]
    B_WBf2 = [bufs(4) for _ in range(NL)]
    NG = len(GROUPS)
    B_XT = bufs(NG)
    B_QT = [bufs(NG) for _ in range(8)]
    B_KT = [bufs(NG) for _ in range(2)]
    B_V = bufs(NG)
    B_MIX = [bufs(NG) for _ in range(16)]
    B_ZX = [bufs(NG) for _ in range(4)]
    B_GZ = [bufs(NG) for _ in range(4)]
    out_ops = []

    def conv_tasks(l, which):
        tasks = []
        if "out" in which:
            tasks += [lambda j=j: conv_one(WBout[l][j], I["w_out"][l][:, 512 * j:512 * (j + 1)], B_WBout[l][j]) for j in range(4)]
        if "f1" in which:
            for j in range(11):
                for jj in (j, 11 + j):
                    tasks.append(lambda jj=jj: conv_one(WBf1[l][jj], I["w_ffn_in"][l][:, 512 * jj:512 * (jj + 1)], B_WBf1[l][jj]))
        if "f2" in which:
            for j in range(4):
                for part in range(4):
                    tasks.append(lambda j=j, part=part: P.dma(
                        "pool", WBf2[l][j][:, 11 * part:11 * (part + 1), :],
                        I["w_ffn_out"][l][1408 * part:1408 * (part + 1), 512 * j:512 * (j + 1)].rearrange("(kc p) n -> p kc n", p=128),
                        writes=[B_WBf2[l][j]]))
        return tasks

    def conv_one(dst, src, B):
        P.dma("pool", dst, src.rearrange("(kc p) n -> p kc n", p=128), writes=[B])

    def conv_w_in(l):
        for j in range(7):
            P.dma("pool", WBin[l][j], I["w_in"][l][:, 512 * j:512 * (j + 1)].rearrange("(kc p) n -> p kc n", p=128),
                  writes=[B_WBin[l][j]])

    def conv_w_out(l):
        for j in range(4):
            P.dma("pool", WBout[l][j], I["w_out"][l][:, 512 * j:512 * (j + 1)].rearrange("(kc p) n -> p kc n", p=128),
                  writes=[B_WBout[l][j]])

    def conv_w_f1(l, js):
        for j in js:
            P.dma("pool", WBf1[l][j], I["w_ffn_in"][l][:, 512 * j:512 * (j + 1)].rearrange("(kc p) n -> p kc n", p=128),
                  writes=[B_WBf1[l][j]])

    def conv_w_f2(l):
        for j in range(4):
            for part in range(4):
                P.dma("pool", WBf2[l][j][:, 11 * part:11 * (part + 1), :],
                      I["w_ffn_out"][l][1408 * part:1408 * (part + 1), 512 * j:512 * (j + 1)].rearrange("(kc p) n -> p kc n", p=128),
                      writes=[B_WBf2[l][j]])

    def preamble():
        stg = PSTG
        B_stg = bufs(2, "stg")
        tmpa = PTMP
        B_tmpa = Buf()
        P.memset("dve", ones_bf, 1.0, writes=[B_const])
        P.memset("dve", ones_f, 1.0, writes=[B_const])
        P.dma("sp", idn_f, idn_d, writes=[B_const])
        P.dma("sp", rot_f, rot_d, writes=[B_const])
        P.dma("sp", stg[0][0:16, :], I["c"].rearrange("(kc p) -> kc p", p=128), writes=[B_stg[0]])
        P.dma("sp", stg[0][16:32, :], I["c_ctx"].rearrange("(kc p) -> kc p", p=128), writes=[B_stg[0]])
        P.tr(ps[:, 0, 0:32], stg[0][0:32, :], idn_f[0:32, 0:32], reads=[B_stg[0], B_const], writes=[PSB[0]])
        P.act(CT.rearrange("p w k -> p (w k)"), ps[:, 0, 0:32], AF.Silu, reads=[PSB[0]], writes=[B_const])
        P.copy("dve", CTb, CT, reads=[B_const], writes=[B_const])
        for l in range(NL):
            s = stg[1]
            P.dma("sp", s[0:96, :], I["b_mod"][l].rearrange("(r p) -> r p", p=128), writes=[B_stg[1]])
            P.dma("sp", s[96:112, :], I["conv_w"][l].rearrange("t (h p) -> (t h) p", p=128), writes=[B_stg[1]])
            P.dma("sp", s[112:116, :], I["conv_b"][l].rearrange("(h p) -> h p", p=128), writes=[B_stg[1]])
            P.dma("sp", s[116:118, :], I["g_qk"][l], writes=[B_stg[1]])
            P.tr(ps[:, 1, 0:118], s[0:118, :], idn_f[0:118, 0:118], reads=[B_stg[1], B_const], writes=[PSB[1]])
            P.copy("dve", SMA[l][:, 0:118], ps[:, 1, 0:118], reads=[PSB[1]], writes=[B_sm[l]])
            s = stg[0]
            P.dma("sp", s[0:64, :], I["g_norm"][l].rearrange("f (kc p) -> (f kc) p", p=128), writes=[B_stg[0]])
            P.dma("sp", s[64:80, :], I["lru_b"][l].rearrange("d g (h p) -> (d g h) p", p=128), writes=[B_stg[0]])
            P.dma("sp", s[80:88, :], I["lru_lam"][l].rearrange("d (h p) -> (d h) p", p=128), writes=[B_stg[0]])
            P.tr(ps[:, 2, 0:88], s[0:88, :], idn_f[0:88, 0:88], reads=[B_stg[0], B_const], writes=[PSB[2]])
            P.copy("dve", SMB[l][:, 0:88], ps[:, 2, 0:88], reads=[PSB[2]], writes=[B_sm[l]])
            P.act(tmpa[:, 0:8], SMB[l][:, 80:88], AF.Exp, reads=[B_sm[l]], writes=[B_tmpa], scale=-1.0)
            P.ts("dve", tmpa[:, 0:8], tmpa[:, 0:8], 1.0, ALU.add, reads=[B_tmpa], writes=[B_tmpa])
            P.act(tmpa[:, 8:16], tmpa[:, 0:8], AF.Ln, reads=[B_tmpa], writes=[B_tmpa])
            P.ts("dve", CL[l][:, 0, :], tmpa[:, 8:16], -8.0, ALU.mult, reads=[B_tmpa], writes=[B_sm[l]])
            P.ts("dve", CL[l][:, 1, :], tmpa[:, 8:16], -16.0, ALU.mult, reads=[B_tmpa], writes=[B_sm[l]])
            s = stg[1]
            P.dma("sp", s[0:2, :], I["g_qk"][l], writes=[B_stg[1]])
            P.op("dve", lambda e, s=s: e.tensor_reduce(out=tmpa[0:2, 16:17], in_=s[0:2, :], axis=mybir.AxisListType.X,
                                                       op=ALU.max, apply_absolute_value=True), reads=[B_stg[1]], writes=[B_tmpa])
            P.ts("dve", tmpa[0:2, 20:22], idn_f[0:2, 0:2], tmpa[0:2, 16:17], ALU.mult, reads=[B_tmpa, B_const], writes=[B_tmpa])
            P.mm(ps[:, 3, 0:2], ones_f[0:2, :], tmpa[0:2, 20:22], True, True, reads=[B_tmpa, B_const], writes=[PSB[3]])
            P.copy("dve", tmpa[:, 24:26], ps[:, 3, 0:2], reads=[PSB[3]], writes=[B_tmpa])
            P.stt(NEGB[l][:, 0:1], tmpa[:, 24:25], -(128.0 ** 0.5), tmpa[:, 25:26], ALU.mult, ALU.mult,
                  reads=[B_tmpa], writes=[B_sm[l]])

    B_modA = [Buf("modA%d" % l) for l in range(NL)]
    B_modB = [Buf("modB%d" % l) for l in range(NL)]
    MODBANK = 3

    def mod_tasks(l, blocks):
        tasks = []
        for j in blocks:
            def task(j=j):
                s = ring_next()
                P.dma("pool", RING[s], I["w_mod"][l][:, 512 * j:512 * (j + 1)].rearrange("(kc p) n -> p kc n", p=128),
                      writes=[B_ring[s]])
                for n in range(4):
                    for kc in range(KC):
                        P.mm(ps[:, MODBANK, n * 2:n * 2 + 2], RING[s][:, kc, n * 128:(n + 1) * 128], CTb[:, :, kc],
                             kc == 0, kc == KC - 1, reads=[B_ring[s], B_const], writes=[PSB[MODBANK]])
                pv = ps[:, MODBANK, 0:8].rearrange("p (c w) -> p c w", w=2)
                Bm = B_modA[l] if j < 8 else B_modB[l]
                for w in range(2):
                    P.tt("dve", MOD[l][:, w, j * 4:(j + 1) * 4], pv[:, :, w], SMA[l][:, j * 4:(j + 1) * 4], ALU.add,
                         reads=[PSB[MODBANK], B_sm[l]], writes=[Bm])
            tasks.append(task)
        return tasks

    def der_A(l):
        gn = SMB[l][:, 0:64].rearrange("p (f k) -> p f k", f=4)
        for w in range(2):
            m = MOD[l][:, w, :].rearrange("p (s k) -> p s k", s=6)
            P.stt(DER[l][:, 0, w, :], m[:, 1, :], 1.0, gn[:, 0, :], ALU.add, ALU.mult, reads=[B_sm[l], B_modA[l]], writes=[B_modA[l]])
            P.copy("dve", DER[l][:, 1, w, :], m[:, 0, :], reads=[B_modA[l]], writes=[B_modA[l]])

    def der_B(l):
        gn = SMB[l][:, 0:64].rearrange("p (f k) -> p f k", f=4)
        for w in range(2):
            m = MOD[l][:, w, :].rearrange("p (s k) -> p s k", s=6)
            P.tt("dve", DER[l][:, 2, w, :], m[:, 2, :], gn[:, 1, :], ALU.mult, reads=[B_sm[l], B_modB[l]], writes=[B_modB[l]])
            P.stt(DER[l][:, 3, w, :], m[:, 4, :], 1.0, gn[:, 2, :], ALU.add, ALU.mult, reads=[B_sm[l], B_modB[l]], writes=[B_modB[l]])
            P.copy("dve", DER[l][:, 4, w, :], m[:, 3, :], reads=[B_modB[l]], writes=[B_modB[l]])
            P.tt("dve", DER[l][:, 5, w, :], m[:, 5, :], gn[:, 3, :], ALU.mult, reads=[B_sm[l], B_modB[l]], writes=[B_modB[l]])

    bg = []

    def run_bg(n=1):
        for _ in range(n):
            if bg:
                bg.pop(0)()

    def norm_stats(src, B_src, W, SQ, B_SQ, bank, RS, B_RS):
        for kc in range(KC):
            sq = SQ[kc % 2]
            P.act(sq[:, :W], src[:, kc, :W], AF.Square, reads=[B_src[kc]], writes=[B_SQ[kc % 2]])
            P.mm(ps[:, bank, :W], ones_bf, sq[:, :W], kc == 0, kc == KC - 1, reads=[B_SQ[kc % 2], B_const], writes=[PSB[bank]])
        P.act(RS[:, :W], ps[:, bank, :W], AF.Ln, reads=[PSB[bank]], writes=[B_RS], scale=1.0 / D, bias=EPSB[:, 0:1])
        P.act(RS[:, :W], RS[:, :W], AF.Exp, reads=[B_RS], writes=[B_RS], scale=-0.5)

    EPSB = pers.alloc((2,), F32)

    def norm_mod(XG, B_XG, W, SQ, B_SQ, bank, RS, B_RS, TMP, B_TMP, A, Bv, HT, B_HT, B_m):
        norm_stats(XG, B_XG, W, SQ, B_SQ, bank, RS, B_RS)
        for kc in range(KC):
            tmp = TMP[kc % 2]
            P.tt("dve", tmp[:, :W], XG[:, kc, :W], RS[:, :W], ALU.mult, reads=[B_XG[kc], B_RS], writes=[B_TMP[kc % 2]])
            P.act(HT[:, kc, :W], tmp[:, :W], AF.Identity, reads=[B_TMP[kc % 2], B_m], writes=[B_HT[kc]],
                  scale=A[:, kc:kc + 1], bias=Bv[:, kc:kc + 1])

    def phase_A(l):
        arena.reset()
        last = l == NL - 1
        XG = arena.alloc((KC, 512), F32)
        B_XG = bufs(KC, "xg")
        HT = arena.alloc((KC, 512), BF16)
        B_HT = bufs(KC, "ht")
        XL = [arena.alloc((D,), F32) for _ in range(2)]
        B_XL = bufs(2, "xl")
        SQ = [arena.alloc((512,), BF16) for _ in range(2)]
        B_SQ = bufs(2)
        RS = arena.alloc((512,), F32)
        B_RS = Buf()
        TMP = [arena.alloc((512,), F32) for _ in range(2)]
        B_TMP = bufs(2)
        COS = arena.alloc((512,), F32)
        SIN = arena.alloc((512,), F32)
        B_CS = Buf()
        QF = [arena.alloc((512,), F32) for _ in range(2)]
        B_QF = bufs(2)
        QSQ = [arena.alloc((512,), BF16) for _ in range(2)]
        B_QSQ = bufs(2)
        QRS = [arena.alloc((512,), F32) for _ in range(2)]
        B_QRS = bufs(2)
        T1 = [arena.alloc((512,), F32) for _ in range(2)]
        B_T1 = bufs(2)
        T2 = [arena.alloc((512,), F32) for _ in range(2)]
        B_T2 = bufs(2)
        QS = [arena.alloc((512,), BF16) for _ in range(3)]
        B_QS = bufs(3)
        VS = arena.alloc((4, 256), BF16)
        B_VS = Buf()
        UT = arena.alloc((4, 512), BF16)
        B_UT = Buf()
        VV = [arena.alloc((512,), BF16) for _ in range(4)]
        B_VV = bufs(4)
        ST = [arena.alloc((4, 128), F32) for _ in range(2)]
        B_ST = bufs(2)
        MS = arena.alloc((4, 512), BF16)
        B_MS = Buf()
        ZS = arena.alloc((4, 512), BF16)
        B_ZS = Buf()
        GS = arena.alloc((4, 512), BF16)
        B_GS = Buf()
        WSTl = arena.alloc((4, 128), BF16)
        BSBl = arena.alloc((4, 128), F32)
        wsl = arena.alloc((4, 128), F32)
        B_wsl = Buf()
        B_ws = Buf()
        BK_MAIN = [0, 1, 2, 3]
        BK_SS, BK_QSS, BK_ROT, BK_TR = 4, 5, 6, 7
        P.dma("sp", wsl, I["w_s"][l].rearrange("g p q -> p g q"), writes=[B_wsl])
        for g in range(4):
            P.tr(ps[:, BK_TR, g * 128:(g + 1) * 128], wsl[:, g, :], idn_f, reads=[B_wsl, B_const], writes=[PSB[BK_TR]])
        P.copy("dve", WSTl.rearrange("p g q -> p (g q)"), ps[:, BK_TR, :], reads=[PSB[BK_TR]], writes=[B_ws])
        P.dma("sp", BSBl.rearrange("p g q -> p (g q)"),
              I["b_s"][l].rearrange("g q -> (g q)").partition_broadcast(128), writes=[B_ws])
        mainctr = [0]
        qctr = [0]
        gq = SMA[l][:, 116:117]
        gk = SMA[l][:, 117:118]

        def mbank():
            b = BK_MAIN[mainctr[0] % 4]
            mainctr[0] += 1
            return b

        pending = []
        for gi, (t0, W) in enumerate(GROUPS):
            w = 1 if gi == 0 else 0
            lat = gi > 0
            need_out = lat or not last
            ntt = W // 128
            if l == 0:
                src = I["x"] if lat else I["ctx"]
                r0 = t0 - NCTX if lat else 0
                for tt in range(ntt):
                    xl = XL[tt % 2]
                    P.dma("sp", xl, src[r0 + tt * 128:r0 + (tt + 1) * 128, :], writes=[B_XL[tt % 2]])
                    for k4 in range(4):
                        tb = (BK_TR, BK_ROT)[k4 % 2]
                        for k in range(4):
                            kc = k4 * 4 + k
                            P.tr(ps[:, tb, k * 128:(k + 1) * 128], xl[:, kc * 128:(kc + 1) * 128], idn_f,
                                 reads=[B_XL[tt % 2], B_const], writes=[PSB[tb]])
                        P.copy("dve" if k4 % 2 else "act", XG[:, k4 * 4:(k4 + 1) * 4, tt * 128:(tt + 1) * 128],
                               ps[:, tb, :].rearrange("p (k t) -> p k t", k=4), reads=[PSB[tb]],
                               writes=B_XG[k4 * 4:(k4 + 1) * 4])
                P.dma("sp", XT.rearrange("(kc p) t -> p kc t", p=128)[:, :, t0:t0 + W], XG[:, :, :W],
                      reads=B_XG, writes=[B_XT[gi]])
            else:
                P.dma("sp", XG[:, :, :W], XT.rearrange("(kc p) t -> p kc t", p=128)[:, :, t0:t0 + W],
                      reads=[B_XT[gi]], writes=B_XG)
            if lat:
                n0 = t0 - NCTX
                P.dma("sp", COS[:, :W], cosT_d[:, n0:n0 + W], writes=[B_CS])
                P.dma("sp", SIN[:, :W], sinT_d[:, n0:n0 + W], writes=[B_CS])
            slots = {}

            def load_block(j):
                s = ring_next()
                ring_pending.add(s)
                P.dma("sp", RING[s], WBin[l][j], reads=[B_WBin[l][j]], writes=[B_ring[s]])
                slots[j] = s

            blocks = [0, 1, 2, 3, 4, 5, 6] if need_out else [2, 5]
            for j in blocks[:3]:
                load_block(j)
            for f in pending:
                f()
            pending = []
            norm_mod(XG, B_XG, W, SQ, B_SQ, BK_SS, RS, B_RS, TMP, B_TMP, DER[l][:, 0, w, :], DER[l][:, 1, w, :], HT, B_HT, B_modA[l])

            def fm_chunk(j, c):
                s = slots[j]
                b = mbank()
                for kc in range(KC):
                    P.mm(ps[:, b, :W], RING[s][:, kc, c * 128:(c + 1) * 128], HT[:, kc, :W], kc == 0, kc == KC - 1,
                         reads=[B_ring[s], B_HT[kc]], writes=[PSB[b]])
                return b

            def qk_post(b, gain, dst_rows, B_dst):
                i = qctr[0] % 2
                qs = qctr[0] % 3
                qb = (BK_QSS, BK_TR)[qctr[0] % 2]
                qctr[0] += 1
                P.act(QF[i][:, :W], ps[:, b, :W], AF.Identity, reads=[PSB[b], B_sm[l]], writes=[B_QF[i]], scale=gain)
                P.act(QSQ[i][:, :W], ps[:, b, :W], AF.Square, reads=[PSB[b]], writes=[B_QSQ[i]])
                P.mm(ps[:, qb, :W], ones_bf, QSQ[i][:, :W], True, True, reads=[B_QSQ[i], B_const], writes=[PSB[qb]])
                P.act(QRS[i][:, :W], ps[:, qb, :W], AF.Ln, reads=[PSB[qb]], writes=[B_QRS[i]], scale=1.0 / 128, bias=EPSB[:, 0:1])
                P.act(ps[:, qb, :W], QRS[i][:, :W], AF.Exp, reads=[B_QRS[i]], writes=[PSB[qb]], scale=-0.5)
                if lat:
                    P.mm(ps[:, BK_ROT, :W], rot_f, QF[i][:, :W], True, True, reads=[B_QF[i], B_const], writes=[PSB[BK_ROT]])
                    P.stt(T1[i][:, :W], ps[:, b, :W], gain, COS[:, :W], ALU.mult, ALU.mult, reads=[PSB[b], B_sm[l], B_CS], writes=[B_T1[i]])
                    P.tt("dve", T2[i][:, :W], ps[:, BK_ROT, :W], SIN[:, :W], ALU.mult, reads=[PSB[BK_ROT], B_CS], writes=[B_T2[i]])
                    P.tt("dve", T1[i][:, :W], T1[i][:, :W], T2[i][:, :W], ALU.add, reads=[B_T1[i], B_T2[i]], writes=[B_T1[i]])
                    P.tt("dve", QS[qs][:, :W], T1[i][:, :W], ps[:, qb, :W], ALU.mult, reads=[B_T1[i], PSB[qb]], writes=[B_QS[qs]])
                else:
                    P.tt("dve", QS[qs][:, :W], QF[i][:, :W], ps[:, qb, :W], ALU.mult, reads=[B_QF[i], PSB[qb]], writes=[B_QS[qs]])
                P.dma("sp", dst_rows[:, t0:t0 + W], QS[qs][:, :W], reads=[B_QS[qs]], writes=[B_dst])

            prevq = []
            for bi, j in enumerate(blocks):
                if j in (0, 1):
                    for c in range(4):
                        b = fm_chunk(j, c)
                        h = j * 4 + c
                        if prevq:
                            qk_post(*prevq.pop())
                        prevq.append((b, gq, QT[h * 128:(h + 1) * 128, :], B_QT[h][gi]))
                elif j == 2:
                    for c in range(2):
                        b = fm_chunk(j, c)
                        if prevq:
                            qk_post(*prevq.pop())
                        prevq.append((b, gk, KT[c * 128:(c + 1) * 128, :], B_KT[c][gi]))
                    s = slots[j]
                    for tt in range(ntt):
                        b = mbank()
                        for kc in range(KC):
                            P.mm(ps[:, b, 0:256], HT[:, kc, tt * 128:(tt + 1) * 128], RING[s][:, kc, 256:512], kc == 0, kc == KC - 1,
                                 reads=[B_ring[s], B_HT[kc]], writes=[PSB[b]])
                        if prevq:
                            qk_post(*prevq.pop())
                        P.copy("act", VS[:, tt, :], ps[:, b, 0:256], reads=[PSB[b]], writes=[B_VS])
                    pending.append(lambda t0=t0, W=W, gi=gi, ntt=ntt: P.dma("sp", VTOK[t0:t0 + W, :].rearrange("(tt p) n -> p tt n", p=128), VS[:, 0:ntt, :],
                                                 reads=[B_VS], writes=[B_V[gi]]))
                elif j == 3:
                    for c in range(4):
                        b = fm_chunk(j, c)
                        P.act(UT[:, c, :W], ps[:, b, :W], AF.Gelu, reads=[PSB[b]], writes=[B_UT])
                elif j == 4:
                    s = slots[j]
                    for tt in range(ntt):
                        b = mbank()
                        for kc in range(KC):
                            P.mm(ps[:, b, :], HT[:, kc, tt * 128:(tt + 1) * 128], RING[s][:, kc, :], kc == 0, kc == KC - 1,
                                 reads=[B_ring[s], B_HT[kc]], writes=[PSB[b]])
                        P.act(VV[tt], ps[:, b, :], AF.Gelu, reads=[PSB[b]], writes=[B_VV[tt]])
                    for tt in range(ntt):
                        b = mbank()
                        for g in range(4):
                            P.mm(ps[:, b, g * 128:(g + 1) * 128], VV[tt][:, g * 128:(g + 1) * 128], WSTl[:, g, :], True, True,
                                 reads=[B_VV[tt], B_ws], writes=[PSB[b]])
                        P.tt("dve", ST[tt % 2], ps[:, b, :].rearrange("p (g q) -> p g q", g=4), BSBl, ALU.add,
                             reads=[PSB[b], B_ws], writes=[B_ST[tt % 2]])
                        P.tt("dve", MS[:, :, tt * 128:(tt + 1) * 128], ST[tt % 2], UT[:, :, tt * 128:(tt + 1) * 128], ALU.mult,
                             reads=[B_ST[tt % 2], B_UT], writes=[B_MS])
                    pending.append(lambda t0=t0, W=W, gi=gi: P.dma("sp", MIXT[1024:1536, :].rearrange("(g p) t -> p g t", p=128)[:, :, t0:t0 + W],
                                                 MS[:, :, :W], reads=[B_MS], writes=[B_MIX[8 + g][gi] for g in range(4)]))
                elif j == 5:
                    for c in range(4):
                        b = fm_chunk(j, c)
                        P.copy("act", ZS[:, c, :W], ps[:, b, :W], reads=[PSB[b]], writes=[B_ZS])
                    pending.append(lambda t0=t0, W=W, gi=gi: P.dma("sp", ZXT.rearrange("(g p) t -> p g t", p=128)[:, :, t0:t0 + W], ZS[:, :, :W],
                                                 reads=[B_ZS], writes=[B_ZX[g][gi] for g in range(4)]))
                elif j == 6:
                    for c in range(4):
                        b = fm_chunk(j, c)
                        P.act(GS[:, c, :W], ps[:, b, :W], AF.Gelu, reads=[PSB[b]], writes=[B_GS])
                    pending.append(lambda t0=t0, W=W, gi=gi: P.dma("sp", GZT.rearrange("(g p) t -> p g t", p=128)[:, :, t0:t0 + W], GS[:, :, :W],
                                                 reads=[B_GS], writes=[B_GZ[g][gi] for g in range(4)]))
                ring_pending.discard(slots[j])
                if bi + 3 < len(blocks):
                    load_block(blocks[bi + 3])
                run_bg(1)
        for f in pending:
            f()

    def phase_B(l):
        arena.reset()
        last = l == NL - 1
        ZXb = arena.alloc((T,), BF16)
        GZb = arena.alloc((T,), BF16)
        XRb = arena.alloc((T,), BF16)
        LS = arena.alloc((T,), BF16)
        XR = arena.alloc((T,), F32)
        Rg = arena.alloc((T,), F32)
        Ig = arena.alloc((T,), F32)
        Aa = arena.alloc((T,), F32)
        Mm = arena.alloc((T,), F32)
        HF = arena.alloc((T,), F32)
        HB = arena.alloc((T,), F32)
        B_ZXb, B_GZb, B_XRb, B_LS, B_XR, B_R, B_I, B_A, B_M, B_HF, B_HB = bufs(11, "lru")
        QTh = [arena.alloc((T,), BF16) for _ in range(2)]
        B_QTh = bufs(2)
        KTk = arena.alloc((T,), BF16)
        B_KTk = Buf()
        Vk = arena.alloc((18, 128), BF16)
        B_Vk = Buf()
        PT = [arena.alloc((2, 512), BF16) for _ in range(3)]
        B_PT = bufs(3)
        RSUM = arena.alloc((512,), F32)
        B_RSUM = Buf()
        OS = [arena.alloc((512,), BF16) for _ in range(2)]
        B_OS = bufs(2)
        LWl = arena.alloc((16, 128), BF16)
        lwl = arena.alloc((16, 128), F32)
        B_lwl = Buf()
        B_lw = Buf()
        P.dma("sp", lwl, I["lru_w"][l].rearrange("d g h i j -> i (d g h) j"), writes=[B_lwl])
        P.copy("dve", LWl, lwl, reads=[B_lwl], writes=[B_lw])
        OBK = [4, 5]
        UBK = [6, 7]
        GBK = [0, 1, 2, 3]
        sw = SMA[l]
        cw = sw[:, 96:112]
        cb = sw[:, 112:116]
        lb = SMB[l][:, 64:80]
        ctrs = {"s": 0, "o": 0, "g": 0, "p": 0, "os": 0, "q": 0}
        allg = list(range(NG))

        def lru_unit(h, d):
            if d == 0:
                P.dma("sp", ZXb, ZXT[h * 128:(h + 1) * 128, :], reads=B_ZX[h], writes=[B_ZXb])
                P.dma("sp", GZb, GZT[h * 128:(h + 1) * 128, :], reads=B_GZ[h], writes=[B_GZb])
                for (s0, L) in ((0, NCTX), (NCTX, NLAT)):
                    P.act(XR[:, s0:s0 + L], ZXb[:, s0:s0 + L], AF.Identity, reads=[B_ZXb, B_sm[l]], writes=[B_XR],
                          scale=cw[:, 2 * 4 + h:2 * 4 + h + 1], bias=cb[:, h:h + 1])
                    for tap in (0, 1, 3):
                        o = tap - 2
                        lo = max(0, -o)
                        hi = L - max(0, o)
                        P.stt(XR[:, s0 + lo:s0 + hi], ZXb[:, s0 + lo + o:s0 + hi + o], cw[:, tap * 4 + h:tap * 4 + h + 1],
                              XR[:, s0 + lo:s0 + hi], ALU.mult, ALU.add, reads=[B_ZXb, B_XR, B_sm[l]], writes=[B_XR])
                P.copy("act", XRb, XR, reads=[B_XR], writes=[B_XRb])
            for (t0, W) in ((0, 512), (512, 512), (1024, 512), (1536, 512), (2048, 256)):
                for gate, dst, B_dst in ((0, Rg, B_R), (1, Ig, B_I)):
                    b = GBK[ctrs["g"] % 4]
                    ctrs["g"] += 1
                    P.mm(ps[:, b, :W], LWl[:, (d * 2 + gate) * 4 + h, :], XRb[:, t0:t0 + W], True, True,
                         reads=[B_lw, B_XRb], writes=[PSB[b]])
                    P.act(dst[:, t0:t0 + W], ps[:, b, :W], AF.Sigmoid, reads=[PSB[b]], writes=[B_dst],
                          bias=lb[:, (d * 2 + gate) * 4 + h:(d * 2 + gate) * 4 + h + 1])
            P.act(Aa, Rg, AF.Exp, reads=[B_R], writes=[B_A], scale=CL[l][:, 0, d * 4 + h:d * 4 + h + 1])
            P.act(Mm, Rg, AF.Exp, reads=[B_R], writes=[B_M], scale=CL[l][:, 1, d * 4 + h:d * 4 + h + 1])
            P.act(Mm, Mm, AF.Sqrt, reads=[B_M], writes=[B_M], scale=-1.0, bias=ONEB[:, 0:1])
            P.tt("dve", Ig, Ig, XR, ALU.mult, reads=[B_I, B_XR], writes=[B_I])
            P.tt("dve", Ig, Ig, Mm, ALU.mult, reads=[B_I, B_M], writes=[B_I])
            if d == 0:
                P.scan(HF, Aa, Ig, 0.0, reads=[B_A, B_I], writes=[B_HF])
            else:
                P.scan(HB[:, 0:NCTX][:, ::-1], Aa[:, 0:NCTX][:, ::-1], Ig[:, 0:NCTX][:, ::-1], 0.0, reads=[B_A, B_I], writes=[B_HB])
                P.scan(HB[:, NCTX:T][:, ::-1], Aa[:, NCTX:T][:, ::-1], Ig[:, NCTX:T][:, ::-1], HB[:, 0:1],
                       reads=[B_A, B_I, B_HB], writes=[B_HB])
                P.tt("dve", HF, HF, HB, ALU.add, reads=[B_HF, B_HB], writes=[B_HF])
                P.tt("dve", LS, HF, GZb, ALU.mult, reads=[B_HF, B_GZb], writes=[B_LS])
                P.dma("sp", MIXT[1536 + h * 128:1536 + (h + 1) * 128, :], LS, reads=[B_LS], writes=B_MIX[12 + h])

        def attn_block(c, kv, qi, q0, W, kcs):
            pairs = [kcs[i:i + 2] for i in range(0, len(kcs), 2)]

            def s_pair(pr):
                pb = ctrs["s"] % 2
                ctrs["s"] += 1
                for i, kc in enumerate(pr):
                    P.mm(ps[:, 2 * pb + i, :W], KTk[:, kc * 128:(kc + 1) * 128], QTh[qi][:, q0:q0 + W], True, True,
                         reads=[B_KTk, B_QTh[qi]], writes=[PSB[2 * pb + i]])
                return pb

            ob = OBK[ctrs["o"] % 2]
            ub = UBK[ctrs["o"] % 2]
            ctrs["o"] += 1
            nb = s_pair(pairs[0])
            nk = len(kcs)
            done = 0
            for ip, pr in enumerate(pairs):
                pb = nb
                pi = ctrs["p"] % 3
                ctrs["p"] += 1
                n2 = len(pr)
                P.act(PT[pi][:, 0:n2, :W], ps[:, 2 * pb:2 * pb + n2, :W], AF.Exp,
                      reads=[PSB[2 * pb + i] for i in range(n2)] + [B_sm[l]], writes=[B_PT[pi]],
                      scale=QSCALE, bias=NEGB[l][:, 0:1])
                if ip + 1 < len(pairs):
                    nb = s_pair(pairs[ip + 1])
                for i, kc in enumerate(pr):
                    P.mm(ps[:, ob, :W], Vk[:, kc, :], PT[pi][:, i, :W], done == 0, done == nk - 1, reads=[B_Vk, B_PT[pi]], writes=[PSB[ob]])
                    P.mm(ps[:, ub, :W], ones_bf, PT[pi][:, i, :W], done == 0, done == nk - 1, reads=[B_const, B_PT[pi]], writes=[PSB[ub]])
                    done += 1
            P.recip(RSUM[:, :W], ps[:, ub, :W], reads=[PSB[ub]], writes=[B_RSUM])
            oi = ctrs["os"] % 2
            ctrs["os"] += 1
            P.tt("dve", OS[oi][:, :W], ps[:, ob, :W], RSUM[:, :W], ALU.mult, reads=[PSB[ob], B_RSUM], writes=[B_OS[oi]])
            gidx = [g for g, (t0, w_) in enumerate(GROUPS) if t0 == q0][0]
            P.dma("sp", MIXT[c * 128:(c + 1) * 128, q0:q0 + W], OS[oi][:, :W], reads=[B_OS[oi]], writes=[B_MIX[c][gidx]])

        def attn_head(c):
            kv = c // 4
            if c % 4 == 0:
                P.dma("sp", KTk, KT[kv * 128:(kv + 1) * 128, :], reads=B_KT[kv], writes=[B_KTk])
                P.dma("sp", Vk, VTOK[:, kv * 128:(kv + 1) * 128].rearrange("(kc p) d -> p kc d", p=128), reads=B_V, writes=[B_Vk])
            qi = ctrs["q"] % 2
            ctrs["q"] += 1
            if last:
                P.dma("sp", QTh[qi][:, NCTX:T], QT[c * 128:(c + 1) * 128, NCTX:T], reads=B_QT[c][1:], writes=[B_QTh[qi]])
            else:
                P.dma("sp", QTh[qi], QT[c * 128:(c + 1) * 128, :], reads=B_QT[c], writes=[B_QTh[qi]])
            for g in range(1, NG):
                q0, W = GROUPS[g]
                attn_block(c, kv, qi, q0, W, list(range(18)))
                run_bg(1)
            if not last:
                attn_block(c, kv, qi, 0, NCTX, [0, 1])

        units = [(h, d) for h in range(4) for d in range(2)]
        for i in range(8):
            attn_head(i)
            lru_unit(*units[i])

    ONEB = pers.alloc((2,), F32)

    H2T = scratch("H2T", [D, T], BF16)
    B_H2T = bufs(NG)

    def proj_norm_update(l, W, nkc, SRC, B_SRC, wload, gvec, XGt, B_XGt, YT, B_YT, SQ, B_SQ, RS, B_RS, TMP, B_TMP, mbank, BK_SS, allow_bg=True, mid=None, after_first_load=None, defer=None, after_block=None):
        sqs = []
        for j in range(4):
            parts = wload(j)
            if j == 0 and after_first_load is not None:
                after_first_load()
            for c in range(4):
                n = j * 4 + c
                b = mbank()
                for (s, k_lo, k_hi) in parts:
                    for kc in range(k_lo, k_hi):
                        P.mm(ps[:, b, :W], RING[s][:, kc - k_lo, c * 128:(c + 1) * 128], SRC[:, kc, :W], kc == 0, kc == nkc - 1,
                             reads=[B_ring[s], B_SRC[kc]], writes=[PSB[b]])
                P.copy("act", YT[:, n, :W], ps[:, b, :W], reads=[PSB[b]], writes=[B_YT[n]])
                sq = SQ[n % 2]
                P.act(sq[:, :W], ps[:, b, :W], AF.Square, reads=[PSB[b]], writes=[B_SQ[n % 2]])
                if c == 3 and allow_bg:
                    run_bg(1)
                if c == 3 and after_block is not None:
                    after_block(j)
                sqs.append(n)
                if len(sqs) > 1:
                    m = sqs[-2]
                    P.mm(ps[:, BK_SS, :W], ones_bf, SQ[m % 2][:, :W], m == 0, False, reads=[B_SQ[m % 2], B_const], writes=[PSB[BK_SS]])
        m = sqs[-1]
        P.mm(ps[:, BK_SS, :W], ones_bf, SQ[m % 2][:, :W], False, True, reads=[B_SQ[m % 2], B_const], writes=[PSB[BK_SS]])
        if mid is not None:
            mid()
        steps = []

        def s0():
            P.act(RS[:, :W], ps[:, BK_SS, :W], AF.Ln, reads=[PSB[BK_SS]], writes=[B_RS], scale=1.0 / D, bias=EPSB[:, 0:1])
            P.act(RS[:, :W], RS[:, :W], AF.Exp, reads=[B_RS], writes=[B_RS], scale=-0.5)
        steps.append(s0)
        for kc in range(KC):
            def sk(kc=kc):
                P.tt("dve", YT[:, kc, :W], YT[:, kc, :W], RS[:, :W], ALU.mult, reads=[B_YT[kc], B_RS], writes=[B_YT[kc]])
                P.stt(XGt[:, kc, :W], YT[:, kc, :W], gvec[:, kc:kc + 1], XGt[:, kc, :W], ALU.mult, ALU.add,
                      reads=[B_YT[kc], B_XGt[kc], B_modB[l]], writes=[B_XGt[kc]])
            steps.append(sk)
        if defer is None:
            for f_ in steps:
                f_()
        else:
            defer.extend(steps)

    def phase_C1(l):
        arena.reset()
        last = l == NL - 1
        XG1 = arena.alloc((KC, 512), F32)
        XG = [XG1, XG1]
        B_XG1 = bufs(KC, "xg")
        B_XG = [B_XG1, B_XG1]
        YT = arena.alloc((KC, 512), F32)
        B_YT = bufs(KC, "yt")
        HT = [arena.alloc((KC, 512), BF16) for _ in range(2)]
        B_HT = [bufs(KC, "ht") for _ in range(2)]
        SQ = [arena.alloc((512,), BF16) for _ in range(2)]
        B_SQ = bufs(2)
        RS = [arena.alloc((512,), F32) for _ in range(2)]
        B_RS = bufs(2)
        TMP = [arena.alloc((512,), F32) for _ in range(2)]
        B_TMP = bufs(2)
        mainctr = [0]

        def mbank():
            b = mainctr[0] % 4
            mainctr[0] += 1
            return b

        XTv = XT.rearrange("(kc p) t -> p kc t", p=128)
        MXv = MIXT.rearrange("(kc p) t -> p kc t", p=128)
        glist = [gi for gi in range(NG) if not (last and gi == 0)]

        def ht_load(k):
            gi = glist[k]
            t0, W = GROUPS[gi]
            P.dma("sp", HT[k % 2][:, :, :W], MXv[:, :, t0:t0 + W], reads=[B_MIX[c_][gi] for c_ in range(16)], writes=B_HT[k % 2])

        def xg_load(k):
            gi = glist[k]
            t0, W = GROUPS[gi]
            for q4 in range(4):
                P.dma("sp", XG1[:, q4 * 4:(q4 + 1) * 4, :W], XTv[:, q4 * 4:(q4 + 1) * 4, t0:t0 + W], reads=[B_XT[gi]],
                      writes=B_XG1[q4 * 4:(q4 + 1) * 4])

        def w_loads():
            parts = []
            for j in range(4):
                s = ring_next()
                P.dma("sp", RING[s], WBout[l][j], reads=[B_WBout[l][j]], writes=[B_ring[s]])
                parts.append([(s, 0, KC)])
            return parts

        ht_load(0)
        xg_load(0)
        parts = w_loads()
        for k, gi in enumerate(glist):
            t0, W = GROUPS[gi]
            w = 1 if gi == 0 else 0
            i = k % 2
            nxt = {}

            def ablk(j, k=k):
                if k + 1 < len(glist):
                    if j == 0:
                        ht_load(k + 1)
                        nxt["parts"] = []
                    s_ = ring_next()
                    P.dma("sp", RING[s_], WBout[l][j], reads=[B_WBout[l][j]], writes=[B_ring[s_]])
                    nxt["parts"].append([(s_, 0, KC)])

            proj_norm_update(l, W, KC, HT[i], B_HT[i], (lambda j, parts=parts: parts[j]), DER[l][:, 2, w, :], XG1, B_XG1, YT, B_YT,
                             SQ, B_SQ, RS[0], B_RS[0], TMP, B_TMP, mbank, 4, allow_bg=False, after_block=ablk)
            P.dma("sp", XTv[:, :, t0:t0 + W], XG1[:, :, :W], reads=B_XG1, writes=[B_XT[gi]])
            norm_mod(XG1, B_XG1, W, SQ, B_SQ, 5, RS[1], B_RS[1], TMP, B_TMP, DER[l][:, 3, w, :], DER[l][:, 4, w, :],
                     HT[i], B_HT[i], B_modB[l])
            P.dma("sp", H2T.rearrange("(kc p) t -> p kc t", p=128)[:, :, t0:t0 + W], HT[i][:, :, :W], reads=B_HT[i], writes=[B_H2T[gi]])
            if k + 1 < len(glist):
                xg_load(k + 1)
                parts = nxt["parts"]

    def phase_C2(l):
        arena.reset()
        last = l == NL - 1
        XG = arena.alloc((KC, 512), F32)
        B_XG = bufs(KC, "xg")
        YT = arena.alloc((KC, 512), F32)
        B_YT = bufs(KC, "yt")
        HT = arena.alloc((KC, 512), BF16)
        B_HT = bufs(KC, "ht")
        AT = arena.alloc((FC, 512), BF16)
        B_AT = bufs(FC, "at")
        SQ = [arena.alloc((512,), BF16) for _ in range(2)]
        B_SQ = bufs(2)
        RS = arena.alloc((512,), F32)
        B_RS = Buf()
        TMP = [arena.alloc((512,), F32) for _ in range(2)]
        B_TMP = bufs(2)
        SG = TMP
        B_SG = B_TMP
        BK_TR = [6, 7]
        mainctr = [0]

        def mbank():
            b = mainctr[0] % 4
            mainctr[0] += 1
            return b

        XTv = XT.rearrange("(kc p) t -> p kc t", p=128)
        glist2 = [gi for gi in range(NG) if not (last and gi == 0)]

        def ht2_load(gi):
            t0_, W_ = GROUPS[gi]
            P.dma("sp", HT[:, :, :W_], H2T.rearrange("(kc p) t -> p kc t", p=128)[:, :, t0_:t0_ + W_], reads=[B_H2T[gi]], writes=B_HT)

        deferred = []
        for gi, (t0, W) in enumerate(GROUPS):
            if last and gi == 0:
                continue
            w = 1 if gi == 0 else 0
            if gi == glist2[0]:
                ht2_load(gi)
            for j in range(11):
                sg_ = ring_next()
                P.dma("sp", RING[sg_], WBf1[l][j], reads=[B_WBf1[l][j]], writes=[B_ring[sg_]])
                su_ = ring_next()
                P.dma("sp", RING[su_], WBf1[l][11 + j], reads=[B_WBf1[l][11 + j]], writes=[B_ring[su_]])
                if j == 6:
                    P.dma("sp", XG[:, :, :W], XTv[:, :, t0:t0 + W], reads=[B_XT[gi]], writes=B_XG)
                for c in range(4):
                    fc = j * 4 + c
                    bg_ = mbank()
                    for kc in range(KC):
                        P.mm(ps[:, bg_, :W], RING[sg_][:, kc, c * 128:(c + 1) * 128], HT[:, kc, :W], kc == 0, kc == KC - 1,
                             reads=[B_ring[sg_], B_HT[kc]], writes=[PSB[bg_]])
                    bu = mbank()
                    for kc in range(KC):
                        P.mm(ps[:, bu, :W], RING[su_][:, kc, c * 128:(c + 1) * 128], HT[:, kc, :W], kc == 0, kc == KC - 1,
                             reads=[B_ring[su_], B_HT[kc]], writes=[PSB[bu]])
                    sg = SG[fc % 2]
                    P.act(sg[:, :W], ps[:, bg_, :W], AF.Silu, reads=[PSB[bg_]], writes=[B_SG[fc % 2]])
                    P.tt("dve", AT[:, fc, :W], sg[:, :W], ps[:, bu, :W], ALU.mult, reads=[B_SG[fc % 2], PSB[bu]], writes=[B_AT[fc]])
                    for _ in range(2):
                        if deferred:
                            deferred.pop(0)()
                run_bg(1)
            while deferred:
                deferred.pop(0)()
            nx = [g_ for g_ in glist2 if g_ > gi]

            def afl(nx=nx):
                if nx:
                    ht2_load(nx[0])

            def wl_f2(j):
                parts = []
                for (k_lo, k_hi) in ((0, 16), (16, 32), (32, 44)):
                    s = ring_next()
                    P.dma("sp", RING[s][:, 0:k_hi - k_lo, :], WBf2[l][j][:, k_lo:k_hi, :], reads=[B_WBf2[l][j]], writes=[B_ring[s]])
                    parts.append((s, k_lo, k_hi))
                return parts

            proj_norm_update(l, W, FC, AT, B_AT, wl_f2, DER[l][:, 5, w, :], XG, B_XG, YT, B_YT, SQ, B_SQ, RS, B_RS, TMP, B_TMP, mbank, 4,
                             after_first_load=afl, defer=deferred)

            def fin(gi=gi, t0=t0, W=W):
                if not last:
                    P.dma("sp", XTv[:, :, t0:t0 + W], XG[:, :, :W], reads=B_XG, writes=[B_XT[gi]])
                    return
                for tt in range(W // 128):
                    OT = YT.rearrange("p k t -> p (k t)")[:, (tt % 2) * D:(tt % 2 + 1) * D]
                    B_OT = B_YT[(tt % 2) * 4:(tt % 2) * 4 + 4]
                    for k4 in range(4):
                        b = BK_TR[k4 % 2]
                        for k in range(4):
                            kc = k4 * 4 + k
                            P.tr(ps[:, b, k * 128:(k + 1) * 128], XG[:, kc, tt * 128:(tt + 1) * 128], idn_f,
                                 reads=[B_XG[kc], B_const], writes=[PSB[b]])
                        P.copy("dve" if k4 % 2 else "act", OT[:, k4 * 512:(k4 + 1) * 512], ps[:, b, :], reads=[PSB[b]], writes=B_OT)
                    r0 = t0 - NCTX + tt * 128
                    out_ops.append(P.dma("sp", out_d[r0:r0 + 128, :], OT, reads=B_OT))
            deferred.append(fin)
        while deferred:
            deferred.pop(0)()

    P.memset("dve", EPSB, EPS, writes=[B_const])
    P.memset("dve", ONEB, 1.0, writes=[B_const])
    for j in range(3):
        conv_one(WBin[0][j], I["w_in"][0][:, 512 * j:512 * (j + 1)], B_WBin[0][j])
    preamble()
    for t_ in mod_tasks(0, range(8)):
        t_()
    der_A(0)
    for j in range(3, 7):
        conv_one(WBin[0][j], I["w_in"][0][:, 512 * j:512 * (j + 1)], B_WBin[0][j])

    def interleave(a_, b_):
        out = []
        na, nb = len(a_), len(b_)
        ia = ib = 0
        while ia < na or ib < nb:
            if ib >= nb or (ia < na and ia * nb <= ib * na):
                out.append(a_[ia])
                ia += 1
            else:
                out.append(b_[ib])
                ib += 1
        return out

    a0 = mod_tasks(0, range(8, 24)) + [lambda: der_B(0)]
    a0 = interleave(a0, [(lambda: None)] * (35 - len(a0)))
    bg.extend(a0 + conv_tasks(0, ("out", "f1")))
    seq = [("P", None, 0)]
    for l in range(NL):
        seq += [("A%d" % l, phase_A, l), ("B%d" % l, phase_B, l), ("C%d" % l, phase_C1, l), ("D%d" % l, phase_C2, l)]
    for name, fn, l in seq:
        if fn is not None:
            if fn is phase_C1:
                run_bg(len(bg))
                if l == 0:
                    for t_ in conv_tasks(0, ("f2",)):
                        t_()
                    bg.extend([lambda j=j: conv_one(WBin[1][j], I["w_in"][1][:, 512 * j:512 * (j + 1)], B_WBin[1][j]) for j in range(7)])
                    bg.extend(interleave(mod_tasks(1, range(24)) + [lambda: der_A(1), lambda: der_B(1)],
                                         conv_tasks(1, ("out", "f1", "f2"))))
            fn(l)
            if fn is phase_C2:
                run_bg(len(bg))
            P.barrier()
        if stop == name:
            break
    if debug:
        dt = arena.ap[:, 0:2048]
        P.barrier()
        B_d = Buf()
        P.copy("dve", dt[:, 0:192], MOD[0].rearrange("p w c -> p (w c)"), writes=[B_d])
        P.copy("dve", dt[:, 192:384], MOD[1].rearrange("p w c -> p (w c)"), writes=[B_d])
        P.copy("dve", dt[:, 384:576], DER[0].rearrange("p a w c -> p (a w c)"), writes=[B_d])
        P.copy("dve", dt[:, 576:592], CL[0].rearrange("p a c -> p (a c)"), writes=[B_d])
        P.copy("dve", dt[:, 592:594], NEGB[0], writes=[B_d])
        P.copy("dve", dt[:, 600:728], SMA[0], writes=[B_d])
        P.copy("dve", dt[:, 728:856], SMB[0], writes=[B_d])
        out_ops.append(P.dma("sp", DBG, dt, reads=[B_d]))
    P.emit(final_wait_ops=out_ops)
    return nc, P


def _consts():
    n = np.arange(NLAT)
    row = (n // 64).astype(np.float32)
    col = (n % 64).astype(np.float32)
    inv = (np.float32(10000.0) ** (-np.arange(0, 64, 2, dtype=np.float32) / np.float32(64))).astype(np.float32)
    ar = row[:, None] * inv
    ac = col[:, None] * inv
    ang = np.concatenate([ar, ar, ac, ac], axis=-1).astype(np.float32)
    cosT = np.ascontiguousarray(np.cos(ang).T.astype(np.float32))
    sinT = np.ascontiguousarray(np.sin(ang).T.astype(np.float32))
    rot = np.zeros((128, 128), np.float32)
    for a in range(2):
        for i in range(32):
            rot[a * 64 + 32 + i, a * 64 + i] = -1.0
            rot[a * 64 + i, a * 64 + 32 + i] = 1.0
    return {"k_cos": cosT, "k_sin": sinT, "k_rot": rot, "k_idn": np.eye(128, dtype=np.float32)}


_CACHE = {}


def kernel(**inputs):
    if "nc" not in _CACHE:
        _CACHE["nc"] = build()[0]
    nc = _CACHE["nc"]
    consts = _consts()
    B = inputs["x"].shape[0]
    shared = {k: np.ascontiguousarray(np.asarray(v, dtype=np.float32)) for k, v in inputs.items() if k not in ("x", "c", "ctx")}
    in_maps = []
    for b in range(B):
        m = dict(shared)
        m.update(consts)
        m["x"] = np.ascontiguousarray(np.asarray(inputs["x"][b], dtype=np.float32))
        m["c"] = np.ascontiguousarray(np.asarray(inputs["c"][b], dtype=np.float32))
        m["ctx"] = np.ascontiguousarray(np.asarray(inputs["ctx"][b], dtype=np.float32))
        in_maps.append(m)
    res = run_bass_kernel_spmd(nc, in_maps, core_ids=list(range(B)))
    return np.stack([np.asarray(r["out"], dtype=np.float32) for r in res.results], axis=0)
```

```python
import numpy as np
import ml_dtypes
from contextlib import ExitStack
import concourse.bass as bass
import concourse.mybir as mybir
from concourse.bass_utils import run_bass_kernel_spmd

F32 = mybir.dt.float32
BF16 = mybir.dt.bfloat16
AF = mybir.ActivationFunctionType
ALU = mybir.AluOpType

ENGS = ("pe", "act", "dve", "pool", "sp")
NDSEM = {"sp": 12, "act": 2, "pool": 4}

D = 2048
NLAT = 2048
NCTX = 256
T = NCTX + NLAT
DFF = 5632
NL = 2
KC = D // 128
FC = DFF // 128
GROUPS = [(0, 256), (256, 512), (768, 512), (1280, 512), (1792, 512)]
EPS = 1e-6
QSCALE = 128.0 ** -0.5


class Buf:
    __slots__ = ("w", "r", "rd", "name", "excl")

    def __init__(self, name="", excl=False):
        self.w = None
        self.r = {}
        self.rd = []
        self.name = name
        self.excl = excl


class Op:
    __slots__ = ("eng", "fn", "deps", "sig", "need_sig", "is_dma", "prev_wait")


class Prog:
    def __init__(self, nc):
        self.nc = nc
        self.ops = {e: [] for e in ENGS}
        self.nops = 0
        self.last_dmas = {e: [] for e in ENGS}

    def op(self, eng, fn, reads=(), writes=(), dma=False, extra_deps=()):
        o = Op()
        o.eng = eng
        o.fn = fn
        o.is_dma = dma
        o.need_sig = dma
        o.sig = None
        o.prev_wait = None
        deps = set(extra_deps)
        for b in reads:
            if b.w is not None:
                deps.add(b.w)
            if b.excl:
                for e2, r in b.r.items():
                    if e2 != eng:
                        deps.add(r)
        for b in writes:
            if b.w is not None:
                deps.add(b.w)
            for r in b.r.values():
                deps.add(r)
            for r in b.rd:
                deps.add(r)
        if eng == "pe" and not dma:
            deps = {d for d in deps if d.is_dma or d.eng != "pe"}
        for d in deps:
            d.need_sig = True
        o.deps = deps
        for b in reads:
            if dma:
                b.rd.append(o)
            else:
                b.r[eng] = o
        for b in writes:
            b.w = o
            b.r = {}
            b.rd = []
        self.ops[eng].append(o)
        if dma:
            ld = self.last_dmas[eng]
            ld.append(o)
            if len(ld) > NDSEM[eng]:
                ld.pop(0)
        self.nops += 1
        return o

    def barrier(self):
        deps = []
        for e in ENGS:
            for o in reversed(self.ops[e]):
                if not o.is_dma and o.fn is not None:
                    deps.append(o)
                    break
            if e != "pool":
                deps.extend(self.last_dmas[e])
        for e in ENGS:
            self.op(e, None, extra_deps=deps)

    def dma(self, q, out, in_, reads=(), writes=()):
        return self.op(q, lambda e: e.dma_start(out=out, in_=in_), reads, writes, dma=True)

    def mm(self, out, lhsT, rhs, start, stop, reads=(), writes=()):
        return self.op("pe", lambda e: e.matmul(out, lhsT, rhs, start=start, stop=stop), reads, writes)

    def tr(self, out, in_, ident, reads=(), writes=()):
        return self.op("pe", lambda e: e.transpose(out=out, in_=in_, identity=ident), reads, writes)

    def act(self, out, in_, func, reads=(), writes=(), scale=None, bias=None):
        kw = {}
        if scale is not None:
            kw["scale"] = scale
        if bias is not None:
            kw["bias"] = bias
        return self.op("act", lambda e: e.activation(out=out, in_=in_, func=func, **kw), reads, writes)

    def tt(self, eng, out, in0, in1, op, reads=(), writes=()):
        return self.op(eng, lambda e: e.tensor_tensor(out=out, in0=in0, in1=in1, op=op), reads, writes)

    def stt(self, out, in0, scalar, in1, op0, op1, reads=(), writes=()):
        return self.op("dve", lambda e: e.scalar_tensor_tensor(out=out, in0=in0, scalar=scalar, in1=in1, op0=op0, op1=op1),
                       reads, writes)

    def ts(self, eng, out, in0, s1, op0, s2=None, op1=None, reads=(), writes=()):
        if op1 is None:
            return self.op(eng, lambda e: e.tensor_scalar(out=out, in0=in0, scalar1=s1, scalar2=None, op0=op0), reads, writes)
        return self.op(eng, lambda e: e.tensor_scalar(out=out, in0=in0, scalar1=s1, scalar2=s2, op0=op0, op1=op1), reads, writes)

    def copy(self, eng, out, in_, reads=(), writes=()):
        if eng == "act":
            return self.op("act", lambda e: e.activation(out=out, in_=in_, func=AF.Identity), reads, writes)
        return self.op(eng, lambda e: e.tensor_copy(out=out, in_=in_), reads, writes)

    def recip(self, out, in_, reads=(), writes=()):
        return self.op("dve", lambda e: e.reciprocal(out=out, in_=in_), reads, writes)

    def scan(self, out, d0, d1, initial, reads=(), writes=()):
        return self.op("dve", lambda e: e.tensor_tensor_scan(out=out, data0=d0, data1=d1, initial=initial,
                                                              op0=ALU.mult, op1=ALU.add), reads, writes)

    def memset(self, eng, ap, val, writes=()):
        return self.op(eng, lambda e: e.memset(ap, val), (), writes)

    def emit(self, final_wait_ops=()):
        nc = self.nc
        with ExitStack() as st:
            sems = {e: st.enter_context(nc.semaphore("s_" + e)) for e in ENGS}
            dsems = {e: [st.enter_context(nc.semaphore("d_%s%d" % (e, i))) for i in range(NDSEM[e])]
                     for e in NDSEM}
            semobj = {}
            for e in ENGS:
                cnt = 0
                nd = NDSEM.get(e, 1)
                duse = [0] * nd
                di = 0
                for o in self.ops[e]:
                    if o.is_dma:
                        s = di % nd
                        di += 1
                        key = ("d", e, s)
                        semobj[key] = dsems[e][s]
                        if duse[s]:
                            o.prev_wait = (key, 16 * duse[s])
                        duse[s] += 1
                        o.sig = (key, 16 * duse[s])
                    elif o.need_sig:
                        assert o.fn is not None
                        cnt += 1
                        key = ("c", e)
                        semobj[key] = sems[e]
                        o.sig = (key, cnt)
            finals = [o.sig for o in final_wait_ops]
            block = st.enter_context(nc.Block())

            def run(e, eng):
                waited = {}
                for o in self.ops[e]:
                    waits = {}
                    for d in o.deps:
                        k, v = d.sig
                        if waited.get(k, 0) < v and waits.get(k, 0) < v:
                            waits[k] = v
                    if o.prev_wait is not None:
                        k, v = o.prev_wait
                        if waited.get(k, 0) < v and waits.get(k, 0) < v:
                            waits[k] = v
                    for k, v in waits.items():
                        eng.wait_ge(semobj[k], v)
                        waited[k] = v
                    if o.fn is None:
                        continue
                    ins = o.fn(eng)
                    if o.sig is not None:
                        ins.then_inc(semobj[o.sig[0]], 16 if o.is_dma else 1)
                if e == "sp":
                    fw = {}
                    for k, v in finals:
                        fw[k] = max(fw.get(k, 0), v)
                    for k, v in fw.items():
                        if waited.get(k, 0) < v:
                            eng.wait_ge(semobj[k], v)

            @block.tensor
            def _(eng):
                run("pe", eng)

            @block.scalar
            def _(eng):
                run("act", eng)

            @block.vector
            def _(eng):
                run("dve", eng)

            @block.gpsimd
            def _(eng):
                run("pool", eng)

            @block.sync
            def _(eng):
                run("sp", eng)


class Arena:
    def __init__(self, nc, name, nbytes):
        self.ap = nc.alloc_sbuf_tensor(name, [128, nbytes // 4], F32).ap()
        self.n = nbytes
        self.off = 0

    def reset(self):
        self.off = 0

    def alloc(self, free, dtype):
        esz = 4 if dtype == F32 else 2
        ne = 1
        for s in free:
            ne *= s
        nb = ne * esz
        assert nb % 4 == 0
        nba = (nb + 63) // 64 * 64
        assert self.off + nba <= self.n, ("arena overflow", self.off, nba, self.n)
        v = self.ap[:, self.off // 4:(self.off + nb) // 4]
        if dtype != F32:
            v = v.bitcast(dtype)
        self.off += nba
        if len(free) == 2:
            v = v.rearrange("p (a b) -> p a b", a=free[0])
        elif len(free) == 3:
            v = v.rearrange("p (a b c) -> p a b c", a=free[0], b=free[1])
        elif len(free) == 4:
            v = v.rearrange("p (a b c d) -> p a b c d", a=free[0], b=free[1], c=free[2])
        return v


def bufs(n, name="", excl=False):
    return [Buf(name + str(i), excl) for i in range(n)]


INPUT_SPECS = [
    ("x", [NLAT, D]), ("c", [D]), ("ctx", [NCTX, D]), ("c_ctx", [D]),
    ("w_mod", [NL, D, 6 * D]), ("b_mod", [NL, 6 * D]), ("g_norm", [NL, 4, D]), ("w_in", [NL, D, 3584]),
    ("g_qk", [NL, 2, 128]), ("w_s", [NL, 4, 128, 128]), ("b_s", [NL, 4, 128]), ("conv_w", [NL, 4, 512]),
    ("conv_b", [NL, 512]), ("lru_w", [NL, 2, 2, 4, 128, 128]), ("lru_b", [NL, 2, 2, 512]),
    ("lru_lam", [NL, 2, 512]), ("w_out", [NL, D, D]), ("w_ffn_in", [NL, D, 2 * DFF]), ("w_ffn_out", [NL, DFF, D]),
]


def build(debug=False, stop=None):
    nc = bass.Bass("TRN2", target_bir_lowering=False)
    I = {}
    for name, shape in INPUT_SPECS:
        I[name] = nc.dram_tensor(name, shape, F32, kind="ExternalInput").ap()
    cosT_d = nc.dram_tensor("k_cos", [128, NLAT], F32, kind="ExternalInput").ap()
    sinT_d = nc.dram_tensor("k_sin", [128, NLAT], F32, kind="ExternalInput").ap()
    rot_d = nc.dram_tensor("k_rot", [128, 128], F32, kind="ExternalInput").ap()
    idn_d = nc.dram_tensor("k_idn", [128, 128], F32, kind="ExternalInput").ap()
    out_d = nc.dram_tensor("out", [NLAT, D], F32, kind="ExternalOutput").ap()

    skind = "ExternalOutput" if debug else "Internal"

    def scratch(name, shape, dt, dbg=True):
        return nc.dram_tensor(name, shape, dt, kind=(skind if dbg else "Internal")).ap()

    WBin = [scratch("WBin%d" % l, [7, 128, KC, 512], BF16, False) for l in range(NL)]
    WBout = [scratch("WBout%d" % l, [4, 128, KC, 512], BF16, False) for l in range(NL)]
    WBf1 = [scratch("WBf1%d" % l, [22, 128, KC, 512], BF16, False) for l in range(NL)]
    WBf2 = [scratch("WBf2%d" % l, [4, 128, FC, 512], BF16, False) for l in range(NL)]
    XT = scratch("XT", [D, T], F32)
    QT = scratch("QT", [1024, T], BF16)
    KT = scratch("KT", [256, T], BF16)
    VTOK = scratch("VTOK", [T, 256], BF16)
    MIXT = scratch("MIXT", [D, T], BF16)
    ZXT = scratch("ZXT", [512, T], BF16)
    GZT = scratch("GZT", [512, T], BF16)
    if debug:
        DBG = scratch("DBG", [128, 2048], F32)

    P = Prog(nc)

    pers = Arena(nc, "pers", 10 * 1024 + 4 * 16384)
    ones_bf = pers.alloc((128,), BF16)
    ones_f = pers.alloc((128,), F32)
    idn_f = pers.alloc((128,), F32)
    rot_f = pers.alloc((128,), F32)
    CT = pers.alloc((2, 16), F32)
    CTb = pers.alloc((2, 16), BF16)
    SMA = [pers.alloc((128,), F32) for _ in range(NL)]
    SMB = [pers.alloc((128,), F32) for _ in range(NL)]
    MOD = [pers.alloc((2, 96), F32) for _ in range(NL)]
    DER = [pers.alloc((6, 2, 16), F32) for _ in range(NL)]
    CL = [pers.alloc((2, 8), F32) for _ in range(NL)]
    NEGB = [pers.alloc((2,), F32) for _ in range(NL)]
    PSTG = [pers.alloc((128,), F32) for _ in range(2)]
    PTMP = pers.alloc((128,), F32)
    NRING = 4
    RING = [pers.alloc((KC, 512), BF16) for _ in range(NRING)]
    B_const = Buf("const")
    B_sm = [Buf("sm%d" % l) for l in range(NL)]
    B_ring = bufs(NRING, "ring")
    ring_ctr = [0]

    ring_pending = set()

    def ring_next():
        while True:
            s = ring_ctr[0] % NRING
            ring_ctr[0] += 1
            if s not in ring_pending:
                return s

    arena = Arena(nc, "arena", 133 * 1024)
    ps = nc.alloc_psum_tensor("ps", [128, 8, 512], F32).ap()
    PSB = bufs(8, "psb", excl=True)

    B_WBin = [bufs(7) for _ in range(NL)]
    B_WBout = [bufs(4) for _ in range(NL)]
    B_WBf1 = [bufs(22) for _ in range(NL)]
    B_WBf2 = [bufs(4) for _ in range(NL)]
    NG = len(GROUPS)
    B_XT = bufs(NG)
    B_QT = [bufs(NG) for _ in range(8)]
    B_KT = [bufs(NG) for _ in range(2)]
    B_V = bufs(NG)
    B_MIX = [bufs(NG) for _ in range(16)]
    B_ZX = [bufs(NG) for _ in range(4)]
    B_GZ = [bufs(NG) for _ in range(4)]
    out_ops = []

    def conv_tasks(l, which):
        tasks = []
        if "out" in which:
            tasks += [lambda j=j: conv_one(WBout[l][j], I["w_out"][l][:, 512 * j:512 * (j + 1)], B_WBout[l][j]) for j in range(4)]
        if "f1" in which:
            for j in range(11):
                for jj in (j, 11 + j):
                    tasks.append(lambda jj=jj: conv_one(WBf1[l][jj], I["w_ffn_in"][l][:, 512 * jj:512 * (jj + 1)], B_WBf1[l][jj]))
        if "f2" in which:
            for j in range(4):
                for part in range(4):
                    tasks.append(lambda j=j, part=part: P.dma(
                        "pool", WBf2[l][j][:, 11 * part:11 * (part + 1), :],
                        I["w_ffn_out"][l][1408 * part:1408 * (part + 1), 512 * j:512 * (j + 1)].rearrange("(kc p) n -> p kc n", p=128),
                        writes=[B_WBf2[l][j]]))
        return tasks

    def conv_one(dst, src, B):
        P.dma("pool", dst, src.rearrange("(kc p) n -> p kc n", p=128), writes=[B])

    def conv_w_in(l):
        for j in range(7):
            P.dma("pool", WBin[l][j], I["w_in"][l][:, 512 * j:512 * (j + 1)].rearrange("(kc p) n -> p kc n", p=128),
                  writes=[B_WBin[l][j]])

    def conv_w_out(l):
        for j in range(4):
            P.dma("pool", WBout[l][j], I["w_out"][l][:, 512 * j:512 * (j + 1)].rearrange("(kc p) n -> p kc n", p=128),
                  writes=[B_WBout[l][j]])

    def conv_w_f1(l, js):
        for j in js:
            P.dma("pool", WBf1[l][j], I["w_ffn_in"][l][:, 512 * j:512 * (j + 1)].rearrange("(kc p) n -> p kc n", p=128),
                  writes=[B_WBf1[l][j]])

    def conv_w_f2(l):
        for j in range(4):
            for part in range(4):
                P.dma("pool", WBf2[l][j][:, 11 * part:11 * (part + 1), :],
                      I["w_ffn_out"][l][1408 * part:1408 * (part + 1), 512 * j:512 * (j + 1)].rearrange("(kc p) n -> p kc n", p=128),
                      writes=[B_WBf2[l][j]])

    def preamble():
        stg = PSTG
        B_stg = bufs(2, "stg")
        tmpa = PTMP
        B_tmpa = Buf()
        P.memset("dve", ones_bf, 1.0, writes=[B_const])
        P.memset("dve", ones_f, 1.0, writes=[B_const])
        P.dma("sp", idn_f, idn_d, writes=[B_const])
        P.dma("sp", rot_f, rot_d, writes=[B_const])
        P.dma("sp", stg[0][0:16, :], I["c"].rearrange("(kc p) -> kc p", p=128), writes=[B_stg[0]])
        P.dma("sp", stg[0][16:32, :], I["c_ctx"].rearrange("(kc p) -> kc p", p=128), writes=[B_stg[0]])
        P.tr(ps[:, 0, 0:32], stg[0][0:32, :], idn_f[0:32, 0:32], reads=[B_stg[0], B_const], writes=[PSB[0]])
        P.act(CT.rearrange("p w k -> p (w k)"), ps[:, 0, 0:32], AF.Silu, reads=[PSB[0]], writes=[B_const])
        P.copy("dve", CTb, CT, reads=[B_const], writes=[B_const])
        for l in range(NL):
            s = stg[1]
            P.dma("sp", s[0:96, :], I["b_mod"][l].rearrange("(r p) -> r p", p=128), writes=[B_stg[1]])
            P.dma("sp", s[96:112, :], I["conv_w"][l].rearrange("t (h p) -> (t h) p", p=128), writes=[B_stg[1]])
            P.dma("sp", s[112:116, :], I["conv_b"][l].rearrange("(h p) -> h p", p=128), writes=[B_stg[1]])
            P.dma("sp", s[116:118, :], I["g_qk"][l], writes=[B_stg[1]])
            P.tr(ps[:, 1, 0:118], s[0:118, :], idn_f[0:118, 0:118], reads=[B_stg[1], B_const], writes=[PSB[1]])
            P.copy("dve", SMA[l][:, 0:118], ps[:, 1, 0:118], reads=[PSB[1]], writes=[B_sm[l]])
            s = stg[0]
            P.dma("sp", s[0:64, :], I["g_norm"][l].rearrange("f (kc p) -> (f kc) p", p=128), writes=[B_stg[0]])
            P.dma("sp", s[64:80, :], I["lru_b"][l].rearrange("d g (h p) -> (d g h) p", p=128), writes=[B_stg[0]])
            P.dma("sp", s[80:88, :], I["lru_lam"][l].rearrange("d (h p) -> (d h) p", p=128), writes=[B_stg[0]])
            P.tr(ps[:, 2, 0:88], s[0:88, :], idn_f[0:88, 0:88], reads=[B_stg[0], B_const], writes=[PSB[2]])
            P.copy("dve", SMB[l][:, 0:88], ps[:, 2, 0:88], reads=[PSB[2]], writes=[B_sm[l]])
            P.act(tmpa[:, 0:8], SMB[l][:, 80:88], AF.Exp, reads=[B_sm[l]], writes=[B_tmpa], scale=-1.0)
            P.ts("dve", tmpa[:, 0:8], tmpa[:, 0:8], 1.0, ALU.add, reads=[B_tmpa], writes=[B_tmpa])
            P.act(tmpa[:, 8:16], tmpa[:, 0:8], AF.Ln, reads=[B_tmpa], writes=[B_tmpa])
            P.ts("dve", CL[l][:, 0, :], tmpa[:, 8:16], -8.0, ALU.mult, reads=[B_tmpa], writes=[B_sm[l]])
            P.ts("dve", CL[l][:, 1, :], tmpa[:, 8:16], -16.0, ALU.mult, reads=[B_tmpa], writes=[B_sm[l]])
            s = stg[1]
            P.dma("sp", s[0:2, :], I["g_qk"][l], writes=[B_stg[1]])
            P.op("dve", lambda e, s=s: e.tensor_reduce(out=tmpa[0:2, 16:17], in_=s[0:2, :], axis=mybir.AxisListType.X,
                                                       op=ALU.max, apply_absolute_value=True), reads=[B_stg[1]], writes=[B_tmpa])
            P.ts("dve", tmpa[0:2, 20:22], idn_f[0:2, 0:2], tmpa[0:2, 16:17], ALU.mult, reads=[B_tmpa, B_const], writes=[B_tmpa])
            P.mm(ps[:, 3, 0:2], ones_f[0:2, :], tmpa[0:2, 20:22], True, True, reads=[B_tmpa, B_const], writes=[PSB[3]])
            P.copy("dve", tmpa[:, 24:26], ps[:, 3, 0:2], reads=[PSB[3]], writes=[B_tmpa])
            P.stt(NEGB[l][:, 0:1], tmpa[:, 24:25], -(128.0 ** 0.5), tmpa[:, 25:26], ALU.mult, ALU.mult,
                  reads=[B_tmpa], writes=[B_sm[l]])

    B_modA = [Buf("modA%d" % l) for l in range(NL)]
    B_modB = [Buf("modB%d" % l) for l in range(NL)]
    MODBANK = 3

    def mod_tasks(l, blocks):
        tasks = []
        for j in blocks:
            def task(j=j):
                s = ring_next()
                P.dma("pool", RING[s], I["w_mod"][l][:, 512 * j:512 * (j + 1)].rearrange("(kc p) n -> p kc n", p=128),
                      writes=[B_ring[s]])
                for n in range(4):
                    for kc in range(KC):
                        P.mm(ps[:, MODBANK, n * 2:n * 2 + 2], RING[s][:, kc, n * 128:(n + 1) * 128], CTb[:, :, kc],
                             kc == 0, kc == KC - 1, reads=[B_ring[s], B_const], writes=[PSB[MODBANK]])
                pv = ps[:, MODBANK, 0:8].rearrange("p (c w) -> p c w", w=2)
                Bm = B_modA[l] if j < 8 else B_modB[l]
                for w in range(2):
                    P.tt("dve", MOD[l][:, w, j * 4:(j + 1) * 4], pv[:, :, w], SMA[l][:, j * 4:(j + 1) * 4], ALU.add,
                         reads=[PSB[MODBANK], B_sm[l]], writes=[Bm])
            tasks.append(task)
        return tasks

    def der_A(l):
        gn = SMB[l][:, 0:64].rearrange("p (f k) -> p f k", f=4)
        for w in range(2):
            m = MOD[l][:, w, :].rearrange("p (s k) -> p s k", s=6)
            P.stt(DER[l][:, 0, w, :], m[:, 1, :], 1.0, gn[:, 0, :], ALU.add, ALU.mult, reads=[B_sm[l], B_modA[l]], writes=[B_modA[l]])
            P.copy("dve", DER[l][:, 1, w, :], m[:, 0, :], reads=[B_modA[l]], writes=[B_modA[l]])

    def der_B(l):
        gn = SMB[l][:, 0:64].rearrange("p (f k) -> p f k", f=4)
        for w in range(2):
            m = MOD[l][:, w, :].rearrange("p (s k) -> p s k", s=6)
            P.tt("dve", DER[l][:, 2, w, :], m[:, 2, :], gn[:, 1, :], ALU.mult, reads=[B_sm[l], B_modB[l]], writes=[B_modB[l]])
            P.stt(DER[l][:, 3, w, :], m[:, 4, :], 1.0, gn[:, 2, :], ALU.add, ALU.mult, reads=[B_sm[l], B_modB[l]], writes=[B_modB[l]])
            P.copy("dve", DER[l][:, 4, w, :], m[:, 3, :], reads=[B_modB[l]], writes=[B_modB[l]])
            P.tt("dve", DER[l][:, 5, w, :], m[:, 5, :], gn[:, 3, :], ALU.mult, reads=[B_sm[l], B_modB[l]], writes=[B_modB[l]])

    bg = []

    def run_bg(n=1):
        for _ in range(n):
            if bg:
                bg.pop(0)()

    def norm_stats(src, B_src, W, SQ, B_SQ, bank, RS, B_RS):
        for kc in range(KC):
            sq = SQ[kc % 2]
            P.act(sq[:, :W], src[:, kc, :W], AF.Square, reads=[B_src[kc]], writes=[B_SQ[kc % 2]])
            P.mm(ps[:, bank, :W], ones_bf, sq[:, :W], kc == 0, kc == KC - 1, reads=[B_SQ[kc % 2], B_const], writes=[PSB[bank]])
        P.act(RS[:, :W], ps[:, bank, :W], AF.Ln, reads=[PSB[bank]], writes=[B_RS], scale=1.0 / D, bias=EPSB[:, 0:1])
        P.act(ps[:, bank, :W], RS[:, :W], AF.Exp, reads=[B_RS], writes=[PSB[bank]], scale=-0.5)

    EPSB = pers.alloc((2,), F32)

    def norm_mod(XG, B_XG, W, SQ, B_SQ, bank, RS, B_RS, TMP, B_TMP, A, Bv, HT, B_HT, B_m):
        norm_stats(XG, B_XG, W, SQ, B_SQ, bank, RS, B_RS)
        for kc in range(KC):
            tmp = TMP[kc % 2]
            P.tt("dve", tmp[:, :W], XG[:, kc, :W], ps[:, bank, :W], ALU.mult, reads=[B_XG[kc], PSB[bank]], writes=[B_TMP[kc % 2]])
            P.act(HT[:, kc, :W], tmp[:, :W], AF.Identity, reads=[B_TMP[kc % 2], B_m], writes=[B_HT[kc]],
                  scale=A[:, kc:kc + 1], bias=Bv[:, kc:kc + 1])

    def phase_A(l):
        arena.reset()
        last = l == NL - 1
        XG = arena.alloc((KC, 512), F32)
        B_XG = bufs(KC, "xg")
        HT = arena.alloc((KC, 512), BF16)
        B_HT = bufs(KC, "ht")
        XL = [arena.alloc((D,), F32) for _ in range(2)]
        B_XL = bufs(2, "xl")
        SQ = [arena.alloc((512,), BF16) for _ in range(2)]
        B_SQ = bufs(2)
        RS = arena.alloc((512,), F32)
        B_RS = Buf()
        TMP = [arena.alloc((512,), F32) for _ in range(2)]
        B_TMP = bufs(2)
        COS = arena.alloc((512,), F32)
        SIN = arena.alloc((512,), F32)
        B_CS = Buf()
        QF = [arena.alloc((512,), F32) for _ in range(2)]
        B_QF = bufs(2)
        QSQ = [arena.alloc((512,), BF16) for _ in range(2)]
        B_QSQ = bufs(2)
        QRS = [arena.alloc((512,), F32) for _ in range(2)]
        B_QRS = bufs(2)
        T1 = [arena.alloc((512,), F32) for _ in range(2)]
        B_T1 = bufs(2)
        T2 = [arena.alloc((512,), F32) for _ in range(2)]
        B_T2 = bufs(2)
        QS = [arena.alloc((512,), BF16) for _ in range(3)]
        B_QS = bufs(3)
        VS = arena.alloc((4, 256), BF16)
        B_VS = Buf()
        UT = arena.alloc((4, 512), BF16)
        B_UT = Buf()
        VV = [arena.alloc((512,), BF16) for _ in range(4)]
        B_VV = bufs(4)
        ST = [arena.alloc((4, 128), F32) for _ in range(2)]
        B_ST = bufs(2)
        MS = arena.alloc((4, 512), BF16)
        B_MS = Buf()
        ZS = arena.alloc((4, 512), BF16)
        B_ZS = Buf()
        GS = arena.alloc((4, 512), BF16)
        B_GS = Buf()
        WSTl = arena.alloc((4, 128), BF16)
        BSBl = arena.alloc((4, 128), F32)
        wsl = arena.alloc((4, 128), F32)
        B_wsl = Buf()
        B_ws = Buf()
        BK_MAIN = [0, 1, 2, 3]
        BK_SS, BK_QSS, BK_ROT, BK_TR = 4, 5, 6, 7
        P.dma("sp", wsl, I["w_s"][l].rearrange("g p q -> p g q"), writes=[B_wsl])
        for g in range(4):
            P.tr(ps[:, BK_TR, g * 128:(g + 1) * 128], wsl[:, g, :], idn_f, reads=[B_wsl, B_const], writes=[PSB[BK_TR]])
        P.copy("dve", WSTl.rearrange("p g q -> p (g q)"), ps[:, BK_TR, :], reads=[PSB[BK_TR]], writes=[B_ws])
        P.dma("sp", BSBl.rearrange("p g q -> p (g q)"),
              I["b_s"][l].rearrange("g q -> (g q)").partition_broadcast(128), writes=[B_ws])
        mainctr = [0]
        qctr = [0]
        gq = SMA[l][:, 116:117]
        gk = SMA[l][:, 117:118]

        def mbank():
            b = BK_MAIN[mainctr[0] % 4]
            mainctr[0] += 1
            return b

        pending = []
        for gi, (t0, W) in enumerate(GROUPS):
            w = 1 if gi == 0 else 0
            lat = gi > 0
            need_out = lat or not last
            ntt = W // 128
            if l == 0:
                src = I["x"] if lat else I["ctx"]
                r0 = t0 - NCTX if lat else 0
                for tt in range(ntt):
                    xl = XL[tt % 2]
                    P.dma("sp", xl, src[r0 + tt * 128:r0 + (tt + 1) * 128, :], writes=[B_XL[tt % 2]])
                    for k4 in range(4):
                        for k in range(4):
                            kc = k4 * 4 + k
                            P.tr(ps[:, BK_TR, k * 128:(k + 1) * 128], xl[:, kc * 128:(kc + 1) * 128], idn_f,
                                 reads=[B_XL[tt % 2], B_const], writes=[PSB[BK_TR]])
                        P.copy("dve" if k4 % 2 else "act", XG[:, k4 * 4:(k4 + 1) * 4, tt * 128:(tt + 1) * 128],
                               ps[:, BK_TR, :].rearrange("p (k t) -> p k t", k=4), reads=[PSB[BK_TR]],
                               writes=B_XG[k4 * 4:(k4 + 1) * 4])
                P.dma("sp", XT.rearrange("(kc p) t -> p kc t", p=128)[:, :, t0:t0 + W], XG[:, :, :W],
                      reads=B_XG, writes=[B_XT[gi]])
            else:
                P.dma("sp", XG[:, :, :W], XT.rearrange("(kc p) t -> p kc t", p=128)[:, :, t0:t0 + W],
                      reads=[B_XT[gi]], writes=B_XG)
            if lat:
                n0 = t0 - NCTX
                P.dma("sp", COS[:, :W], cosT_d[:, n0:n0 + W], writes=[B_CS])
                P.dma("sp", SIN[:, :W], sinT_d[:, n0:n0 + W], writes=[B_CS])
            slots = {}

            def load_block(j):
                s = ring_next()
                ring_pending.add(s)
                P.dma("sp", RING[s], WBin[l][j], reads=[B_WBin[l][j]], writes=[B_ring[s]])
                slots[j] = s

            blocks = [0, 1, 2, 3, 4, 5, 6] if need_out else [2, 5]
            for j in blocks[:3]:
                load_block(j)
            for f in pending:
                f()
            pending = []
            norm_mod(XG, B_XG, W, SQ, B_SQ, BK_SS, RS, B_RS, TMP, B_TMP, DER[l][:, 0, w, :], DER[l][:, 1, w, :], HT, B_HT, B_modA[l])

            def fm_chunk(j, c):
                s = slots[j]
                b = mbank()
                for kc in range(KC):
                    P.mm(ps[:, b, :W], RING[s][:, kc, c * 128:(c + 1) * 128], HT[:, kc, :W], kc == 0, kc == KC - 1,
                         reads=[B_ring[s], B_HT[kc]], writes=[PSB[b]])
                return b

            def qk_post(b, gain, dst_rows, B_dst):
                i = qctr[0] % 2
                qs = qctr[0] % 3
                qb = (BK_QSS, BK_TR)[qctr[0] % 2]
                qctr[0] += 1
                P.act(QF[i][:, :W], ps[:, b, :W], AF.Identity, reads=[PSB[b], B_sm[l]], writes=[B_QF[i]], scale=gain)
                P.act(QSQ[i][:, :W], ps[:, b, :W], AF.Square, reads=[PSB[b]], writes=[B_QSQ[i]])
                P.mm(ps[:, qb, :W], ones_bf, QSQ[i][:, :W], True, True, reads=[B_QSQ[i], B_const], writes=[PSB[qb]])
                P.act(QRS[i][:, :W], ps[:, qb, :W], AF.Ln, reads=[PSB[qb]], writes=[B_QRS[i]], scale=1.0 / 128, bias=EPSB[:, 0:1])
                P.act(ps[:, qb, :W], QRS[i][:, :W], AF.Exp, reads=[B_QRS[i]], writes=[PSB[qb]], scale=-0.5)
                if lat:
                    P.mm(ps[:, BK_ROT, :W], rot_f, QF[i][:, :W], True, True, reads=[B_QF[i], B_const], writes=[PSB[BK_ROT]])
                    P.stt(T1[i][:, :W], ps[:, b, :W], gain, COS[:, :W], ALU.mult, ALU.mult, reads=[PSB[b], B_sm[l], B_CS], writes=[B_T1[i]])
                    P.tt("dve", T2[i][:, :W], ps[:, BK_ROT, :W], SIN[:, :W], ALU.mult, reads=[PSB[BK_ROT], B_CS], writes=[B_T2[i]])
                    P.tt("dve", T1[i][:, :W], T1[i][:, :W], T2[i][:, :W], ALU.add, reads=[B_T1[i], B_T2[i]], writes=[B_T1[i]])
                    P.tt("dve", QS[qs][:, :W], T1[i][:, :W], ps[:, qb, :W], ALU.mult, reads=[B_T1[i], PSB[qb]], writes=[B_QS[qs]])
                else:
                    P.tt("dve", QS[qs][:, :W], QF[i][:, :W], ps[:, qb, :W], ALU.mult, reads=[B_QF[i], PSB[qb]], writes=[B_QS[qs]])
                P.dma("sp", dst_rows[:, t0:t0 + W], QS[qs][:, :W], reads=[B_QS[qs]], writes=[B_dst])

            prevq = []
            for bi, j in enumerate(blocks):
                if j in (0, 1):
                    for c in range(4):
                        b = fm_chunk(j, c)
                        h = j * 4 + c
                        if prevq:
                            qk_post(*prevq.pop())
                        prevq.append((b, gq, QT[h * 128:(h + 1) * 128, :], B_QT[h][gi]))
                elif j == 2:
                    for c in range(2):
                        b = fm_chunk(j, c)
                        if prevq:
                            qk_post(*prevq.pop())
                        prevq.append((b, gk, KT[c * 128:(c + 1) * 128, :], B_KT[c][gi]))
                    s = slots[j]
                    for tt in range(ntt):
                        b = mbank()
                        for kc in range(KC):
                            P.mm(ps[:, b, 0:256], HT[:, kc, tt * 128:(tt + 1) * 128], RING[s][:, kc, 256:512], kc == 0, kc == KC - 1,
                                 reads=[B_ring[s], B_HT[kc]], writes=[PSB[b]])
                        if prevq:
                            qk_post(*prevq.pop())
                        P.copy("act", VS[:, tt, :], ps[:, b, 0:256], reads=[PSB[b]], writes=[B_VS])
                    pending.append(lambda t0=t0, W=W, gi=gi, ntt=ntt: P.dma("sp", VTOK[t0:t0 + W, :].rearrange("(tt p) n -> p tt n", p=128), VS[:, 0:ntt, :],
                                                 reads=[B_VS], writes=[B_V[gi]]))
                elif j == 3:
                    for c in range(4):
                        b = fm_chunk(j, c)
                        P.act(UT[:, c, :W], ps[:, b, :W], AF.Gelu, reads=[PSB[b]], writes=[B_UT])
                elif j == 4:
                    s = slots[j]
                    for tt in range(ntt):
                        b = mbank()
                        for kc in range(KC):
                            P.mm(ps[:, b, :], HT[:, kc, tt * 128:(tt + 1) * 128], RING[s][:, kc, :], kc == 0, kc == KC - 1,
                                 reads=[B_ring[s], B_HT[kc]], writes=[PSB[b]])
                        P.act(VV[tt], ps[:, b, :], AF.Gelu, reads=[PSB[b]], writes=[B_VV[tt]])
                    for tt in range(ntt):
                        b = mbank()
                        for g in range(4):
                            P.mm(ps[:, b, g * 128:(g + 1) * 128], VV[tt][:, g * 128:(g + 1) * 128], WSTl[:, g, :], True, True,
                                 reads=[B_VV[tt], B_ws], writes=[PSB[b]])
                        P.tt("dve", ST[tt % 2], ps[:, b, :].rearrange("p (g q) -> p g q", g=4), BSBl, ALU.add,
                             reads=[PSB[b], B_ws], writes=[B_ST[tt % 2]])
                        P.tt("dve", MS[:, :, tt * 128:(tt + 1) * 128], ST[tt % 2], UT[:, :, tt * 128:(tt + 1) * 128], ALU.mult,
                             reads=[B_ST[tt % 2], B_UT], writes=[B_MS])
                    pending.append(lambda t0=t0, W=W, gi=gi: P.dma("sp", MIXT[1024:1536, :].rearrange("(g p) t -> p g t", p=128)[:, :, t0:t0 + W],
                                                 MS[:, :, :W], reads=[B_MS], writes=[B_MIX[8 + g][gi] for g in range(4)]))
                elif j == 5:
                    for c in range(4):
                        b = fm_chunk(j, c)
                        P.copy("act", ZS[:, c, :W], ps[:, b, :W], reads=[PSB[b]], writes=[B_ZS])
                    pending.append(lambda t0=t0, W=W, gi=gi: P.dma("sp", ZXT.rearrange("(g p) t -> p g t", p=128)[:, :, t0:t0 + W], ZS[:, :, :W],
                                                 reads=[B_ZS], writes=[B_ZX[g][gi] for g in range(4)]))
                elif j == 6:
                    for c in range(4):
                        b = fm_chunk(j, c)
                        P.act(GS[:, c, :W], ps[:, b, :W], AF.Gelu, reads=[PSB[b]], writes=[B_GS])
                    pending.append(lambda t0=t0, W=W, gi=gi: P.dma("sp", GZT.rearrange("(g p) t -> p g t", p=128)[:, :, t0:t0 + W], GS[:, :, :W],
                                                 reads=[B_GS], writes=[B_GZ[g][gi] for g in range(4)]))
                ring_pending.discard(slots[j])
                if bi + 3 < len(blocks):
                    load_block(blocks[bi + 3])
                run_bg(1)
        for f in pending:
            f()

    def phase_B(l):
        arena.reset()
        last = l == NL - 1
        ZXb = arena.alloc((T,), BF16)
        GZb = arena.alloc((T,), BF16)
        XRb = arena.alloc((T,), BF16)
        LS = arena.alloc((T,), BF16)
        XR = arena.alloc((T,), F32)
        Rg = arena.alloc((T,), F32)
        Ig = arena.alloc((T,), F32)
        Aa = arena.alloc((T,), F32)
        Mm = arena.alloc((T,), F32)
        HF = arena.alloc((T,), F32)
        HB = arena.alloc((T,), F32)
        B_ZXb, B_GZb, B_XRb, B_LS, B_XR, B_R, B_I, B_A, B_M, B_HF, B_HB = bufs(11, "lru")
        QTh = [arena.alloc((T,), BF16) for _ in range(2)]
        B_QTh = bufs(2)
        KTk = arena.alloc((T,), BF16)
        B_KTk = Buf()
        Vk = arena.alloc((18, 128), BF16)
        B_Vk = Buf()
        PT = [arena.alloc((2, 512), BF16) for _ in range(3)]
        B_PT = bufs(3)
        RSUM = arena.alloc((512,), F32)
        B_RSUM = Buf()
        OS = [arena.alloc((512,), BF16) for _ in range(2)]
        B_OS = bufs(2)
        LWl = arena.alloc((16, 128), BF16)
        lwl = arena.alloc((16, 128), F32)
        B_lwl = Buf()
        B_lw = Buf()
        P.dma("sp", lwl, I["lru_w"][l].rearrange("d g h i j -> i (d g h) j"), writes=[B_lwl])
        P.copy("dve", LWl, lwl, reads=[B_lwl], writes=[B_lw])
        OBK = [4, 5]
        UBK = [6, 7]
        GBK = [0, 1, 2, 3]
        sw = SMA[l]
        cw = sw[:, 96:112]
        cb = sw[:, 112:116]
        lb = SMB[l][:, 64:80]
        ctrs = {"s": 0, "o": 0, "g": 0, "p": 0, "os": 0, "q": 0}
        allg = list(range(NG))

        def lru_unit(h, d):
            if d == 0:
                P.dma("sp", ZXb, ZXT[h * 128:(h + 1) * 128, :], reads=B_ZX[h], writes=[B_ZXb])
                P.dma("sp", GZb, GZT[h * 128:(h + 1) * 128, :], reads=B_GZ[h], writes=[B_GZb])
                for (s0, L) in ((0, NCTX), (NCTX, NLAT)):
                    P.act(XR[:, s0:s0 + L], ZXb[:, s0:s0 + L], AF.Identity, reads=[B_ZXb, B_sm[l]], writes=[B_XR],
                          scale=cw[:, 2 * 4 + h:2 * 4 + h + 1], bias=cb[:, h:h + 1])
                    for tap in (0, 1, 3):
                        o = tap - 2
                        lo = max(0, -o)
                        hi = L - max(0, o)
                        P.stt(XR[:, s0 + lo:s0 + hi], ZXb[:, s0 + lo + o:s0 + hi + o], cw[:, tap * 4 + h:tap * 4 + h + 1],
                              XR[:, s0 + lo:s0 + hi], ALU.mult, ALU.add, reads=[B_ZXb, B_XR, B_sm[l]], writes=[B_XR])
                P.copy("act", XRb, XR, reads=[B_XR], writes=[B_XRb])
            for (t0, W) in ((0, 512), (512, 512), (1024, 512), (1536, 512), (2048, 256)):
                for gate, dst, B_dst in ((0, Rg, B_R), (1, Ig, B_I)):
                    b = GBK[ctrs["g"] % 4]
                    ctrs["g"] += 1
                    P.mm(ps[:, b, :W], LWl[:, (d * 2 + gate) * 4 + h, :], XRb[:, t0:t0 + W], True, True,
                         reads=[B_lw, B_XRb], writes=[PSB[b]])
                    P.act(dst[:, t0:t0 + W], ps[:, b, :W], AF.Sigmoid, reads=[PSB[b]], writes=[B_dst],
                          bias=lb[:, (d * 2 + gate) * 4 + h:(d * 2 + gate) * 4 + h + 1])
            P.act(Aa, Rg, AF.Exp, reads=[B_R], writes=[B_A], scale=CL[l][:, 0, d * 4 + h:d * 4 + h + 1])
            P.act(Mm, Rg, AF.Exp, reads=[B_R], writes=[B_M], scale=CL[l][:, 1, d * 4 + h:d * 4 + h + 1])
            P.act(Mm, Mm, AF.Sqrt, reads=[B_M], writes=[B_M], scale=-1.0, bias=ONEB[:, 0:1])
            P.tt("dve", Ig, Ig, XR, ALU.mult, reads=[B_I, B_XR], writes=[B_I])
            P.tt("dve", Ig, Ig, Mm, ALU.mult, reads=[B_I, B_M], writes=[B_I])
            if d == 0:
                P.scan(HF, Aa, Ig, 0.0, reads=[B_A, B_I], writes=[B_HF])
            else:
                P.scan(HB[:, 0:NCTX][:, ::-1], Aa[:, 0:NCTX][:, ::-1], Ig[:, 0:NCTX][:, ::-1], 0.0, reads=[B_A, B_I], writes=[B_HB])
                P.scan(HB[:, NCTX:T][:, ::-1], Aa[:, NCTX:T][:, ::-1], Ig[:, NCTX:T][:, ::-1], HB[:, 0:1],
                       reads=[B_A, B_I, B_HB], writes=[B_HB])
                P.tt("dve", HF, HF, HB, ALU.add, reads=[B_HF, B_HB], writes=[B_HF])
                P.tt("dve", LS, HF, GZb, ALU.mult, reads=[B_HF, B_GZb], writes=[B_LS])
                P.dma("sp", MIXT[1536 + h * 128:1536 + (h + 1) * 128, :], LS, reads=[B_LS], writes=B_MIX[12 + h])

        def attn_block(c, kv, qi, q0, W, kcs):
            pairs = [kcs[i:i + 2] for i in range(0, len(kcs), 2)]

            def s_pair(pr):
                pb = ctrs["s"] % 2
                ctrs["s"] += 1
                for i, kc in enumerate(pr):
                    P.mm(ps[:, 2 * pb + i, :W], KTk[:, kc * 128:(kc + 1) * 128], QTh[qi][:, q0:q0 + W], True, True,
                         reads=[B_KTk, B_QTh[qi]], writes=[PSB[2 * pb + i]])
                return pb

            ob = OBK[ctrs["o"] % 2]
            ub = UBK[ctrs["o"] % 2]
            ctrs["o"] += 1
            nb = s_pair(pairs[0])
            nk = len(kcs)
            done = 0
            for ip, pr in enumerate(pairs):
                pb = nb
                pi = ctrs["p"] % 3
                ctrs["p"] += 1
                n2 = len(pr)
                P.act(PT[pi][:, 0:n2, :W], ps[:, 2 * pb:2 * pb + n2, :W], AF.Exp,
                      reads=[PSB[2 * pb + i] for i in range(n2)] + [B_sm[l]], writes=[B_PT[pi]],
                      scale=QSCALE, bias=NEGB[l][:, 0:1])
                if ip + 1 < len(pairs):
                    nb = s_pair(pairs[ip + 1])
                for i, kc in enumerate(pr):
                    P.mm(ps[:, ob, :W], Vk[:, kc, :], PT[pi][:, i, :W], done == 0, done == nk - 1, reads=[B_Vk, B_PT[pi]], writes=[PSB[ob]])
                    P.mm(ps[:, ub, :W], ones_bf, PT[pi][:, i, :W], done == 0, done == nk - 1, reads=[B_const, B_PT[pi]], writes=[PSB[ub]])
                    done += 1
            P.recip(RSUM[:, :W], ps[:, ub, :W], reads=[PSB[ub]], writes=[B_RSUM])
            oi = ctrs["os"] % 2
            ctrs["os"] += 1
            P.tt("dve", OS[oi][:, :W], ps[:, ob, :W], RSUM[:, :W], ALU.mult, reads=[PSB[ob], B_RSUM], writes=[B_OS[oi]])
            gidx = [g for g, (t0, w_) in enumerate(GROUPS) if t0 == q0][0]
            P.dma("sp", MIXT[c * 128:(c + 1) * 128, q0:q0 + W], OS[oi][:, :W], reads=[B_OS[oi]], writes=[B_MIX[c][gidx]])

        def attn_head(c):
            kv = c // 4
            if c % 4 == 0:
                P.dma("sp", KTk, KT[kv * 128:(kv + 1) * 128, :], reads=B_KT[kv], writes=[B_KTk])
                P.dma("sp", Vk, VTOK[:, kv * 128:(kv + 1) * 128].rearrange("(kc p) d -> p kc d", p=128), reads=B_V, writes=[B_Vk])
            qi = ctrs["q"] % 2
            ctrs["q"] += 1
            if last:
                P.dma("sp", QTh[qi][:, NCTX:T], QT[c * 128:(c + 1) * 128, NCTX:T], reads=B_QT[c][1:], writes=[B_QTh[qi]])
            else:
                P.dma("sp", QTh[qi], QT[c * 128:(c + 1) * 128, :], reads=B_QT[c], writes=[B_QTh[qi]])
            for g in range(1, NG):
                q0, W = GROUPS[g]
                attn_block(c, kv, qi, q0, W, list(range(18)))
                run_bg(1)
            if not last:
                attn_block(c, kv, qi, 0, NCTX, [0, 1])

        units = [(h, d) for h in range(4) for d in range(2)]
        for i in range(8):
            attn_head(i)
            lru_unit(*units[i])

    ONEB = pers.alloc((2,), F32)

    H2T = scratch("H2T", [D, T], BF16)
    B_H2T = bufs(NG)

    def proj_norm_update(l, W, nkc, SRC, B_SRC, wload, gvec, XGt, B_XGt, YT, B_YT, SQ, B_SQ, RS, B_RS, TMP, B_TMP, mbank, BK_SS, allow_bg=True, mid=None, after_first_load=None, defer=None, after_block=None, rs_bank=None):
        sqs = []
        for j in range(4):
            parts = wload(j)
            if j == 0 and after_first_load is not None:
                after_first_load()
            for c in range(4):
                n = j * 4 + c
                b = mbank()
                for (s, k_lo, k_hi) in parts:
                    for kc in range(k_lo, k_hi):
                        P.mm(ps[:, b, :W], RING[s][:, kc - k_lo, c * 128:(c + 1) * 128], SRC[:, kc, :W], kc == 0, kc == nkc - 1,
                             reads=[B_ring[s], B_SRC[kc]], writes=[PSB[b]])
                P.copy("act", YT[:, n, :W], ps[:, b, :W], reads=[PSB[b]], writes=[B_YT[n]])
                sq = SQ[n % 2]
                P.act(sq[:, :W], ps[:, b, :W], AF.Square, reads=[PSB[b]], writes=[B_SQ[n % 2]])
                if c == 3 and allow_bg:
                    run_bg(1)
                if c == 3 and after_block is not None:
                    after_block(j)
                sqs.append(n)
                if len(sqs) > 1:
                    m = sqs[-2]
                    P.mm(ps[:, BK_SS, :W], ones_bf, SQ[m % 2][:, :W], m == 0, False, reads=[B_SQ[m % 2], B_const], writes=[PSB[BK_SS]])
        m = sqs[-1]
        P.mm(ps[:, BK_SS, :W], ones_bf, SQ[m % 2][:, :W], False, True, reads=[B_SQ[m % 2], B_const], writes=[PSB[BK_SS]])
        if mid is not None:
            mid()
        steps = []
        rsb = BK_SS if rs_bank is None else rs_bank

        def s0():
            P.act(RS[:, :W], ps[:, BK_SS, :W], AF.Ln, reads=[PSB[BK_SS]], writes=[B_RS], scale=1.0 / D, bias=EPSB[:, 0:1])
            P.act(ps[:, rsb, :W], RS[:, :W], AF.Exp, reads=[B_RS], writes=[PSB[rsb]], scale=-0.5)
        steps.append(s0)
        for kc in range(KC):
            def sk(kc=kc):
                P.tt("dve", YT[:, kc, :W], YT[:, kc, :W], ps[:, rsb, :W], ALU.mult, reads=[B_YT[kc], PSB[rsb]], writes=[B_YT[kc]])
                P.stt(XGt[:, kc, :W], YT[:, kc, :W], gvec[:, kc:kc + 1], XGt[:, kc, :W], ALU.mult, ALU.add,
                      reads=[B_YT[kc], B_XGt[kc], B_modB[l]], writes=[B_XGt[kc]])
            steps.append(sk)
        if defer is None:
            for f_ in steps:
                f_()
        else:
            defer.extend(steps)

    def phase_C1(l):
        arena.reset()
        last = l == NL - 1
        XG1 = arena.alloc((KC, 512), F32)
        XG = [XG1, XG1]
        B_XG1 = bufs(KC, "xg")
        B_XG = [B_XG1, B_XG1]
        YT = arena.alloc((KC, 512), F32)
        B_YT = bufs(KC, "yt")
        HT = [arena.alloc((KC, 512), BF16) for _ in range(2)]
        B_HT = [bufs(KC, "ht") for _ in range(2)]
        SQ = [arena.alloc((512,), BF16) for _ in range(2)]
        B_SQ = bufs(2)
        RS = [arena.alloc((512,), F32) for _ in range(2)]
        B_RS = bufs(2)
        TMP = [arena.alloc((512,), F32) for _ in range(2)]
        B_TMP = bufs(2)
        mainctr = [0]

        def mbank():
            b = mainctr[0] % 4
            mainctr[0] += 1
            return b

        XTv = XT.rearrange("(kc p) t -> p kc t", p=128)
        MXv = MIXT.rearrange("(kc p) t -> p kc t", p=128)
        glist = [gi for gi in range(NG) if not (last and gi == 0)]

        def ht_load(k):
            gi = glist[k]
            t0, W = GROUPS[gi]
            P.dma("sp", HT[k % 2][:, :, :W], MXv[:, :, t0:t0 + W], reads=[B_MIX[c_][gi] for c_ in range(16)], writes=B_HT[k % 2])

        def xg_load(k):
            gi = glist[k]
            t0, W = GROUPS[gi]
            for q4 in range(4):
                P.dma("sp", XG1[:, q4 * 4:(q4 + 1) * 4, :W], XTv[:, q4 * 4:(q4 + 1) * 4, t0:t0 + W], reads=[B_XT[gi]],
                      writes=B_XG1[q4 * 4:(q4 + 1) * 4])

        def w_loads():
            parts = []
            for j in range(4):
                s = ring_next()
                P.dma("sp", RING[s], WBout[l][j], reads=[B_WBout[l][j]], writes=[B_ring[s]])
                parts.append([(s, 0, KC)])
            return parts

        ht_load(0)
        xg_load(0)
        parts = w_loads()
        for k, gi in enumerate(glist):
            t0, W = GROUPS[gi]
            w = 1 if gi == 0 else 0
            i = k % 2
            nxt = {}

            def ablk(j, k=k):
                if k + 1 < len(glist):
                    if j == 0:
                        ht_load(k + 1)
                        nxt["parts"] = []
                    s_ = ring_next()
                    P.dma("sp", RING[s_], WBout[l][j], reads=[B_WBout[l][j]], writes=[B_ring[s_]])
                    nxt["parts"].append([(s_, 0, KC)])

            proj_norm_update(l, W, KC, HT[i], B_HT[i], (lambda j, parts=parts: parts[j]), DER[l][:, 2, w, :], XG1, B_XG1, YT, B_YT,
                             SQ, B_SQ, RS[0], B_RS[0], TMP, B_TMP, mbank, 4, allow_bg=False, after_block=ablk, rs_bank=6)
            P.dma("sp", XTv[:, :, t0:t0 + W], XG1[:, :, :W], reads=B_XG1, writes=[B_XT[gi]])
            norm_mod(XG1, B_XG1, W, SQ, B_SQ, 5, RS[1], B_RS[1], TMP, B_TMP, DER[l][:, 3, w, :], DER[l][:, 4, w, :],
                     HT[i], B_HT[i], B_modB[l])
            P.dma("sp", H2T.rearrange("(kc p) t -> p kc t", p=128)[:, :, t0:t0 + W], HT[i][:, :, :W], reads=B_HT[i], writes=[B_H2T[gi]])
            if k + 1 < len(glist):
                xg_load(k + 1)
                parts = nxt["parts"]

    def phase_C2(l):
        arena.reset()
        last = l == NL - 1
        XG = arena.alloc((KC, 512), F32)
        B_XG = bufs(KC, "xg")
        YT = arena.alloc((KC, 512), F32)
        B_YT = bufs(KC, "yt")
        HT = arena.alloc((KC, 512), BF16)
        B_HT = bufs(KC, "ht")
        AT = arena.alloc((FC, 512), BF16)
        B_AT = bufs(FC, "at")
        SQ = [arena.alloc((512,), BF16) for _ in range(2)]
        B_SQ = bufs(2)
        RS = arena.alloc((512,), F32)
        B_RS = Buf()
        TMP = [arena.alloc((512,), F32) for _ in range(2)]
        B_TMP = bufs(2)
        SG = TMP
        B_SG = B_TMP
        BK_TR = [6, 7]
        mainctr = [0]

        def mbank():
            b = mainctr[0] % 4
            mainctr[0] += 1
            return b

        XTv = XT.rearrange("(kc p) t -> p kc t", p=128)
        glist2 = [gi for gi in range(NG) if not (last and gi == 0)]

        def ht2_load(gi):
            t0_, W_ = GROUPS[gi]
            P.dma("sp", HT[:, :, :W_], H2T.rearrange("(kc p) t -> p kc t", p=128)[:, :, t0_:t0_ + W_], reads=[B_H2T[gi]], writes=B_HT)

        deferred = []
        for gi, (t0, W) in enumerate(GROUPS):
            if last and gi == 0:
                continue
            w = 1 if gi == 0 else 0
            if gi == glist2[0]:
                ht2_load(gi)
            for j in range(11):
                sg_ = ring_next()
                P.dma("sp", RING[sg_], WBf1[l][j], reads=[B_WBf1[l][j]], writes=[B_ring[sg_]])
                su_ = ring_next()
                P.dma("sp", RING[su_], WBf1[l][11 + j], reads=[B_WBf1[l][11 + j]], writes=[B_ring[su_]])
                if j == 6:
                    P.dma("sp", XG[:, :, :W], XTv[:, :, t0:t0 + W], reads=[B_XT[gi]], writes=B_XG)
                for c in range(4):
                    fc = j * 4 + c
                    bg_ = mbank()
                    for kc in range(KC):
                        P.mm(ps[:, bg_, :W], RING[sg_][:, kc, c * 128:(c + 1) * 128], HT[:, kc, :W], kc == 0, kc == KC - 1,
                             reads=[B_ring[sg_], B_HT[kc]], writes=[PSB[bg_]])
                    bu = mbank()
                    for kc in range(KC):
                        P.mm(ps[:, bu, :W], RING[su_][:, kc, c * 128:(c + 1) * 128], HT[:, kc, :W], kc == 0, kc == KC - 1,
                             reads=[B_ring[su_], B_HT[kc]], writes=[PSB[bu]])
                    sg = SG[fc % 2]
                    P.act(sg[:, :W], ps[:, bg_, :W], AF.Silu, reads=[PSB[bg_]], writes=[B_SG[fc % 2]])
                    P.tt("dve", AT[:, fc, :W], sg[:, :W], ps[:, bu, :W], ALU.mult, reads=[B_SG[fc % 2], PSB[bu]], writes=[B_AT[fc]])
                    for _ in range(2):
                        if deferred:
                            deferred.pop(0)()
                run_bg(1)
            while deferred:
                deferred.pop(0)()
            nx = [g_ for g_ in glist2 if g_ > gi]

            def afl(nx=nx):
                if nx:
                    ht2_load(nx[0])

            def wl_f2(j):
                parts = []
                for (k_lo, k_hi) in ((0, 16), (16, 32), (32, 44)):
                    s = ring_next()
                    P.dma("sp", RING[s][:, 0:k_hi - k_lo, :], WBf2[l][j][:, k_lo:k_hi, :], reads=[B_WBf2[l][j]], writes=[B_ring[s]])
                    parts.append((s, k_lo, k_hi))
                return parts

            proj_norm_update(l, W, FC, AT, B_AT, wl_f2, DER[l][:, 5, w, :], XG, B_XG, YT, B_YT, SQ, B_SQ, RS, B_RS, TMP, B_TMP, mbank, 4,
                             after_first_load=afl, defer=deferred)

            def fin(gi=gi, t0=t0, W=W):
                if not last:
                    P.dma("sp", XTv[:, :, t0:t0 + W], XG[:, :, :W], reads=B_XG, writes=[B_XT[gi]])
                    return
                for tt in range(W // 128):
                    OT = YT.rearrange("p k t -> p (k t)")[:, (tt % 2) * D:(tt % 2 + 1) * D]
                    B_OT = B_YT[(tt % 2) * 4:(tt % 2) * 4 + 4]
                    for k4 in range(4):
                        b = BK_TR[k4 % 2]
                        for k in range(4):
                            kc = k4 * 4 + k
                            P.tr(ps[:, b, k * 128:(k + 1) * 128], XG[:, kc, tt * 128:(tt + 1) * 128], idn_f,
                                 reads=[B_XG[kc], B_const], writes=[PSB[b]])
                        P.copy("dve" if k4 % 2 else "act", OT[:, k4 * 512:(k4 + 1) * 512], ps[:, b, :], reads=[PSB[b]], writes=B_OT)
                    r0 = t0 - NCTX + tt * 128
                    out_ops.append(P.dma("sp", out_d[r0:r0 + 128, :], OT, reads=B_OT))
            deferred.append(fin)
        while deferred:
            deferred.pop(0)()

    P.memset("dve", EPSB, EPS, writes=[B_const])
    P.memset("dve", ONEB, 1.0, writes=[B_const])
    conv_w_in(0)
    preamble()
    for t_ in mod_tasks(0, range(8)):
        t_()
    der_A(0)

    def interleave(a_, b_):
        out = []
        na, nb = len(a_), len(b_)
        ia = ib = 0
        while ia < na or ib < nb:
            if ib >= nb or (ia < na and ia * nb <= ib * na):
                out.append(a_[ia])
                ia += 1
            else:
                out.append(b_[ib])
                ib += 1
        return out

    a0 = mod_tasks(0, range(8, 24)) + [lambda: der_B(0)]
    a0 = interleave(a0, [(lambda: None)] * (35 - len(a0)))
    bg.extend(a0 + conv_tasks(0, ("out", "f1")))
    seq = [("P", None, 0)]
    for l in range(NL):
        seq += [("A%d" % l, phase_A, l), ("B%d" % l, phase_B, l), ("C%d" % l, phase_C1, l), ("D%d" % l, phase_C2, l)]
    for name, fn, l in seq:
        if fn is not None:
            if fn is phase_C1:
                run_bg(len(bg))
                if l == 0:
                    for t_ in conv_tasks(0, ("f2",)):
                        t_()
                    bg.extend([lambda j=j: conv_one(WBin[1][j], I["w_in"][1][:, 512 * j:512 * (j + 1)], B_WBin[1][j]) for j in range(7)])
                    bg.extend(interleave(mod_tasks(1, range(24)) + [lambda: der_A(1), lambda: der_B(1)],
                                         conv_tasks(1, ("out", "f1", "f2"))))
            fn(l)
            if fn is phase_C2:
                run_bg(len(bg))
            P.barrier()
        if stop == name:
            break
    if debug:
        dt = arena.ap[:, 0:2048]
        P.barrier()
        B_d = Buf()
        P.copy("dve", dt[:, 0:192], MOD[0].rearrange("p w c -> p (w c)"), writes=[B_d])
        P.copy("dve", dt[:, 192:384], MOD[1].rearrange("p w c -> p (w c)"), writes=[B_d])
        P.copy("dve", dt[:, 384:576], DER[0].rearrange("p a w c -> p (a w c)"), writes=[B_d])
        P.copy("dve", dt[:, 576:592], CL[0].rearrange("p a c -> p (a c)"), writes=[B_d])
        P.copy("dve", dt[:, 592:594], NEGB[0], writes=[B_d])
        P.copy("dve", dt[:, 600:728], SMA[0], writes=[B_d])
        P.copy("dve", dt[:, 728:856], SMB[0], writes=[B_d])
        out_ops.append(P.dma("sp", DBG, dt, reads=[B_d]))
    P.emit(final_wait_ops=out_ops)
    return nc, P


def _consts():
    n = np.arange(NLAT)
    row = (n // 64).astype(np.float32)
    col = (n % 64).astype(np.float32)
    inv = (np.float32(10000.0) ** (-np.arange(0, 64, 2, dtype=np.float32) / np.float32(64))).astype(np.float32)
    ar = row[:, None] * inv
    ac = col[:, None] * inv
    ang = np.concatenate([ar, ar, ac, ac], axis=-1).astype(np.float32)
    cosT = np.ascontiguousarray(np.cos(ang).T.astype(np.float32))
    sinT = np.ascontiguousarray(np.sin(ang).T.astype(np.float32))
    rot = np.zeros((128, 128), np.float32)
    for a in range(2):
        for i in range(32):
            rot[a * 64 + 32 + i, a * 64 + i] = -1.0
            rot[a * 64 + i, a * 64 + 32 + i] = 1.0
    return {"k_cos": cosT, "k_sin": sinT, "k_rot": rot, "k_idn": np.eye(128, dtype=np.float32)}


_CACHE = {}


def kernel(**inputs):
    if "nc" not in _CACHE:
        _CACHE["nc"] = build()[0]
    nc = _CACHE["nc"]
    consts = _consts()
    B = inputs["x"].shape[0]
    shared = {k: np.ascontiguousarray(np.asarray(v, dtype=np.float32)) for k, v in inputs.items() if k not in ("x", "c", "ctx")}
    in_maps = []
    for b in range(B):
        m = dict(shared)
        m.update(consts)
        m["x"] = np.ascontiguousarray(np.asarray(inputs["x"][b], dtype=np.float32))
        m["c"] = np.ascontiguousarray(np.asarray(inputs["c"][b], dtype=np.float32))
        m["ctx"] = np.ascontiguousarray(np.asarray(inputs["ctx"][b], dtype=np.float32))
        in_maps.append(m)
    res = run_bass_kernel_spmd(nc, in_maps, core_ids=list(range(B)))
    return np.stack([np.asarray(r["out"], dtype=np.float32) for r in res.results], axis=0)
```

```python
import numpy as np
import ml_dtypes
from contextlib import ExitStack
import concourse.bass as bass
import concourse.mybir as mybir
from concourse.bass_utils import run_bass_kernel_spmd

F32 = mybir.dt.float32
BF16 = mybir.dt.bfloat16
AF = mybir.ActivationFunctionType
ALU = mybir.AluOpType

ENGS = ("pe", "act", "dve", "pool", "sp")
NDSEM = {"sp": 12, "act": 2, "pool": 4}

D = 2048
NLAT = 2048
NCTX = 256
T = NCTX + NLAT
DFF = 5632
NL = 2
KC = D // 128
FC = DFF // 128
GROUPS = [(0, 256), (256, 512), (768, 512), (1280, 512), (1792, 512)]
EPS = 1e-6
QSCALE = 128.0 ** -0.5


class Buf:
    __slots__ = ("w", "r", "rd", "name", "excl")

    def __init__(self, name="", excl=False):
        self.w = None
        self.r = {}
        self.rd = []
        self.name = name
        self.excl = excl


class Op:
    __slots__ = ("eng", "fn", "deps", "sig", "need_sig", "is_dma", "prev_wait")


class Prog:
    def __init__(self, nc):
        self.nc = nc
        self.ops = {e: [] for e in ENGS}
        self.nops = 0
        self.last_dmas = {e: [] for e in ENGS}

    def op(self, eng, fn, reads=(), writes=(), dma=False, extra_deps=()):
        o = Op()
        o.eng = eng
        o.fn = fn
        o.is_dma = dma
        o.need_sig = dma
        o.sig = None
        o.prev_wait = None
        deps = set(extra_deps)
        for b in reads:
            if b.w is not None:
                deps.add(b.w)
            if b.excl:
                for e2, r in b.r.items():
                    if e2 != eng:
                        deps.add(r)
        for b in writes:
            if b.w is not None:
                deps.add(b.w)
            for r in b.r.values():
                deps.add(r)
            for r in b.rd:
                deps.add(r)
        if eng == "pe" and not dma:
            deps = {d for d in deps if d.is_dma or d.eng != "pe"}
        for d in deps:
            d.need_sig = True
        o.deps = deps
        for b in reads:
            if dma:
                b.rd.append(o)
            else:
                b.r[eng] = o
        for b in writes:
            b.w = o
            b.r = {}
            b.rd = []
        self.ops[eng].append(o)
        if dma:
            ld = self.last_dmas[eng]
            ld.append(o)
            if len(ld) > NDSEM[eng]:
                ld.pop(0)
        self.nops += 1
        return o

    def barrier(self):
        deps = []
        for e in ENGS:
            for o in reversed(self.ops[e]):
                if not o.is_dma and o.fn is not None:
                    deps.append(o)
                    break
            if e != "pool":
                deps.extend(self.last_dmas[e])
        for e in ENGS:
            self.op(e, None, extra_deps=deps)

    def dma(self, q, out, in_, reads=(), writes=()):
        return self.op(q, lambda e: e.dma_start(out=out, in_=in_), reads, writes, dma=True)

    def mm(self, out, lhsT, rhs, start, stop, reads=(), writes=()):
        return self.op("pe", lambda e: e.matmul(out, lhsT, rhs, start=start, stop=stop), reads, writes)

    def tr(self, out, in_, ident, reads=(), writes=()):
        return self.op("pe", lambda e: e.transpose(out=out, in_=in_, identity=ident), reads, writes)

    def act(self, out, in_, func, reads=(), writes=(), scale=None, bias=None):
        kw = {}
        if scale is not None:
            kw["scale"] = scale
        if bias is not None:
            kw["bias"] = bias
        return self.op("act", lambda e: e.activation(out=out, in_=in_, func=func, **kw), reads, writes)

    def tt(self, eng, out, in0, in1, op, reads=(), writes=()):
        return self.op(eng, lambda e: e.tensor_tensor(out=out, in0=in0, in1=in1, op=op), reads, writes)

    def stt(self, out, in0, scalar, in1, op0, op1, reads=(), writes=()):
        return self.op("dve", lambda e: e.scalar_tensor_tensor(out=out, in0=in0, scalar=scalar, in1=in1, op0=op0, op1=op1),
                       reads, writes)

    def ts(self, eng, out, in0, s1, op0, s2=None, op1=None, reads=(), writes=()):
        if op1 is None:
            return self.op(eng, lambda e: e.tensor_scalar(out=out, in0=in0, scalar1=s1, scalar2=None, op0=op0), reads, writes)
        return self.op(eng, lambda e: e.tensor_scalar(out=out, in0=in0, scalar1=s1, scalar2=s2, op0=op0, op1=op1), reads, writes)

    def copy(self, eng, out, in_, reads=(), writes=()):
        if eng == "act":
            return self.op("act", lambda e: e.activation(out=out, in_=in_, func=AF.Identity), reads, writes)
        return self.op(eng, lambda e: e.tensor_copy(out=out, in_=in_), reads, writes)

    def recip(self, out, in_, reads=(), writes=()):
        return self.op("dve", lambda e: e.reciprocal(out=out, in_=in_), reads, writes)

    def scan(self, out, d0, d1, initial, reads=(), writes=()):
        return self.op("dve", lambda e: e.tensor_tensor_scan(out=out, data0=d0, data1=d1, initial=initial,
                                                              op0=ALU.mult, op1=ALU.add), reads, writes)

    def memset(self, eng, ap, val, writes=()):
        return self.op(eng, lambda e: e.memset(ap, val), (), writes)

    def emit(self, final_wait_ops=()):
        nc = self.nc
        with ExitStack() as st:
            sems = {e: st.enter_context(nc.semaphore("s_" + e)) for e in ENGS}
            dsems = {e: [st.enter_context(nc.semaphore("d_%s%d" % (e, i))) for i in range(NDSEM[e])]
                     for e in NDSEM}
            semobj = {}
            for e in ENGS:
                cnt = 0
                nd = NDSEM.get(e, 1)
                duse = [0] * nd
                di = 0
                for o in self.ops[e]:
                    if o.is_dma:
                        s = di % nd
                        di += 1
                        key = ("d", e, s)
                        semobj[key] = dsems[e][s]
                        if duse[s]:
                            o.prev_wait = (key, 16 * duse[s])
                        duse[s] += 1
                        o.sig = (key, 16 * duse[s])
                    elif o.need_sig:
                        assert o.fn is not None
                        cnt += 1
                        key = ("c", e)
                        semobj[key] = sems[e]
                        o.sig = (key, cnt)
            finals = [o.sig for o in final_wait_ops]
            block = st.enter_context(nc.Block())

            def run(e, eng):
                waited = {}
                for o in self.ops[e]:
                    waits = {}
                    for d in o.deps:
                        k, v = d.sig
                        if waited.get(k, 0) < v and waits.get(k, 0) < v:
                            waits[k] = v
                    if o.prev_wait is not None:
                        k, v = o.prev_wait
                        if waited.get(k, 0) < v and waits.get(k, 0) < v:
                            waits[k] = v
                    for k, v in waits.items():
                        eng.wait_ge(semobj[k], v)
                        waited[k] = v
                    if o.fn is None:
                        continue
                    ins = o.fn(eng)
                    if o.sig is not None:
                        ins.then_inc(semobj[o.sig[0]], 16 if o.is_dma else 1)
                if e == "sp":
                    fw = {}
                    for k, v in finals:
                        fw[k] = max(fw.get(k, 0), v)
                    for k, v in fw.items():
                        if waited.get(k, 0) < v:
                            eng.wait_ge(semobj[k], v)

            @block.tensor
            def _(eng):
                run("pe", eng)

            @block.scalar
            def _(eng):
                run("act", eng)

            @block.vector
            def _(eng):
                run("dve", eng)

            @block.gpsimd
            def _(eng):
                run("pool", eng)

            @block.sync
            def _(eng):
                run("sp", eng)


class Arena:
    def __init__(self, nc, name, nbytes):
        self.ap = nc.alloc_sbuf_tensor(name, [128, nbytes // 4], F32).ap()
        self.n = nbytes
        self.off = 0

    def reset(self):
        self.off = 0

    def alloc(self, free, dtype):
        esz = 4 if dtype == F32 else 2
        ne = 1
        for s in free:
            ne *= s
        nb = ne * esz
        assert nb % 4 == 0
        nba = (nb + 63) // 64 * 64
        assert self.off + nba <= self.n, ("arena overflow", self.off, nba, self.n)
        v = self.ap[:, self.off // 4:(self.off + nb) // 4]
        if dtype != F32:
            v = v.bitcast(dtype)
        self.off += nba
        if len(free) == 2:
            v = v.rearrange("p (a b) -> p a b", a=free[0])
        elif len(free) == 3:
            v = v.rearrange("p (a b c) -> p a b c", a=free[0], b=free[1])
        elif len(free) == 4:
            v = v.rearrange("p (a b c d) -> p a b c d", a=free[0], b=free[1], c=free[2])
        return v


def bufs(n, name="", excl=False):
    return [Buf(name + str(i), excl) for i in range(n)]


INPUT_SPECS = [
    ("x", [NLAT, D]), ("c", [D]), ("ctx", [NCTX, D]), ("c_ctx", [D]),
    ("w_mod", [NL, D, 6 * D]), ("b_mod", [NL, 6 * D]), ("g_norm", [NL, 4, D]), ("w_in", [NL, D, 3584]),
    ("g_qk", [NL, 2, 128]), ("w_s", [NL, 4, 128, 128]), ("b_s", [NL, 4, 128]), ("conv_w", [NL, 4, 512]),
    ("conv_b", [NL, 512]), ("lru_w", [NL, 2, 2, 4, 128, 128]), ("lru_b", [NL, 2, 2, 512]),
    ("lru_lam", [NL, 2, 512]), ("w_out", [NL, D, D]), ("w_ffn_in", [NL, D, 2 * DFF]), ("w_ffn_out", [NL, DFF, D]),
]


def build(debug=False, stop=None):
    nc = bass.Bass("TRN2", target_bir_lowering=False)
    I = {}
    for name, shape in INPUT_SPECS:
        I[name] = nc.dram_tensor(name, shape, F32, kind="ExternalInput").ap()
    cosT_d = nc.dram_tensor("k_cos", [128, NLAT], F32, kind="ExternalInput").ap()
    sinT_d = nc.dram_tensor("k_sin", [128, NLAT], F32, kind="ExternalInput").ap()
    rot_d = nc.dram_tensor("k_rot", [128, 128], F32, kind="ExternalInput").ap()
    idn_d = nc.dram_tensor("k_idn", [128, 128], F32, kind="ExternalInput").ap()
    out_d = nc.dram_tensor("out", [NLAT, D], F32, kind="ExternalOutput").ap()

    skind = "ExternalOutput" if debug else "Internal"

    def scratch(name, shape, dt, dbg=True):
        return nc.dram_tensor(name, shape, dt, kind=(skind if dbg else "Internal")).ap()

    WBin = [scratch("WBin%d" % l, [7, 128, KC, 512], BF16, False) for l in range(NL)]
    WBout = [scratch("WBout%d" % l, [4, 128, KC, 512], BF16, False) for l in range(NL)]
    WBf1 = [scratch("WBf1%d" % l, [22, 128, KC, 512], BF16, False) for l in range(NL)]
    WBf2 = [scratch("WBf2%d" % l, [4, 128, FC, 512], BF16, False) for l in range(NL)]
    XT = scratch("XT", [D, T], F32)
    QT = scratch("QT", [1024, T], BF16)
    KT = scratch("KT", [256, T], BF16)
    VTOK = scratch("VTOK", [T, 256], BF16)
    MIXT = scratch("MIXT", [D, T], BF16)
    ZXT = scratch("ZXT", [512, T], BF16)
    GZT = scratch("GZT", [512, T], BF16)
    if debug:
        DBG = scratch("DBG", [128, 2048], F32)

    P = Prog(nc)

    pers = Arena(nc, "pers", 10 * 1024 + 4 * 16384)
    ones_bf = pers.alloc((128,), BF16)
    ones_f = pers.alloc((128,), F32)
    idn_f = pers.alloc((128,), F32)
    rot_f = pers.alloc((128,), F32)
    CT = pers.alloc((2, 16), F32)
    CTb = pers.alloc((2, 16), BF16)
    SMA = [pers.alloc((128,), F32) for _ in range(NL)]
    SMB = [pers.alloc((128,), F32) for _ in range(NL)]
    MOD = [pers.alloc((2, 96), F32) for _ in range(NL)]
    DER = [pers.alloc((6, 2, 16), F32) for _ in range(NL)]
    CL = [pers.alloc((2, 8), F32) for _ in range(NL)]
    NEGB = [pers.alloc((2,), F32) for _ in range(NL)]
    PSTG = [pers.alloc((128,), F32) for _ in range(2)]
    PTMP = pers.alloc((128,), F32)
    NRING = 4
    RING = [pers.alloc((KC, 512), BF16) for _ in range(NRING)]
    B_const = Buf("const")
    B_sm = [Buf("sm%d" % l) for l in range(NL)]
    B_ring = bufs(NRING, "ring")
    ring_ctr = [0]

    ring_pending = set()

    def ring_next():
        while True:
            s = ring_ctr[0] % NRING
            ring_ctr[0] += 1
            if s not in ring_pending:
                return s

    arena = Arena(nc, "arena", 133 * 1024)
    ps = nc.alloc_psum_tensor("ps", [128, 8, 512], F32).ap()
    PSB = bufs(8, "psb", excl=True)

    B_WBin = [bufs(7) for _ in range(NL)]
    B_WBout = [bufs(4) for _ in range(NL)]
    B_WBf1 = [bufs(22) for _ in range(NL)]
    B_WBf2 = [bufs(4) for _ in range(NL)]
    NG = len(GROUPS)
    B_XT = bufs(NG)
    B_QT = [bufs(NG) for _ in range(8)]
    B_KT = [bufs(NG) for _ in range(2)]
    B_V = bufs(NG)
    B_MIX = [bufs(NG) for _ in range(16)]
    B_ZX = [bufs(NG) for _ in range(4)]
    B_GZ = [bufs(NG) for _ in range(4)]
    out_ops = []

    def conv_tasks(l, which):
        tasks = []
        if "out" in which:
            tasks += [lambda j=j: conv_one(WBout[l][j], I["w_out"][l][:, 512 * j:512 * (j + 1)], B_WBout[l][j]) for j in range(4)]
        if "f1" in which:
            for j in range(11):
                for jj in (j, 11 + j):
                    tasks.append(lambda jj=jj: conv_one(WBf1[l][jj], I["w_ffn_in"][l][:, 512 * jj:512 * (jj + 1)], B_WBf1[l][jj]))
        if "f2" in which:
            for j in range(4):
                for part in range(4):
                    tasks.append(lambda j=j, part=part: P.dma(
                        "pool", WBf2[l][j][:, 11 * part:11 * (part + 1), :],
                        I["w_ffn_out"][l][1408 * part:1408 * (part + 1), 512 * j:512 * (j + 1)].rearrange("(kc p) n -> p kc n", p=128),
                        writes=[B_WBf2[l][j]]))
        return tasks

    def conv_one(dst, src, B):
        P.dma("pool", dst, src.rearrange("(kc p) n -> p kc n", p=128), writes=[B])

    def conv_w_in(l):
        for j in range(7):
            P.dma("pool", WBin[l][j], I["w_in"][l][:, 512 * j:512 * (j + 1)].rearrange("(kc p) n -> p kc n", p=128),
                  writes=[B_WBin[l][j]])

    def conv_w_out(l):
        for j in range(4):
            P.dma("pool", WBout[l][j], I["w_out"][l][:, 512 * j:512 * (j + 1)].rearrange("(kc p) n -> p kc n", p=128),
                  writes=[B_WBout[l][j]])

    def conv_w_f1(l, js):
        for j in js:
            P.dma("pool", WBf1[l][j], I["w_ffn_in"][l][:, 512 * j:512 * (j + 1)].rearrange("(kc p) n -> p kc n", p=128),
                  writes=[B_WBf1[l][j]])

    def conv_w_f2(l):
        for j in range(4):
            for part in range(4):
                P.dma("pool", WBf2[l][j][:, 11 * part:11 * (part + 1), :],
                      I["w_ffn_out"][l][1408 * part:1408 * (part + 1), 512 * j:512 * (j + 1)].rearrange("(kc p) n -> p kc n", p=128),
                      writes=[B_WBf2[l][j]])

    def preamble():
        stg = PSTG
        B_stg = bufs(2, "stg")
        tmpa = PTMP
        B_tmpa = Buf()
        P.memset("dve", ones_bf, 1.0, writes=[B_const])
        P.memset("dve", ones_f, 1.0, writes=[B_const])
        P.dma("sp", idn_f, idn_d, writes=[B_const])
        P.dma("sp", rot_f, rot_d, writes=[B_const])
        P.dma("sp", stg[0][0:16, :], I["c"].rearrange("(kc p) -> kc p", p=128), writes=[B_stg[0]])
        P.dma("sp", stg[0][16:32, :], I["c_ctx"].rearrange("(kc p) -> kc p", p=128), writes=[B_stg[0]])
        P.tr(ps[:, 0, 0:32], stg[0][0:32, :], idn_f[0:32, 0:32], reads=[B_stg[0], B_const], writes=[PSB[0]])
        P.act(CT.rearrange("p w k -> p (w k)"), ps[:, 0, 0:32], AF.Silu, reads=[PSB[0]], writes=[B_const])
        P.copy("dve", CTb, CT, reads=[B_const], writes=[B_const])
        for l in range(NL):
            s = stg[1]
            P.dma("sp", s[0:96, :], I["b_mod"][l].rearrange("(r p) -> r p", p=128), writes=[B_stg[1]])
            P.dma("sp", s[96:112, :], I["conv_w"][l].rearrange("t (h p) -> (t h) p", p=128), writes=[B_stg[1]])
            P.dma("sp", s[112:116, :], I["conv_b"][l].rearrange("(h p) -> h p", p=128), writes=[B_stg[1]])
            P.dma("sp", s[116:118, :], I["g_qk"][l], writes=[B_stg[1]])
            P.tr(ps[:, 1, 0:118], s[0:118, :], idn_f[0:118, 0:118], reads=[B_stg[1], B_const], writes=[PSB[1]])
            P.copy("dve", SMA[l][:, 0:118], ps[:, 1, 0:118], reads=[PSB[1]], writes=[B_sm[l]])
            s = stg[0]
            P.dma("sp", s[0:64, :], I["g_norm"][l].rearrange("f (kc p) -> (f kc) p", p=128), writes=[B_stg[0]])
            P.dma("sp", s[64:80, :], I["lru_b"][l].rearrange("d g (h p) -> (d g h) p", p=128), writes=[B_stg[0]])
            P.dma("sp", s[80:88, :], I["lru_lam"][l].rearrange("d (h p) -> (d h) p", p=128), writes=[B_stg[0]])
            P.tr(ps[:, 2, 0:88], s[0:88, :], idn_f[0:88, 0:88], reads=[B_stg[0], B_const], writes=[PSB[2]])
            P.copy("dve", SMB[l][:, 0:88], ps[:, 2, 0:88], reads=[PSB[2]], writes=[B_sm[l]])
            P.act(tmpa[:, 0:8], SMB[l][:, 80:88], AF.Exp, reads=[B_sm[l]], writes=[B_tmpa], scale=-1.0)
            P.ts("dve", tmpa[:, 0:8], tmpa[:, 0:8], 1.0, ALU.add, reads=[B_tmpa], writes=[B_tmpa])
            P.act(tmpa[:, 8:16], tmpa[:, 0:8], AF.Ln, reads=[B_tmpa], writes=[B_tmpa])
            P.ts("dve", CL[l][:, 0, :], tmpa[:, 8:16], -8.0, ALU.mult, reads=[B_tmpa], writes=[B_sm[l]])
            P.ts("dve", CL[l][:, 1, :], tmpa[:, 8:16], -16.0, ALU.mult, reads=[B_tmpa], writes=[B_sm[l]])
            s = stg[1]
            P.dma("sp", s[0:2, :], I["g_qk"][l], writes=[B_stg[1]])
            P.op("dve", lambda e, s=s: e.tensor_reduce(out=tmpa[0:2, 16:17], in_=s[0:2, :], axis=mybir.AxisListType.X,
                                                       op=ALU.max, apply_absolute_value=True), reads=[B_stg[1]], writes=[B_tmpa])
            P.ts("dve", tmpa[0:2, 20:22], idn_f[0:2, 0:2], tmpa[0:2, 16:17], ALU.mult, reads=[B_tmpa, B_const], writes=[B_tmpa])
            P.mm(ps[:, 3, 0:2], ones_f[0:2, :], tmpa[0:2, 20:22], True, True, reads=[B_tmpa, B_const], writes=[PSB[3]])
            P.copy("dve", tmpa[:, 24:26], ps[:, 3, 0:2], reads=[PSB[3]], writes=[B_tmpa])
            P.stt(NEGB[l][:, 0:1], tmpa[:, 24:25], -(128.0 ** 0.5), tmpa[:, 25:26], ALU.mult, ALU.mult,
                  reads=[B_tmpa], writes=[B_sm[l]])

    B_modA = [Buf("modA%d" % l) for l in range(NL)]
    B_modB = [Buf("modB%d" % l) for l in range(NL)]
    MODBANK = 3

    def mod_tasks(l, blocks):
        tasks = []
        for j in blocks:
            def task(j=j):
                s = ring_next()
                P.dma("pool", RING[s], I["w_mod"][l][:, 512 * j:512 * (j + 1)].rearrange("(kc p) n -> p kc n", p=128),
                      writes=[B_ring[s]])
                for n in range(4):
                    for kc in range(KC):
                        P.mm(ps[:, MODBANK, n * 2:n * 2 + 2], RING[s][:, kc, n * 128:(n + 1) * 128], CTb[:, :, kc],
                             kc == 0, kc == KC - 1, reads=[B_ring[s], B_const], writes=[PSB[MODBANK]])
                pv = ps[:, MODBANK, 0:8].rearrange("p (c w) -> p c w", w=2)
                Bm = B_modA[l] if j < 8 else B_modB[l]
                for w in range(2):
                    P.tt("dve", MOD[l][:, w, j * 4:(j + 1) * 4], pv[:, :, w], SMA[l][:, j * 4:(j + 1) * 4], ALU.add,
                         reads=[PSB[MODBANK], B_sm[l]], writes=[Bm])
            tasks.append(task)
        return tasks

    def der_A(l):
        gn = SMB[l][:, 0:64].rearrange("p (f k) -> p f k", f=4)
        for w in range(2):
            m = MOD[l][:, w, :].rearrange("p (s k) -> p s k", s=6)
            P.stt(DER[l][:, 0, w, :], m[:, 1, :], 1.0, gn[:, 0, :], ALU.add, ALU.mult, reads=[B_sm[l], B_modA[l]], writes=[B_modA[l]])
            P.copy("dve", DER[l][:, 1, w, :], m[:, 0, :], reads=[B_modA[l]], writes=[B_modA[l]])

    def der_B(l):
        gn = SMB[l][:, 0:64].rearrange("p (f k) -> p f k", f=4)
        for w in range(2):
            m = MOD[l][:, w, :].rearrange("p (s k) -> p s k", s=6)
            P.tt("dve", DER[l][:, 2, w, :], m[:, 2, :], gn[:, 1, :], ALU.mult, reads=[B_sm[l], B_modB[l]], writes=[B_modB[l]])
            P.stt(DER[l][:, 3, w, :], m[:, 4, :], 1.0, gn[:, 2, :], ALU.add, ALU.mult, reads=[B_sm[l], B_modB[l]], writes=[B_modB[l]])
            P.copy("dve", DER[l][:, 4, w, :], m[:, 3, :], reads=[B_modB[l]], writes=[B_modB[l]])
            P.tt("dve", DER[l][:, 5, w, :], m[:, 5, :], gn[:, 3, :], ALU.mult, reads=[B_sm[l], B_modB[l]], writes=[B_modB[l]])

    bg = []

    def run_bg(n=1):
        for _ in range(n):
            if bg:
                bg.pop(0)()

    def norm_stats(src, B_src, W, SQ, B_SQ, bank, RS, B_RS):
        for kc in range(KC):
            sq = SQ[kc % 2]
            P.act(sq[:, :W], src[:, kc, :W], AF.Square, reads=[B_src[kc]], writes=[B_SQ[kc % 2]])
            P.mm(ps[:, bank, :W], ones_bf, sq[:, :W], kc == 0, kc == KC - 1, reads=[B_SQ[kc % 2], B_const], writes=[PSB[bank]])
        P.act(RS[:, :W], ps[:, bank, :W], AF.Ln, reads=[PSB[bank]], writes=[B_RS], scale=1.0 / D, bias=EPSB[:, 0:1])
        P.act(ps[:, bank, :W], RS[:, :W], AF.Exp, reads=[B_RS], writes=[PSB[bank]], scale=-0.5)

    EPSB = pers.alloc((2,), F32)

    def norm_mod(XG, B_XG, W, SQ, B_SQ, bank, RS, B_RS, TMP, B_TMP, A, Bv, HT, B_HT, B_m):
        norm_stats(XG, B_XG, W, SQ, B_SQ, bank, RS, B_RS)
        for kc in range(KC):
            tmp = TMP[kc % 2]
            P.tt("dve", tmp[:, :W], XG[:, kc, :W], ps[:, bank, :W], ALU.mult, reads=[B_XG[kc], PSB[bank]], writes=[B_TMP[kc % 2]])
            P.act(HT[:, kc, :W], tmp[:, :W], AF.Identity, reads=[B_TMP[kc % 2], B_m], writes=[B_HT[kc]],
                  scale=A[:, kc:kc + 1], bias=Bv[:, kc:kc + 1])

    def phase_A(l):
        arena.reset()
        last = l == NL - 1
        XG = arena.alloc((KC, 512), F32)
        B_XG = bufs(KC, "xg")
        HT = arena.alloc((KC, 512), BF16)
        B_HT = bufs(KC, "ht")
        XL = [arena.alloc((D,), F32) for _ in range(2)]
        B_XL = bufs(2, "xl")
        SQ = [arena.alloc((512,), BF16) for _ in range(2)]
        B_SQ = bufs(2)
        RS = arena.alloc((512,), F32)
        B_RS = Buf()
        TMP = [arena.alloc((512,), F32) for _ in range(2)]
        B_TMP = bufs(2)
        COS = arena.alloc((512,), F32)
        SIN = arena.alloc((512,), F32)
        B_CS = Buf()
        QF = [arena.alloc((512,), F32) for _ in range(2)]
        B_QF = bufs(2)
        QSQ = [arena.alloc((512,), BF16) for _ in range(2)]
        B_QSQ = bufs(2)
        QRS = [arena.alloc((512,), F32) for _ in range(2)]
        B_QRS = bufs(2)
        T1 = [arena.alloc((512,), F32) for _ in range(2)]
        B_T1 = bufs(2)
        T2 = [arena.alloc((512,), F32) for _ in range(2)]
        B_T2 = bufs(2)
        QS = [arena.alloc((512,), BF16) for _ in range(3)]
        B_QS = bufs(3)
        VS = arena.alloc((4, 256), BF16)
        B_VS = Buf()
        UT = arena.alloc((4, 512), BF16)
        B_UT = Buf()
        VV = [arena.alloc((512,), BF16) for _ in range(4)]
        B_VV = bufs(4)
        ST = [arena.alloc((4, 128), F32) for _ in range(2)]
        B_ST = bufs(2)
        MS = arena.alloc((4, 512), BF16)
        B_MS = Buf()
        ZS = arena.alloc((4, 512), BF16)
        B_ZS = Buf()
        GS = arena.alloc((4, 512), BF16)
        B_GS = Buf()
        WSTl = arena.alloc((4, 128), BF16)
        BSBl = arena.alloc((4, 128), F32)
        wsl = arena.alloc((4, 128), F32)
        B_wsl = Buf()
        B_ws = Buf()
        BK_MAIN = [0, 1, 2, 3]
        BK_SS, BK_QSS, BK_ROT, BK_TR = 4, 5, 6, 7
        P.dma("sp", wsl, I["w_s"][l].rearrange("g p q -> p g q"), writes=[B_wsl])
        for g in range(4):
            P.tr(ps[:, BK_TR, g * 128:(g + 1) * 128], wsl[:, g, :], idn_f, reads=[B_wsl, B_const], writes=[PSB[BK_TR]])
        P.copy("dve", WSTl.rearrange("p g q -> p (g q)"), ps[:, BK_TR, :], reads=[PSB[BK_TR]], writes=[B_ws])
        P.dma("sp", BSBl.rearrange("p g q -> p (g q)"),
              I["b_s"][l].rearrange("g q -> (g q)").partition_broadcast(128), writes=[B_ws])
        mainctr = [0]
        qctr = [0]
        gq = SMA[l][:, 116:117]
        gk = SMA[l][:, 117:118]

        def mbank():
            b = BK_MAIN[mainctr[0] % 4]
            mainctr[0] += 1
            return b

        pending = []
        for gi, (t0, W) in enumerate(GROUPS):
            w = 1 if gi == 0 else 0
            lat = gi > 0
            need_out = lat or not last
            ntt = W // 128
            if l == 0:
                src = I["x"] if lat else I["ctx"]
                r0 = t0 - NCTX if lat else 0
                for tt in range(ntt):
                    xl = XL[tt % 2]
                    P.dma("sp", xl, src[r0 + tt * 128:r0 + (tt + 1) * 128, :], writes=[B_XL[tt % 2]])
                    for k4 in range(4):
                        for k in range(4):
                            kc = k4 * 4 + k
                            P.tr(ps[:, BK_TR, k * 128:(k + 1) * 128], xl[:, kc * 128:(kc + 1) * 128], idn_f,
                                 reads=[B_XL[tt % 2], B_const], writes=[PSB[BK_TR]])
                        P.copy("dve" if k4 % 2 else "act", XG[:, k4 * 4:(k4 + 1) * 4, tt * 128:(tt + 1) * 128],
                               ps[:, BK_TR, :].rearrange("p (k t) -> p k t", k=4), reads=[PSB[BK_TR]],
                               writes=B_XG[k4 * 4:(k4 + 1) * 4])
                P.dma("sp", XT.rearrange("(kc p) t -> p kc t", p=128)[:, :, t0:t0 + W], XG[:, :, :W],
                      reads=B_XG, writes=[B_XT[gi]])
            else:
                P.dma("sp", XG[:, :, :W], XT.rearrange("(kc p) t -> p kc t", p=128)[:, :, t0:t0 + W],
                      reads=[B_XT[gi]], writes=B_XG)
            if lat:
                n0 = t0 - NCTX
                P.dma("sp", COS[:, :W], cosT_d[:, n0:n0 + W], writes=[B_CS])
                P.dma("sp", SIN[:, :W], sinT_d[:, n0:n0 + W], writes=[B_CS])
            slots = {}

            def load_block(j):
                s = ring_next()
                ring_pending.add(s)
                P.dma("sp", RING[s], WBin[l][j], reads=[B_WBin[l][j]], writes=[B_ring[s]])
                slots[j] = s

            blocks = [0, 1, 2, 3, 4, 5, 6] if need_out else [2, 5]
            for j in blocks[:3]:
                load_block(j)
            for f in pending:
                f()
            pending = []
            norm_mod(XG, B_XG, W, SQ, B_SQ, BK_SS, RS, B_RS, TMP, B_TMP, DER[l][:, 0, w, :], DER[l][:, 1, w, :], HT, B_HT, B_modA[l])

            def fm_chunk(j, c):
                s = slots[j]
                b = mbank()
                for kc in range(KC):
                    P.mm(ps[:, b, :W], RING[s][:, kc, c * 128:(c + 1) * 128], HT[:, kc, :W], kc == 0, kc == KC - 1,
                         reads=[B_ring[s], B_HT[kc]], writes=[PSB[b]])
                return b

            def qk_post(b, gain, dst_rows, B_dst):
                i = qctr[0] % 2
                qs = qctr[0] % 3
                qb = (BK_QSS, BK_TR)[qctr[0] % 2]
                qctr[0] += 1
                P.act(QF[i][:, :W], ps[:, b, :W], AF.Identity, reads=[PSB[b], B_sm[l]], writes=[B_QF[i]], scale=gain)
                P.act(QSQ[i][:, :W], ps[:, b, :W], AF.Square, reads=[PSB[b]], writes=[B_QSQ[i]])
                P.mm(ps[:, qb, :W], ones_bf, QSQ[i][:, :W], True, True, reads=[B_QSQ[i], B_const], writes=[PSB[qb]])
                P.act(QRS[i][:, :W], ps[:, qb, :W], AF.Ln, reads=[PSB[qb]], writes=[B_QRS[i]], scale=1.0 / 128, bias=EPSB[:, 0:1])
                P.act(ps[:, qb, :W], QRS[i][:, :W], AF.Exp, reads=[B_QRS[i]], writes=[PSB[qb]], scale=-0.5)
                if lat:
                    P.mm(ps[:, BK_ROT, :W], rot_f, QF[i][:, :W], True, True, reads=[B_QF[i], B_const], writes=[PSB[BK_ROT]])
                    P.stt(T1[i][:, :W], ps[:, b, :W], gain, COS[:, :W], ALU.mult, ALU.mult, reads=[PSB[b], B_sm[l], B_CS], writes=[B_T1[i]])
                    P.tt("dve", T2[i][:, :W], ps[:, BK_ROT, :W], SIN[:, :W], ALU.mult, reads=[PSB[BK_ROT], B_CS], writes=[B_T2[i]])
                    P.tt("dve", T1[i][:, :W], T1[i][:, :W], T2[i][:, :W], ALU.add, reads=[B_T1[i], B_T2[i]], writes=[B_T1[i]])
                    P.tt("dve", QS[qs][:, :W], T1[i][:, :W], ps[:, qb, :W], ALU.mult, reads=[B_T1[i], PSB[qb]], writes=[B_QS[qs]])
                else:
                    P.tt("dve", QS[qs][:, :W], QF[i][:, :W], ps[:, qb, :W], ALU.mult, reads=[B_QF[i], PSB[qb]], writes=[B_QS[qs]])
                P.dma("sp", dst_rows[:, t0:t0 + W], QS[qs][:, :W], reads=[B_QS[qs]], writes=[B_dst])

            prevq = []
            for bi, j in enumerate(blocks):
                if j in (0, 1):
                    for c in range(4):
                        b = fm_chunk(j, c)
                        h = j * 4 + c
                        if prevq:
                            qk_post(*prevq.pop())
                        prevq.append((b, gq, QT[h * 128:(h + 1) * 128, :], B_QT[h][gi]))
                elif j == 2:
                    for c in range(2):
                        b = fm_chunk(j, c)
                        if prevq:
                            qk_post(*prevq.pop())
                        prevq.append((b, gk, KT[c * 128:(c + 1) * 128, :], B_KT[c][gi]))
                    s = slots[j]
                    for tt in range(ntt):
                        b = mbank()
                        for kc in range(KC):
                            P.mm(ps[:, b, 0:256], HT[:, kc, tt * 128:(tt + 1) * 128], RING[s][:, kc, 256:512], kc == 0, kc == KC - 1,
                                 reads=[B_ring[s], B_HT[kc]], writes=[PSB[b]])
                        if prevq:
                            qk_post(*prevq.pop())
                        P.copy("act", VS[:, tt, :], ps[:, b, 0:256], reads=[PSB[b]], writes=[B_VS])
                    pending.append(lambda t0=t0, W=W, gi=gi, ntt=ntt: P.dma("sp", VTOK[t0:t0 + W, :].rearrange("(tt p) n -> p tt n", p=128), VS[:, 0:ntt, :],
                                                 reads=[B_VS], writes=[B_V[gi]]))
                elif j == 3:
                    for c in range(4):
                        b = fm_chunk(j, c)
                        P.act(UT[:, c, :W], ps[:, b, :W], AF.Gelu, reads=[PSB[b]], writes=[B_UT])
                elif j == 4:
                    s = slots[j]
                    for tt in range(ntt):
                        b = mbank()
                        for kc in range(KC):
                            P.mm(ps[:, b, :], HT[:, kc, tt * 128:(tt + 1) * 128], RING[s][:, kc, :], kc == 0, kc == KC - 1,
                                 reads=[B_ring[s], B_HT[kc]], writes=[PSB[b]])
                        P.act(VV[tt], ps[:, b, :], AF.Gelu, reads=[PSB[b]], writes=[B_VV[tt]])
                    for tt in range(ntt):
                        b = mbank()
                        for g in range(4):
                            P.mm(ps[:, b, g * 128:(g + 1) * 128], VV[tt][:, g * 128:(g + 1) * 128], WSTl[:, g, :], True, True,
                                 reads=[B_VV[tt], B_ws], writes=[PSB[b]])
                        P.tt("dve", ST[tt % 2], ps[:, b, :].rearrange("p (g q) -> p g q", g=4), BSBl, ALU.add,
                             reads=[PSB[b], B_ws], writes=[B_ST[tt % 2]])
                        P.tt("dve", MS[:, :, tt * 128:(tt + 1) * 128], ST[tt % 2], UT[:, :, tt * 128:(tt + 1) * 128], ALU.mult,
                             reads=[B_ST[tt % 2], B_UT], writes=[B_MS])
                    pending.append(lambda t0=t0, W=W, gi=gi: P.dma("sp", MIXT[1024:1536, :].rearrange("(g p) t -> p g t", p=128)[:, :, t0:t0 + W],
                                                 MS[:, :, :W], reads=[B_MS], writes=[B_MIX[8 + g][gi] for g in range(4)]))
                elif j == 5:
                    for c in range(4):
                        b = fm_chunk(j, c)
                        P.copy("act", ZS[:, c, :W], ps[:, b, :W], reads=[PSB[b]], writes=[B_ZS])
                    pending.append(lambda t0=t0, W=W, gi=gi: P.dma("sp", ZXT.rearrange("(g p) t -> p g t", p=128)[:, :, t0:t0 + W], ZS[:, :, :W],
                                                 reads=[B_ZS], writes=[B_ZX[g][gi] for g in range(4)]))
                elif j == 6:
                    for c in range(4):
                        b = fm_chunk(j, c)
                        P.act(GS[:, c, :W], ps[:, b, :W], AF.Gelu, reads=[PSB[b]], writes=[B_GS])
                    pending.append(lambda t0=t0, W=W, gi=gi: P.dma("sp", GZT.rearrange("(g p) t -> p g t", p=128)[:, :, t0:t0 + W], GS[:, :, :W],
                                                 reads=[B_GS], writes=[B_GZ[g][gi] for g in range(4)]))
                ring_pending.discard(slots[j])
                if bi + 3 < len(blocks):
                    load_block(blocks[bi + 3])
                run_bg(1)
        for f in pending:
            f()

    def phase_B(l):
        arena.reset()
        last = l == NL - 1
        ZXb = arena.alloc((T,), BF16)
        GZb = arena.alloc((T,), BF16)
        XRb = arena.alloc((T,), BF16)
        LS = arena.alloc((T,), BF16)
        XR = arena.alloc((T,), F32)
        Rg = arena.alloc((T,), F32)
        Ig = arena.alloc((T,), F32)
        Aa = arena.alloc((T,), F32)
        Mm = arena.alloc((T,), F32)
        HF = arena.alloc((T,), F32)
        HB = arena.alloc((T,), F32)
        B_ZXb, B_GZb, B_XRb, B_LS, B_XR, B_R, B_I, B_A, B_M, B_HF, B_HB = bufs(11, "lru")
        QTh = [arena.alloc((T,), BF16) for _ in range(2)]
        B_QTh = bufs(2)
        KTk = arena.alloc((T,), BF16)
        B_KTk = Buf()
        Vk = arena.alloc((18, 128), BF16)
        B_Vk = Buf()
        PT = [arena.alloc((2, 512), BF16) for _ in range(3)]
        B_PT = bufs(3)
        RSUM = arena.alloc((512,), F32)
        B_RSUM = Buf()
        OS = [arena.alloc((512,), BF16) for _ in range(2)]
        B_OS = bufs(2)
        LWl = arena.alloc((16, 128), BF16)
        lwl = arena.alloc((16, 128), F32)
        B_lwl = Buf()
        B_lw = Buf()
        P.dma("sp", lwl, I["lru_w"][l].rearrange("d g h i j -> i (d g h) j"), writes=[B_lwl])
        P.copy("dve", LWl, lwl, reads=[B_lwl], writes=[B_lw])
        OBK = [4, 5]
        UBK = [6, 7]
        GBK = [0, 1, 2, 3]
        sw = SMA[l]
        cw = sw[:, 96:112]
        cb = sw[:, 112:116]
        lb = SMB[l][:, 64:80]
        ctrs = {"s": 0, "o": 0, "g": 0, "p": 0, "os": 0, "q": 0}
        allg = list(range(NG))

        def lru_unit(h, d):
            if d == 0:
                P.dma("sp", ZXb, ZXT[h * 128:(h + 1) * 128, :], reads=B_ZX[h], writes=[B_ZXb])
                P.dma("sp", GZb, GZT[h * 128:(h + 1) * 128, :], reads=B_GZ[h], writes=[B_GZb])
                for (s0, L) in ((0, NCTX), (NCTX, NLAT)):
                    P.ts("dve", XR[:, s0:s0 + L], ZXb[:, s0:s0 + L], cw[:, 2 * 4 + h:2 * 4 + h + 1], ALU.mult,
                         cb[:, h:h + 1], ALU.add, reads=[B_ZXb, B_sm[l]], writes=[B_XR])
                    for tap in (0, 1, 3):
                        o = tap - 2
                        lo = max(0, -o)
                        hi = L - max(0, o)
                        P.stt(XR[:, s0 + lo:s0 + hi], ZXb[:, s0 + lo + o:s0 + hi + o], cw[:, tap * 4 + h:tap * 4 + h + 1],
                              XR[:, s0 + lo:s0 + hi], ALU.mult, ALU.add, reads=[B_ZXb, B_XR, B_sm[l]], writes=[B_XR])
                P.copy("dve", XRb, XR, reads=[B_XR], writes=[B_XRb])
            for gate, dst, B_dst in ((0, Rg, B_R), (1, Ig, B_I)):
                gi_ = (d * 2 + gate) * 4 + h
                for k in range(4):
                    P.mm(ps[:, k, :], LWl[:, gi_, :], XRb[:, k * 512:(k + 1) * 512], True, True,
                         reads=[B_lw, B_XRb], writes=[PSB[k]])
                tb_ = OBK[gate]
                P.mm(ps[:, tb_, 0:256], LWl[:, gi_, :], XRb[:, 2048:2304], True, True, reads=[B_lw, B_XRb], writes=[PSB[tb_]])
                P.act(dst[:, 0:2048].rearrange("p (k t) -> p k t", k=4), ps[:, 0:4, :], AF.Sigmoid, reads=[PSB[k] for k in range(4)],
                      writes=[B_dst], bias=lb[:, gi_:gi_ + 1])
                P.act(dst[:, 2048:2304], ps[:, tb_, 0:256], AF.Sigmoid, reads=[PSB[tb_]], writes=[B_dst], bias=lb[:, gi_:gi_ + 1])
            P.act(Aa, Rg, AF.Exp, reads=[B_R], writes=[B_A], scale=CL[l][:, 0, d * 4 + h:d * 4 + h + 1])
            P.act(Mm, Rg, AF.Exp, reads=[B_R], writes=[B_M], scale=CL[l][:, 1, d * 4 + h:d * 4 + h + 1])
            P.act(Mm, Mm, AF.Sqrt, reads=[B_M], writes=[B_M], scale=-1.0, bias=ONEB[:, 0:1])
            P.tt("dve", Ig, Ig, XR, ALU.mult, reads=[B_I, B_XR], writes=[B_I])
            P.tt("dve", Ig, Ig, Mm, ALU.mult, reads=[B_I, B_M], writes=[B_I])
            if d == 0:
                P.scan(HF, Aa, Ig, 0.0, reads=[B_A, B_I], writes=[B_HF])
            else:
                P.scan(HB[:, 0:NCTX][:, ::-1], Aa[:, 0:NCTX][:, ::-1], Ig[:, 0:NCTX][:, ::-1], 0.0, reads=[B_A, B_I], writes=[B_HB])
                P.scan(HB[:, NCTX:T][:, ::-1], Aa[:, NCTX:T][:, ::-1], Ig[:, NCTX:T][:, ::-1], HB[:, 0:1],
                       reads=[B_A, B_I, B_HB], writes=[B_HB])
                P.tt("dve", HF, HF, HB, ALU.add, reads=[B_HF, B_HB], writes=[B_HF])
                P.tt("dve", LS, HF, GZb, ALU.mult, reads=[B_HF, B_GZb], writes=[B_LS])
                P.dma("sp", MIXT[1536 + h * 128:1536 + (h + 1) * 128, :], LS, reads=[B_LS], writes=B_MIX[12 + h])

        def attn_block(c, kv, qi, q0, W, kcs):
            pairs = [kcs[i:i + 2] for i in range(0, len(kcs), 2)]

            def s_pair(pr):
                pb = ctrs["s"] % 2
                ctrs["s"] += 1
                for i, kc in enumerate(pr):
                    P.mm(ps[:, 2 * pb + i, :W], KTk[:, kc * 128:(kc + 1) * 128], QTh[qi][:, q0:q0 + W], True, True,
                         reads=[B_KTk, B_QTh[qi]], writes=[PSB[2 * pb + i]])
                return pb

            ob = OBK[ctrs["o"] % 2]
            ub = UBK[ctrs["o"] % 2]
            ctrs["o"] += 1
            nb = s_pair(pairs[0])
            nk = len(kcs)
            done = 0
            for ip, pr in enumerate(pairs):
                pb = nb
                pi = ctrs["p"] % 3
                ctrs["p"] += 1
                n2 = len(pr)
                P.act(PT[pi][:, 0:n2, :W], ps[:, 2 * pb:2 * pb + n2, :W], AF.Exp,
                      reads=[PSB[2 * pb + i] for i in range(n2)] + [B_sm[l]], writes=[B_PT[pi]],
                      scale=QSCALE, bias=NEGB[l][:, 0:1])
                if ip + 1 < len(pairs):
                    nb = s_pair(pairs[ip + 1])
                for i, kc in enumerate(pr):
                    P.mm(ps[:, ob, :W], Vk[:, kc, :], PT[pi][:, i, :W], done == 0, done == nk - 1, reads=[B_Vk, B_PT[pi]], writes=[PSB[ob]])
                    P.mm(ps[:, ub, :W], ones_bf, PT[pi][:, i, :W], done == 0, done == nk - 1, reads=[B_const, B_PT[pi]], writes=[PSB[ub]])
                    done += 1
            P.recip(RSUM[:, :W], ps[:, ub, :W], reads=[PSB[ub]], writes=[B_RSUM])
            oi = ctrs["os"] % 2
            ctrs["os"] += 1
            P.tt("dve", OS[oi][:, :W], ps[:, ob, :W], RSUM[:, :W], ALU.mult, reads=[PSB[ob], B_RSUM], writes=[B_OS[oi]])
            gidx = [g for g, (t0, w_) in enumerate(GROUPS) if t0 == q0][0]
            P.dma("sp", MIXT[c * 128:(c + 1) * 128, q0:q0 + W], OS[oi][:, :W], reads=[B_OS[oi]], writes=[B_MIX[c][gidx]])

        def attn_head(c):
            kv = c // 4
            if c % 4 == 0:
                P.dma("sp", KTk, KT[kv * 128:(kv + 1) * 128, :], reads=B_KT[kv], writes=[B_KTk])
                P.dma("sp", Vk, VTOK[:, kv * 128:(kv + 1) * 128].rearrange("(kc p) d -> p kc d", p=128), reads=B_V, writes=[B_Vk])
            qi = ctrs["q"] % 2
            ctrs["q"] += 1
            if last:
                P.dma("sp", QTh[qi][:, NCTX:T], QT[c * 128:(c + 1) * 128, NCTX:T], reads=B_QT[c][1:], writes=[B_QTh[qi]])
            else:
                P.dma("sp", QTh[qi], QT[c * 128:(c + 1) * 128, :], reads=B_QT[c], writes=[B_QTh[qi]])
            for g in range(1, NG):
                q0, W = GROUPS[g]
                attn_block(c, kv, qi, q0, W, list(range(18)))
                run_bg(1)
            if not last:
                attn_block(c, kv, qi, 0, NCTX, [0, 1])

        units = [(h, d) for h in range(4) for d in range(2)]
        for i in range(8):
            attn_head(i)
            lru_unit(*units[i])

    ONEB = pers.alloc((2,), F32)

    H2T = scratch("H2T", [D, T], BF16)
    B_H2T = bufs(NG)

    def proj_norm_update(l, W, nkc, SRC, B_SRC, wload, gvec, XGt, B_XGt, YT, B_YT, SQ, B_SQ, RS, B_RS, TMP, B_TMP, mbank, BK_SS, allow_bg=True, mid=None, after_first_load=None, defer=None, after_block=None, rs_bank=None):
        sqs = []
        for j in range(4):
            parts = wload(j)
            if j == 0 and after_first_load is not None:
                after_first_load()
            for c in range(4):
                n = j * 4 + c
                b = mbank()
                for (s, k_lo, k_hi) in parts:
                    for kc in range(k_lo, k_hi):
                        P.mm(ps[:, b, :W], RING[s][:, kc - k_lo, c * 128:(c + 1) * 128], SRC[:, kc, :W], kc == 0, kc == nkc - 1,
                             reads=[B_ring[s], B_SRC[kc]], writes=[PSB[b]])
                P.copy("act", YT[:, n, :W], ps[:, b, :W], reads=[PSB[b]], writes=[B_YT[n]])
                sq = SQ[n % 2]
                P.act(sq[:, :W], ps[:, b, :W], AF.Square, reads=[PSB[b]], writes=[B_SQ[n % 2]])
                if c == 3 and allow_bg:
                    run_bg(1)
                if c == 3 and after_block is not None:
                    after_block(j)
                sqs.append(n)
                if len(sqs) > 1:
                    m = sqs[-2]
                    P.mm(ps[:, BK_SS, :W], ones_bf, SQ[m % 2][:, :W], m == 0, False, reads=[B_SQ[m % 2], B_const], writes=[PSB[BK_SS]])
        m = sqs[-1]
        P.mm(ps[:, BK_SS, :W], ones_bf, SQ[m % 2][:, :W], False, True, reads=[B_SQ[m % 2], B_const], writes=[PSB[BK_SS]])
        if mid is not None:
            mid()
        steps = []
        rsb = BK_SS if rs_bank is None else rs_bank

        def s0():
            P.act(RS[:, :W], ps[:, BK_SS, :W], AF.Ln, reads=[PSB[BK_SS]], writes=[B_RS], scale=1.0 / D, bias=EPSB[:, 0:1])
            P.act(ps[:, rsb, :W], RS[:, :W], AF.Exp, reads=[B_RS], writes=[PSB[rsb]], scale=-0.5)
        steps.append(s0)
        for kc in range(KC):
            def sk(kc=kc):
                P.tt("dve", YT[:, kc, :W], YT[:, kc, :W], ps[:, rsb, :W], ALU.mult, reads=[B_YT[kc], PSB[rsb]], writes=[B_YT[kc]])
                P.stt(XGt[:, kc, :W], YT[:, kc, :W], gvec[:, kc:kc + 1], XGt[:, kc, :W], ALU.mult, ALU.add,
                      reads=[B_YT[kc], B_XGt[kc], B_modB[l]], writes=[B_XGt[kc]])
            steps.append(sk)
        if defer is None:
            for f_ in steps:
                f_()
        else:
            defer.extend(steps)

    def phase_C1(l):
        arena.reset()
        last = l == NL - 1
        XG1 = arena.alloc((KC, 512), F32)
        XG = [XG1, XG1]
        B_XG1 = bufs(KC, "xg")
        B_XG = [B_XG1, B_XG1]
        YT = arena.alloc((KC, 512), F32)
        B_YT = bufs(KC, "yt")
        HT = [arena.alloc((KC, 512), BF16) for _ in range(2)]
        B_HT = [bufs(KC, "ht") for _ in range(2)]
        SQ = [arena.alloc((512,), BF16) for _ in range(2)]
        B_SQ = bufs(2)
        RS = [arena.alloc((512,), F32) for _ in range(2)]
        B_RS = bufs(2)
        TMP = [arena.alloc((512,), F32) for _ in range(2)]
        B_TMP = bufs(2)
        mainctr = [0]

        def mbank():
            b = mainctr[0] % 4
            mainctr[0] += 1
            return b

        XTv = XT.rearrange("(kc p) t -> p kc t", p=128)
        MXv = MIXT.rearrange("(kc p) t -> p kc t", p=128)
        glist = [gi for gi in range(NG) if not (last and gi == 0)]

        def ht_load(k):
            gi = glist[k]
            t0, W = GROUPS[gi]
            P.dma("sp", HT[k % 2][:, :, :W], MXv[:, :, t0:t0 + W], reads=[B_MIX[c_][gi] for c_ in range(16)], writes=B_HT[k % 2])

        def xg_load(k):
            gi = glist[k]
            t0, W = GROUPS[gi]
            for q4 in range(4):
                P.dma("sp", XG1[:, q4 * 4:(q4 + 1) * 4, :W], XTv[:, q4 * 4:(q4 + 1) * 4, t0:t0 + W], reads=[B_XT[gi]],
                      writes=B_XG1[q4 * 4:(q4 + 1) * 4])

        def w_loads():
            parts = []
            for j in range(4):
                s = ring_next()
                P.dma("sp", RING[s], WBout[l][j], reads=[B_WBout[l][j]], writes=[B_ring[s]])
                parts.append([(s, 0, KC)])
            return parts

        ht_load(0)
        xg_load(0)
        parts = w_loads()
        for k, gi in enumerate(glist):
            t0, W = GROUPS[gi]
            w = 1 if gi == 0 else 0
            i = k % 2
            nxt = {}

            def ablk(j, k=k):
                if k + 1 < len(glist):
                    if j == 0:
                        ht_load(k + 1)
                        nxt["parts"] = []
                    s_ = ring_next()
                    P.dma("sp", RING[s_], WBout[l][j], reads=[B_WBout[l][j]], writes=[B_ring[s_]])
                    nxt["parts"].append([(s_, 0, KC)])

            proj_norm_update(l, W, KC, HT[i], B_HT[i], (lambda j, parts=parts: parts[j]), DER[l][:, 2, w, :], XG1, B_XG1, YT, B_YT,
                             SQ, B_SQ, RS[0], B_RS[0], TMP, B_TMP, mbank, 4, allow_bg=False, after_block=ablk, rs_bank=6)
            P.dma("sp", XTv[:, :, t0:t0 + W], XG1[:, :, :W], reads=B_XG1, writes=[B_XT[gi]])
            norm_mod(XG1, B_XG1, W, SQ, B_SQ, 5, RS[1], B_RS[1], TMP, B_TMP, DER[l][:, 3, w, :], DER[l][:, 4, w, :],
                     HT[i], B_HT[i], B_modB[l])
            P.dma("sp", H2T.rearrange("(kc p) t -> p kc t", p=128)[:, :, t0:t0 + W], HT[i][:, :, :W], reads=B_HT[i], writes=[B_H2T[gi]])
            if k + 1 < len(glist):
                xg_load(k + 1)
                parts = nxt["parts"]

    def phase_C2(l):
        arena.reset()
        last = l == NL - 1
        XG = arena.alloc((KC, 512), F32)
        B_XG = bufs(KC, "xg")
        YT = arena.alloc((KC, 512), F32)
        B_YT = bufs(KC, "yt")
        HT = arena.alloc((KC, 512), BF16)
        B_HT = bufs(KC, "ht")
        AT = arena.alloc((FC, 512), BF16)
        B_AT = bufs(FC, "at")
        SQ = [arena.alloc((512,), BF16) for _ in range(2)]
        B_SQ = bufs(2)
        RS = arena.alloc((512,), F32)
        B_RS = Buf()
        TMP = [arena.alloc((512,), F32) for _ in range(2)]
        B_TMP = bufs(2)
        SG = TMP
        B_SG = B_TMP
        BK_TR = [6, 7]
        mainctr = [0]

        def mbank():
            b = mainctr[0] % 4
            mainctr[0] += 1
            return b

        XTv = XT.rearrange("(kc p) t -> p kc t", p=128)
        glist2 = [gi for gi in range(NG) if not (last and gi == 0)]

        def ht2_load(gi):
            t0_, W_ = GROUPS[gi]
            P.dma("sp", HT[:, :, :W_], H2T.rearrange("(kc p) t -> p kc t", p=128)[:, :, t0_:t0_ + W_], reads=[B_H2T[gi]], writes=B_HT)

        deferred = []
        for gi, (t0, W) in enumerate(GROUPS):
            if last and gi == 0:
                continue
            w = 1 if gi == 0 else 0
            if gi == glist2[0]:
                ht2_load(gi)
            for j in range(11):
                sg_ = ring_next()
                P.dma("sp", RING[sg_], WBf1[l][j], reads=[B_WBf1[l][j]], writes=[B_ring[sg_]])
                su_ = ring_next()
                P.dma("sp", RING[su_], WBf1[l][11 + j], reads=[B_WBf1[l][11 + j]], writes=[B_ring[su_]])
                if j == 6:
                    P.dma("sp", XG[:, :, :W], XTv[:, :, t0:t0 + W], reads=[B_XT[gi]], writes=B_XG)
                for c in range(4):
                    fc = j * 4 + c
                    bg_ = mbank()
                    for kc in range(KC):
                        P.mm(ps[:, bg_, :W], RING[sg_][:, kc, c * 128:(c + 1) * 128], HT[:, kc, :W], kc == 0, kc == KC - 1,
                             reads=[B_ring[sg_], B_HT[kc]], writes=[PSB[bg_]])
                    bu = mbank()
                    for kc in range(KC):
                        P.mm(ps[:, bu, :W], RING[su_][:, kc, c * 128:(c + 1) * 128], HT[:, kc, :W], kc == 0, kc == KC - 1,
                             reads=[B_ring[su_], B_HT[kc]], writes=[PSB[bu]])
                    sg = SG[fc % 2]
                    P.act(sg[:, :W], ps[:, bg_, :W], AF.Silu, reads=[PSB[bg_]], writes=[B_SG[fc % 2]])
                    P.tt("dve", AT[:, fc, :W], sg[:, :W], ps[:, bu, :W], ALU.mult, reads=[B_SG[fc % 2], PSB[bu]], writes=[B_AT[fc]])
                    for _ in range(2):
                        if deferred:
                            deferred.pop(0)()
                run_bg(1)
            while deferred:
                deferred.pop(0)()
            nx = [g_ for g_ in glist2 if g_ > gi]

            def afl(nx=nx):
                if nx:
                    ht2_load(nx[0])

            def wl_f2(j):
                parts = []
                for (k_lo, k_hi) in ((0, 16), (16, 32), (32, 44)):
                    s = ring_next()
                    P.dma("sp", RING[s][:, 0:k_hi - k_lo, :], WBf2[l][j][:, k_lo:k_hi, :], reads=[B_WBf2[l][j]], writes=[B_ring[s]])
                    parts.append((s, k_lo, k_hi))
                return parts

            proj_norm_update(l, W, FC, AT, B_AT, wl_f2, DER[l][:, 5, w, :], XG, B_XG, YT, B_YT, SQ, B_SQ, RS, B_RS, TMP, B_TMP, mbank, 4,
                             after_first_load=afl, defer=deferred)

            def fin(gi=gi, t0=t0, W=W):
                if not last:
                    P.dma("sp", XTv[:, :, t0:t0 + W], XG[:, :, :W], reads=B_XG, writes=[B_XT[gi]])
                    return
                for tt in range(W // 128):
                    OT = YT.rearrange("p k t -> p (k t)")[:, (tt % 2) * D:(tt % 2 + 1) * D]
                    B_OT = B_YT[(tt % 2) * 4:(tt % 2) * 4 + 4]
                    for k4 in range(4):
                        b = BK_TR[k4 % 2]
                        for k in range(4):
                            kc = k4 * 4 + k
                            P.tr(ps[:, b, k * 128:(k + 1) * 128], XG[:, kc, tt * 128:(tt + 1) * 128], idn_f,
                                 reads=[B_XG[kc], B_const], writes=[PSB[b]])
                        P.copy("dve" if k4 % 2 else "act", OT[:, k4 * 512:(k4 + 1) * 512], ps[:, b, :], reads=[PSB[b]], writes=B_OT)
                    r0 = t0 - NCTX + tt * 128
                    out_ops.append(P.dma("sp", out_d[r0:r0 + 128, :], OT, reads=B_OT))
            deferred.append(fin)
        while deferred:
            deferred.pop(0)()

    P.memset("dve", EPSB, EPS, writes=[B_const])
    P.memset("dve", ONEB, 1.0, writes=[B_const])
    conv_w_in(0)
    preamble()
    for t_ in mod_tasks(0, range(8)):
        t_()
    der_A(0)

    def interleave(a_, b_):
        out = []
        na, nb = len(a_), len(b_)
        ia = ib = 0
        while ia < na or ib < nb:
            if ib >= nb or (ia < na and ia * nb <= ib * na):
                out.append(a_[ia])
                ia += 1
            else:
                out.append(b_[ib])
                ib += 1
        return out

    a0 = mod_tasks(0, range(8, 24)) + [lambda: der_B(0)]
    a0 = interleave(a0, [(lambda: None)] * (35 - len(a0)))
    bg.extend(a0 + conv_tasks(0, ("out", "f1")))
    seq = [("P", None, 0)]
    for l in range(NL):
        seq += [("A%d" % l, phase_A, l), ("B%d" % l, phase_B, l), ("C%d" % l, phase_C1, l), ("D%d" % l, phase_C2, l)]
    for name, fn, l in seq:
        if fn is not None:
            if fn is phase_C1:
                run_bg(len(bg))
                if l == 0:
                    for t_ in conv_tasks(0, ("f2",)):
                        t_()
                    bg.extend([lambda j=j: conv_one(WBin[1][j], I["w_in"][1][:, 512 * j:512 * (j + 1)], B_WBin[1][j]) for j in range(7)])
                    bg.extend(interleave(mod_tasks(1, range(24)) + [lambda: der_A(1), lambda: der_B(1)],
                                         conv_tasks(1, ("out", "f1", "f2"))))
            fn(l)
            if fn is phase_C2:
                run_bg(len(bg))
            P.barrier()
        if stop == name:
            break
    if debug:
        dt = arena.ap[:, 0:2048]
        P.barrier()
        B_d = Buf()
        P.copy("dve", dt[:, 0:192], MOD[0].rearrange("p w c -> p (w c)"), writes=[B_d])
        P.copy("dve", dt[:, 192:384], MOD[1].rearrange("p w c -> p (w c)"), writes=[B_d])
        P.copy("dve", dt[:, 384:576], DER[0].rearrange("p a w c -> p (a w c)"), writes=[B_d])
        P.copy("dve", dt[:, 576:592], CL[0].rearrange("p a c -> p (a c)"), writes=[B_d])
        P.copy("dve", dt[:, 592:594], NEGB[0], writes=[B_d])
        P.copy("dve", dt[:, 600:728], SMA[0], writes=[B_d])
        P.copy("dve", dt[:, 728:856], SMB[0], writes=[B_d])
        out_ops.append(P.dma("sp", DBG, dt, reads=[B_d]))
    P.emit(final_wait_ops=out_ops)
    return nc, P


def _consts():
    n = np.arange(NLAT)
    row = (n // 64).astype(np.float32)
    col = (n % 64).astype(np.float32)
    inv = (np.float32(10000.0) ** (-np.arange(0, 64, 2, dtype=np.float32) / np.float32(64))).astype(np.float32)
    ar = row[:, None] * inv
    ac = col[:, None] * inv
    ang = np.concatenate([ar, ar, ac, ac], axis=-1).astype(np.float32)
    cosT = np.ascontiguousarray(np.cos(ang).T.astype(np.float32))
    sinT = np.ascontiguousarray(np.sin(ang).T.astype(np.float32))
    rot = np.zeros((128, 128), np.float32)
    for a in range(2):
        for i in range(32):
            rot[a * 64 + 32 + i, a * 64 + i] = -1.0
            rot[a * 64 + i, a * 64 + 32 + i] = 1.0
    return {"k_cos": cosT, "k_sin": sinT, "k_rot": rot, "k_idn": np.eye(128, dtype=np.float32)}


_CACHE = {}


def kernel(**inputs):
    if "nc" not in _CACHE:
        _CACHE["nc"] = build()[0]
    nc = _CACHE["nc"]
    consts = _consts()
    B = inputs["x"].shape[0]
    shared = {k: np.ascontiguousarray(np.asarray(v, dtype=np.float32)) for k, v in inputs.items() if k not in ("x", "c", "ctx")}
    in_maps = []
    for b in range(B):
        m = dict(shared)
        m.update(consts)
        m["x"] = np.ascontiguousarray(np.asarray(inputs["x"][b], dtype=np.float32))
        m["c"] = np.ascontiguousarray(np.asarray(inputs["c"][b], dtype=np.float32))
        m["ctx"] = np.ascontiguousarray(np.asarray(inputs["ctx"][b], dtype=np.float32))
        in_maps.append(m)
    res = run_bass_kernel_spmd(nc, in_maps, core_ids=list(range(B)))
    return np.stack([np.asarray(r["out"], dtype=np.float32) for r in res.results], axis=0)
```
